# Optimizing a Trainium2 kernel written in Bass

```python
import math
import jax, jax.numpy as jnp
from jax import lax
import numpy as np

D_MODEL = 1024
BATCH = 2
SEQ = 8192
DEPTH = 1

SSM_WIDTH = D_MODEL // 2
SSM_GROUP = 16
SSM_GROUPS = SSM_WIDTH // SSM_GROUP
SSM_STATE = 64
DT_MIN = 1e-3
DT_MAX = 1e-1
LAMBDA_RE_MAX = -1e-4
HGRN_WIDTH = D_MODEL // 2
HGRN_HEAD_DIM = 128
HGRN_HEADS = HGRN_WIDTH // HGRN_HEAD_DIM
HGRN_CHUNK = 64
NORM_EPS = 1e-6

IN_WIDTH = 2 * SSM_WIDTH + 5 * HGRN_WIDTH + 2 * D_MODEL
SPLIT_POINTS = (
    SSM_WIDTH,
    2 * SSM_WIDTH,
    2 * SSM_WIDTH + HGRN_WIDTH,
    2 * SSM_WIDTH + 2 * HGRN_WIDTH,
    2 * SSM_WIDTH + 3 * HGRN_WIDTH,
    2 * SSM_WIDTH + 4 * HGRN_WIDTH,
    2 * SSM_WIDTH + 5 * HGRN_WIDTH,
    2 * SSM_WIDTH + 5 * HGRN_WIDTH + D_MODEL,
)

kernel_name = 'hybrid_s5_hgrn2_gated_block'


def rms_norm(x, w):
    xf = x.astype(jnp.float32)
    y = xf * lax.rsqrt(jnp.mean(xf * xf, axis=-1, keepdims=True) + NORM_EPS)
    return (y * w.astype(jnp.float32)).astype(x.dtype)


def s5_branch(u, z, lam_re, lam_im, b_re, b_im, c_re, c_im, d, log_dt, w_glu, b_glu):
    f32 = jnp.float32
    bsz, seqlen, _ = u.shape
    ug = u.astype(f32).reshape(bsz, seqlen, SSM_GROUPS, SSM_GROUP)
    lr = jnp.minimum(lam_re.astype(f32), LAMBDA_RE_MAX)
    li = lam_im.astype(f32)
    dt = jnp.exp(log_dt.astype(f32))[:, None]
    mag = jnp.exp(lr * dt)
    ab_re = mag * jnp.cos(li * dt)
    ab_im = mag * jnp.sin(li * dt)
    den = lr * lr + li * li
    nr = ab_re - 1.0
    coef_re = (nr * lr + ab_im * li) / den
    coef_im = (ab_im * lr - nr * li) / den
    br = b_re.astype(f32)
    bi = b_im.astype(f32)
    bb_re = coef_re[..., None] * br - coef_im[..., None] * bi
    bb_im = coef_re[..., None] * bi + coef_im[..., None] * br
    bu_re = jnp.einsum('blgp,gnp->blgn', ug, bb_re)
    bu_im = jnp.einsum('blgp,gnp->blgn', ug, bb_im)
    a_re = jnp.broadcast_to(ab_re, bu_re.shape)
    a_im = jnp.broadcast_to(ab_im, bu_im.shape)

    def combine(e1, e2):
        a1r, a1i, b1r, b1i = e1
        a2r, a2i, b2r, b2i = e2
        return (a2r * a1r - a2i * a1i,
                a2r * a1i + a2i * a1r,
                a2r * b1r - a2i * b1i + b2r,
                a2r * b1i + a2i * b1r + b2i)

    _, _, h_re, h_im = lax.associative_scan(combine, (a_re, a_im, bu_re, bu_im), axis=1)
    y = (jnp.einsum('blgn,gpn->blgp', h_re, c_re.astype(f32))
         - jnp.einsum('blgn,gpn->blgp', h_im, c_im.astype(f32))
         + d.astype(f32) * ug)
    y = jax.nn.gelu(y.reshape(bsz, seqlen, SSM_WIDTH))
    y = y * jax.nn.sigmoid(y @ w_glu.astype(f32) + b_glu.astype(f32))
    return y * jax.nn.silu(z.astype(f32))


def hgrn2_branch(q, f, i, og, z, lower_bound, head_norm_w):
    f32 = jnp.float32
    bsz, seqlen, _ = q.shape
    n_chunks = seqlen // HGRN_CHUNK

    def to_chunks(t):
        return t.reshape(bsz, n_chunks, HGRN_CHUNK, HGRN_HEADS, HGRN_HEAD_DIM).transpose(1, 0, 3, 2, 4)

    qf = jax.nn.silu(q.astype(f32))
    forget = lower_bound + (1.0 - lower_bound) * jax.nn.sigmoid(f.astype(f32))
    log_f = jnp.log(forget)
    key = 1.0 - forget
    causal = jnp.tril(jnp.ones((HGRN_CHUNK, HGRN_CHUNK), dtype=bool))[:, :, None]

    def step(state, inp):
        qc, kc, vc, gc = inp
        cum = jnp.cumsum(gc, axis=2)
        o_inter = jnp.einsum('bhtk,bhkv->bhtv', qc * jnp.exp(cum), state)
        diff = cum[:, :, :, None, :] - cum[:, :, None, :, :]
        decay = jnp.exp(jnp.where(causal, diff, -jnp.inf))
        scores = jnp.einsum('bhtk,bhsk,bhtsk->bhts', qc, kc, decay)
        o_intra = jnp.einsum('bhts,bhsv->bhtv', scores, vc)
        last = cum[:, :, -1:, :]
        new_state = (jnp.exp(last[:, :, 0, :])[..., None] * state
                     + jnp.einsum('bhsk,bhsv->bhkv', kc * jnp.exp(last - cum), vc))
        return new_state, o_inter + o_intra

    s0 = jnp.zeros((bsz, HGRN_HEADS, HGRN_HEAD_DIM, HGRN_HEAD_DIM), f32)
    _, o = lax.scan(step, s0, (to_chunks(qf), to_chunks(key), to_chunks(i.astype(f32)), to_chunks(log_f)))
    o = o.transpose(1, 0, 3, 2, 4).reshape(bsz, seqlen, HGRN_HEADS, HGRN_HEAD_DIM)
    o = o * jax.nn.sigmoid(og.astype(f32)).reshape(bsz, seqlen, HGRN_HEADS, HGRN_HEAD_DIM)
    o = o * lax.rsqrt(jnp.mean(o * o, axis=-1, keepdims=True) + NORM_EPS)
    o = o * head_norm_w.astype(f32).reshape(HGRN_HEADS, HGRN_HEAD_DIM)
    o = o.reshape(bsz, seqlen, HGRN_WIDTH)
    return o * jax.nn.silu(z.astype(f32))


def setup_inputs(seed: int = 0) -> dict:
    key = jax.random.key(seed)
    ks = jax.random.split(key, 24)
    nrm = jax.random.normal
    f32 = jnp.float32
    n_idx = jnp.arange(SSM_STATE, dtype=f32)
    x = nrm(ks[0], (BATCH, SEQ, D_MODEL), f32)
    norm_w = 1.0 + 0.01 * nrm(ks[1], (DEPTH, D_MODEL), f32)
    w_in = nrm(ks[2], (DEPTH, D_MODEL, IN_WIDTH), f32) * D_MODEL ** -0.5
    ssm_lambda_re = -0.5 + 0.01 * nrm(ks[3], (DEPTH, SSM_GROUPS, SSM_STATE), f32)
    ssm_lambda_im = math.pi * n_idx + 0.01 * nrm(ks[4], (DEPTH, SSM_GROUPS, SSM_STATE), f32)
    ssm_b_re = nrm(ks[5], (DEPTH, SSM_GROUPS, SSM_STATE, SSM_GROUP), f32) * (2 * SSM_GROUP) ** -0.5
    ssm_b_im = nrm(ks[6], (DEPTH, SSM_GROUPS, SSM_STATE, SSM_GROUP), f32) * (2 * SSM_GROUP) ** -0.5
    ssm_c_re = nrm(ks[7], (DEPTH, SSM_GROUPS, SSM_GROUP, SSM_STATE), f32) * (2 * SSM_STATE) ** -0.5
    ssm_c_im = nrm(ks[8], (DEPTH, SSM_GROUPS, SSM_GROUP, SSM_STATE), f32) * (2 * SSM_STATE) ** -0.5
    ssm_d = nrm(ks[9], (DEPTH, SSM_GROUPS, SSM_GROUP), f32)
    ssm_log_dt = jax.random.uniform(ks[10], (DEPTH, SSM_GROUPS), f32,
                                    minval=math.log(DT_MIN), maxval=math.log(DT_MAX))
    ssm_w_glu = nrm(ks[11], (DEPTH, SSM_WIDTH, SSM_WIDTH), f32) * SSM_WIDTH ** -0.5
    ssm_b_glu = 0.01 * nrm(ks[12], (DEPTH, SSM_WIDTH), f32)
    hgrn_lb_logits = 0.1 * nrm(ks[13], (DEPTH + 1, HGRN_WIDTH), f32)
    hgrn_norm_w = 1.0 + 0.01 * nrm(ks[14], (DEPTH, HGRN_WIDTH), f32)
    w_proj_a = nrm(ks[15], (DEPTH, SSM_WIDTH, D_MODEL), f32) * SSM_WIDTH ** -0.5
    w_proj_b = nrm(ks[16], (DEPTH, HGRN_WIDTH, D_MODEL), f32) * HGRN_WIDTH ** -0.5
    w_out = nrm(ks[17], (DEPTH, D_MODEL, D_MODEL), f32) * D_MODEL ** -0.5
    final_norm_w = 1.0 + 0.01 * nrm(ks[18], (D_MODEL,), f32)
    return {'x': x, 'norm_w': norm_w, 'w_in': w_in,
            'ssm_lambda_re': ssm_lambda_re, 'ssm_lambda_im': ssm_lambda_im,
            'ssm_b_re': ssm_b_re, 'ssm_b_im': ssm_b_im,
            'ssm_c_re': ssm_c_re, 'ssm_c_im': ssm_c_im,
            'ssm_d': ssm_d, 'ssm_log_dt': ssm_log_dt,
            'ssm_w_glu': ssm_w_glu, 'ssm_b_glu': ssm_b_glu,
            'hgrn_lb_logits': hgrn_lb_logits, 'hgrn_norm_w': hgrn_norm_w,
            'w_proj_a': w_proj_a, 'w_proj_b': w_proj_b, 'w_out': w_out,
            'final_norm_w': final_norm_w}


def reference(x, norm_w, w_in, ssm_lambda_re, ssm_lambda_im, ssm_b_re, ssm_b_im,
              ssm_c_re, ssm_c_im, ssm_d, ssm_log_dt, ssm_w_glu, ssm_b_glu,
              hgrn_lb_logits, hgrn_norm_w, w_proj_a, w_proj_b, w_out, final_norm_w):
    lower_bounds = jnp.cumsum(jax.nn.softmax(hgrn_lb_logits.astype(jnp.float32), axis=0), axis=0)
    h = x
    for layer in range(DEPTH):
        xn = rms_norm(h, norm_w[layer])
        proj = xn @ w_in[layer]
        u_a, z_a, q_b, f_b, i_b, og_b, z_b, g_a, g_b = jnp.split(proj, SPLIT_POINTS, axis=-1)
        y_a = s5_branch(u_a, z_a, ssm_lambda_re[layer], ssm_lambda_im[layer],
                        ssm_b_re[layer], ssm_b_im[layer], ssm_c_re[layer], ssm_c_im[layer],
                        ssm_d[layer], ssm_log_dt[layer], ssm_w_glu[layer], ssm_b_glu[layer])
        y_b = hgrn2_branch(q_b, f_b, i_b, og_b, z_b, lower_bounds[layer], hgrn_norm_w[layer])
        merged = (jax.nn.sigmoid(g_a.astype(jnp.float32)) * (y_a @ w_proj_a[layer])
                  + jax.nn.sigmoid(g_b.astype(jnp.float32)) * (y_b @ w_proj_b[layer]))
        h = h + (merged @ w_out[layer]).astype(h.dtype)
    return rms_norm(h, final_norm_w)
```

```python
import math
import numpy as np
import ml_dtypes
import concourse.bass as bass
import concourse.mybir as mybir
from concourse.bass_utils import run_bass_kernel_spmd

F32 = mybir.dt.float32
BF16 = mybir.dt.bfloat16
I32 = mybir.dt.int32
AF = mybir.ActivationFunctionType
ALU = mybir.AluOpType
AX = mybir.AxisListType

TT = 2048
D = 1024
NT = TT // 128
NB = TT // 512
EPS = 1e-6
TWO_PI = 2.0 * math.pi

C_U, C_ZA, C_Q, C_F, C_I, C_OG, C_ZB, C_GA, C_GB = 0, 512, 1024, 1536, 2048, 2560, 3072, 3584, 4608


class Buf:
    def __init__(self, name, fence):
        self.name = name
        self.fence = fence

    def k(self, *idx):
        return (self, idx)


class Tracker:
    def __init__(self, nc):
        self.nc = nc
        self.eng = {"pe": nc.tensor, "act": nc.scalar, "dve": nc.vector, "pool": nc.gpsimd, "sp": nc.sync}
        self.semh = {}
        self.cnt = {}
        for e in self.eng:
            self.semh[e] = nc.semaphore("s_" + e).__enter__()
            self.cnt[e] = 0
        self.waited = {e: {} for e in self.eng}
        self.res = {}
        self.same_engine_sync = True
        self.ninstr = {e: 0 for e in self.eng}
        self.pending = []
        self.defer = None

    def dma_sem(self, name):
        if name not in self.semh:
            self.semh[name] = self.nc.semaphore("d_" + name).__enter__()
            self.cnt[name] = 0
        return name

    def _state(self, key):
        st = self.res.get(key)
        if st is None:
            fence = key[0].fence if isinstance(key[0], Buf) else {}
            st = {"w": None, "r": dict(fence)}
            self.res[key] = st
        return st

    def _collect(self, reads, writes):
        evs = {}

        def add(sk, v):
            if evs.get(sk, 0) < v:
                evs[sk] = v

        for k in reads:
            st = self._state(k)
            if st["w"]:
                add(*st["w"])
        for k in writes:
            st = self._state(k)
            if st["w"]:
                add(*st["w"])
            for sk, v in st["r"].items():
                add(sk, v)
        return evs

    def _wait(self, eng, evs):
        for sk, v in evs.items():
            if sk == eng and (eng == "pe" or eng == "sp" or not self.same_engine_sync):
                continue
            if self.waited[eng].get(sk, 0) < v:
                self.eng[eng].wait_ge(self.semh[sk], v)
                self.waited[eng][sk] = v

    def _record(self, ev, reads, writes):
        for k in reads:
            st = self._state(k)
            if st["r"].get(ev[0], 0) < ev[1]:
                st["r"][ev[0]] = ev[1]
        for k in writes:
            self.res[k] = {"w": ev, "r": {}}

    def stop_defer(self):
        self.pending = self.defer
        self.defer = None

    def replay(self, n=None):
        q = self.pending
        assert self.defer is None
        k = len(q) if n is None else min(n, len(q))
        for ent in q[:k]:
            if ent[0] == "op":
                self.op(*ent[1:])
            else:
                ent[1].free(*ent[2])
        self.pending = q[k:]
        return len(self.pending)

    def op(self, eng, fn, reads=(), writes=()):
        if self.defer is not None:
            self.defer.append(("op", eng, fn, list(reads), list(writes)))
            return
        self._wait(eng, self._collect(reads, writes))
        ins = fn(self.eng[eng])
        self.cnt[eng] += 1
        self.ninstr[eng] += 1
        ins.then_inc(self.semh[eng], 1)
        self._record((eng, self.cnt[eng]), reads, writes)

    def dma(self, queue, semname, out, in_, reads=(), writes=(), **kw):
        self.dma_sem(semname)
        self._wait(queue, self._collect(reads, writes))
        ins = self.eng[queue].dma_start(out=out, in_=in_, **kw)
        self.cnt[semname] += 16
        ins.then_inc(self.semh[semname], 16)
        self._record((semname, self.cnt[semname]), reads, writes)

    def collective(self, semname, fn, reads=(), writes=()):
        self.dma_sem(semname)
        self._wait("pool", self._collect(reads, writes))
        ins = fn(self.eng["pool"])
        self.cnt[semname] += 1
        ins.then_inc(self.semh[semname], 1)
        self._record((semname, self.cnt[semname]), reads, writes)

    def retire(self, bufs):
        fence = {}
        for key, st in self.res.items():
            if isinstance(key[0], Buf) and key[0] in bufs:
                if st["w"] and fence.get(st["w"][0], 0) < st["w"][1]:
                    fence[st["w"][0]] = st["w"][1]
                for sk, v in st["r"].items():
                    if fence.get(sk, 0) < v:
                        fence[sk] = v
        for b in bufs:
            for sk, v in b.fence.items():
                if fence.get(sk, 0) < v:
                    fence[sk] = v
        return fence

    def final_wait(self, eng):
        for sk, v in self.cnt.items():
            if v > 0 and sk != eng and self.waited[eng].get(sk, 0) < v:
                self.eng[eng].wait_ge(self.semh[sk], v)
                self.waited[eng][sk] = v


class Arena:
    def __init__(self, nc, tr, nbytes):
        self.tr = tr
        self.n = nbytes
        self.t = nc.sbuf_tensor("arena", [128, nbytes // 2], BF16).__enter__()
        self.live = []
        self.retired = []

    def alloc(self, name, shape, dtype):
        esz = 4 if dtype in (F32, I32) else 2
        nel = int(np.prod(shape[1:]))
        nb = (nel * esz + 63) // 64 * 64
        pos = 0
        for s, e, _ in sorted(self.live, key=lambda z: z[0]):
            if pos + nb <= s:
                break
            pos = max(pos, e)
        assert pos + nb <= self.n, f"arena full allocating {name} {shape}: live={[(b.name, s, e) for s, e, b in self.live]}"
        fence = {}
        keep = []
        for s, e, f in self.retired:
            if s < pos + nb and pos < e:
                for sk, v in f.items():
                    if fence.get(sk, 0) < v:
                        fence[sk] = v
                if not (pos <= s and e <= pos + nb):
                    keep.append((s, e, f))
            else:
                keep.append((s, e, f))
        self.retired = keep
        b = Buf(name, fence)
        self.live.append((pos, pos + nb, b))
        ap = self.t[:, pos // 2: pos // 2 + nel * esz // 2]
        if esz == 4:
            ap = ap.bitcast(dtype)
        elif dtype != BF16:
            ap = ap.bitcast(dtype)
        if len(shape) > 2:
            names = [f"d{i}" for i in range(len(shape) - 1)]
            ap = ap.rearrange(f"p ({' '.join(names)}) -> p {' '.join(names)}", **{n: v for n, v in zip(names[:-1], shape[1:-1])})
        b.ap = ap
        b.shape = shape
        return b

    def free(self, *bufs):
        if self.tr.defer is not None:
            self.tr.defer.append(("free", self, bufs))
            return
        fence = self.tr.retire(set(bufs))
        for b in bufs:
            ent = [z for z in self.live if z[2] is b]
            assert ent, b.name
            self.live.remove(ent[0])
            self.retired.append((ent[0][0], ent[0][1], fence))


class PsumPool:
    def __init__(self, banks):
        self.banks = banks
        self.i = 0

    def next(self):
        b = self.banks[self.i % len(self.banks)]
        self.i += 1
        return b


def build_nc(debug=None):
    nc = bass.Bass("TRN2", target_bir_lowering=False)
    tr = Tracker(nc)
    dbg = {}

    def din(name, shape, dt=F32):
        return nc.dram_tensor(name, list(shape), dt, kind="ExternalInput").ap()

    x_d = din("x", [TT, D])
    w_in_d = din("w_in", [D, 5632])
    nw_d = din("nw", [128, 8])
    lbl_d = din("lbl", [128, 2, 4])
    hnw_d = din("hnw", [128, 4])
    use_d = din("use", [128, 4])
    identb_d = din("identb", [128, 128], BF16)
    maskbc_d = din("maskbc", [128, 128])
    mask512_d = din("mask512", [128, 512])
    wpb_d = din("w_proj_b", [512, D])
    wpa_d = din("w_proj_a", [512, D])
    wout_d = din("w_out", [D, D])
    wglu_d = din("w_glu", [512, 512])
    bglu_d = din("bglu", [128, 4])
    fnw_d = din("fnw", [128, D])
    lamre_d = din("lamre", [128, 16])
    lamim_d = din("lamim", [128, 16])
    logdt_d = din("logdt", [128, 16])
    bre_d = din("bre", [128, 16, 16])
    bim_d = din("bim", [128, 16, 16])
    cre_d = din("cre", [128, 16, 16])
    cim_d = din("cim", [128, 16, 16])
    dbc_d = din("dbc", [128, 32, 16])
    kv_d = din("kv", [128, 41])
    cidx_d = din("cidx", [128, 128])
    identf_d = din("identf", [128, 128])
    maskT_d = din("maskT", [128, 256])
    out_d = nc.dram_tensor("out", [TT, D], F32, kind="ExternalOutput").ap()
    toep_dr = nc.dram_tensor("toep_dr", [128, 32, 256], BF16)
    cp_dr = nc.dram_tensor("cp_dr", [128, 2, 16, 256], BF16)
    bp_dr = nc.dram_tensor("bp_dr", [128, 2, 16, 2, 128], BF16)
    tab_dr = nc.dram_tensor("tab_dr", [128, 3, 16, 128], F32)
    agi_s = nc.dram_tensor("agi_s", [128, 32], F32)
    ago_s = nc.dram_tensor("ago_s", [4 * 128, 32], F32)
    if debug:
        dbg["yb"] = nc.dram_tensor("dbg_yb", [128, 4, TT], BF16, kind="ExternalOutput").ap()
        dbg["sin"] = nc.dram_tensor("dbg_sin", [128, 4, 128], F32, kind="ExternalOutput").ap()
        dbg["ya"] = nc.dram_tensor("dbg_ya", [128, 4, TT], BF16, kind="ExternalOutput").ap()
        dbg["toep"] = nc.dram_tensor("dbg_toep", [128, 32, 256], BF16, kind="ExternalOutput").ap()
        dbg["hin"] = nc.dram_tensor("dbg_hin", [128, 2, 16], F32, kind="ExternalOutput").ap()
        dbg["yfm"] = nc.dram_tensor("dbg_yfm", [128, 4, TT], BF16, kind="ExternalOutput").ap()
    agi_h = nc.dram_tensor("agi_h", [128, 516], F32)
    ago_h = nc.dram_tensor("ago_h", [4 * 128, 516], F32)

    ar = Arena(nc, tr, 192 * 1024)
    stop_at = debug.get("stop") if isinstance(debug, dict) else None
    if isinstance(debug, dict) and debug.get("nosync"):
        tr.same_engine_sync = False

    class _Stop(Exception):
        pass

    def checkpoint(name):
        if stop_at == name:
            raise _Stop()

    def sb(name, shape, dt=F32):
        t = nc.sbuf_tensor(name, list(shape), dt).__enter__()
        b = Buf(name, {})
        b.ap = t[:] if False else t
        return b, t

    cst = {}
    for name, shape, dt, src in [
        ("nw", [128, 8], F32, nw_d), ("lbl", [128, 2, 4], F32, lbl_d), ("hnw", [128, 4], F32, hnw_d),
        ("use", [128, 4], F32, use_d), ("identb", [128, 128], BF16, identb_d),
        ("maskbc", [128, 128], F32, maskbc_d), ("mask512", [128, 512], F32, mask512_d),
        ("bglu", [128, 4], F32, bglu_d),
    ]:
        b, t = sb("c_" + name, shape, dt)
        cst[name] = (b, t)
        tr.dma("sp", "const", t[:], src, writes=[b.k()])
    for name in cst:
        tr.res[cst[name][0].k()] = {"w": ("const", tr.cnt["const"]), "r": {}}
    CB = lambda n: cst[n][0]
    CT = lambda n: cst[n][1]

    def small(name, shape, dt=F32):
        b, t = sb(name, shape, dt)
        return b, t

    ps_f = [nc.psum_tensor(f"psf{i}", [128, 512], F32).__enter__() for i in range(6)]
    ps_b = [nc.psum_tensor(f"psb{i}", [128, 1024], BF16).__enter__() for i in range(2)]
    psk_f = [("psf", i) for i in range(6)]
    psk_b = [("psb", i) for i in range(2)]
    mmpool = PsumPool([0, 1, 2, 3])
    smpool = PsumPool([4, 5])
    trpool = PsumPool([0, 1])

    xnT = ar.alloc("xnT", [128, 8, TT], BF16)
    wbf = [ar.alloc(f"wbf{i}", [128, 8, 512], BF16) for i in range(3)]
    def load_weight_cols(dst_buf, dst_ap_fn, src_d, kcs, col0, ncols, scale=None, dst_keys=None, eng="pool", only=None):
        assert scale is None
        for c0 in range(0, ncols, 256):
            if only is not None and c0 != only:
                continue
            src = src_d[:, col0 + c0: col0 + c0 + 256].rearrange("(kc p) c -> p kc c", p=128)
            keys = dst_keys if dst_keys is not None else [dst_buf.k()]
            tr.dma("pool", "w_" + dst_buf.name, dst_ap_fn(c0), src, writes=keys)

    def load_win_group(slot, col0, eng="pool", **kw):
        wb = wbf[slot]
        load_weight_cols(wb, lambda c0: wb.ap[:, :, c0:c0 + 256], w_in_d, 8, col0, 512, eng=eng, **kw)

    SL_F, SL_Q, SL_I, SL_OG, SL_ZB, SL_U, SL_ZA = 0, 1, 2, 3, 0, 1, 2

    def sigmoid3(dst_ap, dst_keys, src_ap, src_keys, tmp, nbias=None, nbias_keys=()):
        dk = list(dst_keys)
        if nbias is None:
            tr.op("act", lambda e: e.activation(out=dst_ap, in_=src_ap, func=AF.Exp, scale=-1.0), reads=list(src_keys), writes=dk)
        else:
            tr.op("act", lambda e: e.activation(out=dst_ap, in_=src_ap, func=AF.Exp, scale=-1.0, bias=nbias),
                  reads=list(src_keys) + list(nbias_keys), writes=dk)
        tr.op("act", lambda e: e.activation(out=dst_ap, in_=dst_ap, func=AF.Ln, bias=1.0), reads=dk, writes=dk)
        tr.op("act", lambda e: e.activation(out=dst_ap, in_=dst_ap, func=AF.Exp, scale=-1.0), reads=dk, writes=dk)

    def sigmoid_dve(dst_ap, dst_keys, src_ap, src_keys):
        dk = list(dst_keys)
        tr.op("act", lambda e: e.activation(out=dst_ap, in_=src_ap, func=AF.Exp, scale=-1.0), reads=list(src_keys), writes=dk)
        tr.op("dve", lambda e: e.tensor_scalar(out=dst_ap, in0=dst_ap, scalar1=1.0, scalar2=None, op0=ALU.add), reads=dk, writes=dk)
        tr.op("dve", lambda e: e.reciprocal(out=dst_ap, in_=dst_ap), reads=dk, writes=dk)

    def rstd_act(ap, keys, inv_n):
        tr.op("act", lambda e: e.activation(out=ap, in_=ap, func=AF.Ln, scale=inv_n, bias=epsc[:, 0:1]), reads=list(keys) + [epsc_b.k()], writes=list(keys))
        tr.op("act", lambda e: e.activation(out=ap, in_=ap, func=AF.Exp, scale=-0.5), reads=list(keys), writes=list(keys))

    epsc_b, epsc = small("epsc", [128, 1])
    tr.op("dve", lambda e: e.memset(epsc[:, :], EPS), writes=[epsc_b.k()])

    a2k_b, a2k = small("a2k", [128, 2, 16])
    rho16_b, rho16 = small("rho16", [128, 16])
    PI = math.pi

    def bk(bs):
        return [b.k() for b in bs]

    s5h = {}

    def s5_setup():
        A = lambda n, shp, dt=F32: ar.alloc("s5_" + n, shp, dt)
        P = {}
        for n, shp, src in [("lamre", [128, 16], lamre_d), ("lamim", [128, 16], lamim_d), ("logdt", [128, 16], logdt_d),
                            ("bre", [128, 16, 16], bre_d), ("bim", [128, 16, 16], bim_d), ("cre", [128, 16, 16], cre_d),
                            ("cim", [128, 16, 16], cim_d), ("dbc", [128, 32, 16], dbc_d), ("kv", [128, 41], kv_d),
                            ("cidx", [128, 128], cidx_d), ("identf", [128, 128], identf_d), ("maskT", [128, 256], maskT_d)]:
            b = A(n, shp)
            tr.dma("sp", "const2", b.ap, src, writes=[b.k()])
            P[n] = b
        for b in P.values():
            tr.res[b.k()] = {"w": ("const2", tr.cnt["const2"]), "r": {}}
        tr.defer = []

        def tt(eng, out_b, out_ap, a_b, a_ap, b_b, b_ap, op):
            tr.op(eng, lambda e: e.tensor_tensor(out=out_ap, in0=a_ap, in1=b_ap, op=op), reads=bk([a_b, b_b]), writes=bk([out_b]))

        def ts(eng, out_b, out_ap, a_b, a_ap, s1, s2, op0, op1=None):
            if op1 is None:
                tr.op(eng, lambda e: e.tensor_scalar(out=out_ap, in0=a_ap, scalar1=s1, scalar2=None, op0=op0), reads=bk([a_b]), writes=bk([out_b]))
            else:
                tr.op(eng, lambda e: e.tensor_scalar(out=out_ap, in0=a_ap, scalar1=s1, scalar2=s2, op0=op0, op1=op1), reads=bk([a_b]), writes=bk([out_b]))

        def act(out_b, out_ap, a_b, a_ap, func, **kw):
            tr.op("act", lambda e: e.activation(out=out_ap, in_=a_ap, func=func, **kw), reads=bk([a_b]), writes=bk([out_b]))

        def range_reduce(ang_b, shape):
            ti = A("rr_i", shape, I32)
            tf = A("rr_f", shape, F32)
            ts("dve", ti, ti.ap, ang_b, ang_b.ap, 1.0 / TWO_PI, None, ALU.mult)
            tr.op("dve", lambda e: e.tensor_copy(out=tf.ap, in_=ti.ap), reads=bk([ti]), writes=bk([tf]))
            tr.op("dve", lambda e: e.scalar_tensor_tensor(out=ang_b.ap, in0=tf.ap, scalar=-TWO_PI, in1=ang_b.ap, op0=ALU.mult, op1=ALU.add),
                  reads=bk([tf, ang_b]), writes=bk([ang_b]))
            ts("dve", ang_b, ang_b.ap, ang_b, ang_b.ap, -PI, PI, ALU.max, ALU.min)
            ar.free(ti, tf)

        def sincos(r_b, sin_b, sin_ap, cos_b, cos_ap, shape):
            ab = A("sc_ab", shape)
            act(sin_b, sin_ap, r_b, r_b.ap, AF.Sin)
            act(ab, ab.ap, r_b, r_b.ap, AF.Sin, scale=0.5)
            tt("dve", ab, ab.ap, ab, ab.ap, ab, ab.ap, ALU.mult)
            tr.op("dve", lambda e: e.tensor_scalar(out=cos_ap, in0=ab.ap, scalar1=-2.0, scalar2=1.0, op0=ALU.mult, op1=ALU.add),
                  reads=[ab.k()], writes=[cos_b.k()])
            ar.free(ab)

        hpi_b, hpi = small("hpi", [128, 1])

        NK = 41
        lr, dtt, lrd, lid = A("lr", [128, 16]), A("dt", [128, 16]), A("lrd", [128, 16]), A("lid", [128, 16])
        ts("dve", lr, lr.ap, P["lamre"], P["lamre"].ap, -1e-4, None, ALU.min)
        act(dtt, dtt.ap, P["logdt"], P["logdt"].ap, AF.Exp)
        tt("dve", lrd, lrd.ap, lr, lr.ap, dtt, dtt.ap, ALU.mult)
        tt("dve", lid, lid.ap, P["lamim"], P["lamim"].ap, dtt, dtt.ap, ALU.mult)
        E, ang = A("E", [128, 16, NK]), A("ang", [128, 16, NK])
        kvb = P["kv"].ap.unsqueeze(1).to_broadcast([128, 16, NK])
        tt("dve", E, E.ap, lrd, lrd.ap.unsqueeze(2).to_broadcast([128, 16, NK]), P["kv"], kvb, ALU.mult)
        tt("dve", ang, ang.ap, lid, lid.ap.unsqueeze(2).to_broadcast([128, 16, NK]), P["kv"], kvb, ALU.mult)
        act(E, E.ap, E, E.ap, AF.Exp)
        range_reduce(ang, [128, 16, NK])
        Sn, Cs = A("Sn", [128, 16, NK]), A("Cs", [128, 16, NK])
        sincos(ang, Sn, Sn.ap, Cs, Cs.ap, [128, 16, NK])
        EC, ES, nEC, nES = A("EC", [128, 16, NK]), A("ES", [128, 16, NK]), A("nEC", [128, 16, NK]), A("nES", [128, 16, NK])
        tt("dve", EC, EC.ap, E, E.ap, Cs, Cs.ap, ALU.mult)
        tt("dve", ES, ES.ap, E, E.ap, Sn, Sn.ap, ALU.mult)
        ts("dve", nEC, nEC.ap, EC, EC.ap, -1.0, None, ALU.mult)
        ts("dve", nES, nES.ap, ES, ES.ap, -1.0, None, ALU.mult)
        ar.free(E, ang, Sn, Cs)
        checkpoint("setup1")
        den, t0, nr, cfr, cfi = A("den", [128, 16]), A("t0", [128, 16]), A("nr", [128, 16]), A("cfr", [128, 16]), A("cfi", [128, 16])
        li = P["lamim"]
        abre, abim = EC.ap[:, :, 9], ES.ap[:, :, 9]
        tt("dve", t0, t0.ap, lr, lr.ap, lr, lr.ap, ALU.mult)
        tt("dve", den, den.ap, li, li.ap, li, li.ap, ALU.mult)
        tt("dve", den, den.ap, den, den.ap, t0, t0.ap, ALU.add)
        tr.op("dve", lambda e: e.reciprocal(out=den.ap, in_=den.ap), reads=bk([den]), writes=bk([den]))
        ts("dve", nr, nr.ap, EC, abre, -1.0, None, ALU.add)
        tt("dve", cfr, cfr.ap, nr, nr.ap, lr, lr.ap, ALU.mult)
        tt("dve", t0, t0.ap, ES, abim, li, li.ap, ALU.mult)
        tt("dve", cfr, cfr.ap, cfr, cfr.ap, t0, t0.ap, ALU.add)
        tt("dve", cfr, cfr.ap, cfr, cfr.ap, den, den.ap, ALU.mult)
        tt("dve", cfi, cfi.ap, ES, abim, lr, lr.ap, ALU.mult)
        tt("dve", t0, t0.ap, nr, nr.ap, li, li.ap, ALU.mult)
        tt("dve", cfi, cfi.ap, cfi, cfi.ap, t0, t0.ap, ALU.subtract)
        tt("dve", cfi, cfi.ap, cfi, cfi.ap, den, den.ap, ALU.mult)
        bbre, bbim, t1s, t2s = A("bbre", [128, 16, 16]), A("bbim", [128, 16, 16]), A("t1s", [128, 16, 16]), A("t2s", [128, 16, 16])
        cb = lambda b: b.ap.unsqueeze(2).to_broadcast([128, 16, 16])
        tt("dve", t1s, t1s.ap, cfr, cb(cfr), P["bre"], P["bre"].ap, ALU.mult)
        tt("dve", t2s, t2s.ap, cfi, cb(cfi), P["bim"], P["bim"].ap, ALU.mult)
        tt("dve", bbre, bbre.ap, t1s, t1s.ap, t2s, t2s.ap, ALU.subtract)
        tt("dve", t1s, t1s.ap, cfr, cb(cfr), P["bim"], P["bim"].ap, ALU.mult)
        tt("dve", t2s, t2s.ap, cfi, cb(cfi), P["bre"], P["bre"].ap, ALU.mult)
        tt("dve", bbim, bbim.ap, t1s, t1s.ap, t2s, t2s.ap, ALU.add)
        ar.free(den, t0, nr, cfr, cfi, t1s, t2s)
        checkpoint("setup1b")

        def outer(eng, out_b, out_ap, pw_b, lo, ns, vec_b):
            tr.op(eng, lambda e: e.tensor_tensor(out=out_ap, in0=pw_b.ap[:, :, lo:lo + ns].unsqueeze(3).to_broadcast([128, 16, ns, 16]),
                                                 in1=vec_b.ap.unsqueeze(2).to_broadcast([128, 16, ns, 16]), op=ALU.mult),
                  reads=bk([pw_b, vec_b]), writes=bk([out_b]))

        tr.stop_defer()
        yield
        Xre, Xim, o1, o1d = A("Xre", [128, 16, 16, 16]), A("Xim", [128, 16, 16, 16]), A("o1", [128, 16, 16, 16]), A("o1d", [128, 16, 16, 16])
        Yre, Yim = A("Yre", [128, 16, 8, 16]), A("Yim", [128, 16, 8, 16])
        outer("pool", Xre, Xre.ap, EC, 9, 16, P["cre"])
        outer("pool", o1, o1.ap, ES, 9, 16, P["cim"])
        tt("pool", Xre, Xre.ap, Xre, Xre.ap, o1, o1.ap, ALU.subtract)
        outer("dve", Xim, Xim.ap, nEC, 9, 16, P["cim"])
        outer("dve", o1d, o1d.ap, nES, 9, 16, P["cre"])
        tt("dve", Xim, Xim.ap, Xim, Xim.ap, o1d, o1d.ap, ALU.add)
        o1y = o1d.ap[:, :, 0:8, :]
        outer("dve", Yre, Yre.ap, EC, 0, 8, bbre)
        outer("dve", o1d, o1y, ES, 0, 8, bbim)
        tt("dve", Yre, Yre.ap, Yre, Yre.ap, o1d, o1y, ALU.subtract)
        outer("dve", Yim, Yim.ap, EC, 0, 8, bbim)
        outer("dve", o1d, o1y, ES, 0, 8, bbre)
        tt("dve", Yim, Yim.ap, Yim, Yim.ap, o1d, o1y, ALU.add)
        ar.free(o1, o1d)
        cpb = A("cpb", [128, 2, 16, 256], BF16)
        act(cpb, cpb.ap[:, 0], Xre, Xre.ap.rearrange("p a s q -> p a (s q)"), AF.Copy)
        act(cpb, cpb.ap[:, 1], Xim, Xim.ap.rearrange("p a s q -> p a (s q)"), AF.Copy)
        tr.dma("sp", "s5st", cp_dr.ap(), cpb.ap, reads=bk([cpb]), writes=[("cp_dr",)])
        ar.free(cpb)
        checkpoint("setup2")
        bsnb = A("bsnb", [128, 2, 16, 256], BF16)
        o1q, o2q = A("o1q", [128, 4, 16, 16]), A("o2q", [128, 4, 16, 16])
        bsn_ops = []

        def outer_q(out_b, pw_b, vec_b, q):
            bsn_ops.append(lambda: tr.op("dve", lambda e: e.tensor_tensor(
                out=out_b.ap, in0=pw_b.ap[:, 4 * q:4 * q + 4, 25:41].unsqueeze(3).to_broadcast([128, 4, 16, 16]),
                in1=vec_b.ap[:, 4 * q:4 * q + 4, :].unsqueeze(2).to_broadcast([128, 4, 16, 16]), op=ALU.mult),
                reads=bk([pw_b, vec_b]), writes=bk([out_b])))

        def comb_q(ri, q, op):
            fl = lambda b: b.ap.rearrange("p a s q -> p a (s q)")
            bsn_ops.append(lambda: tr.op("dve", lambda e: e.tensor_tensor(out=bsnb.ap[:, ri, 4 * q:4 * q + 4, :], in0=fl(o1q), in1=fl(o2q), op=op),
                                         reads=bk([o1q, o2q]), writes=[bsnb.k(ri, q)]))

        for q in range(4):
            outer_q(o1q, EC, bbre, q)
            outer_q(o2q, ES, bbim, q)
            comb_q(0, q, ALU.subtract)
            outer_q(o1q, EC, bbim, q)
            outer_q(o2q, ES, bbre, q)
            comb_q(1, q, ALU.add)
        toepb = A("toepb", [128, 32, 256], BF16)
        tmpT = [A(f"tmpT{j}", [128, 256]) for j in range(2)]
        dg = [A(f"dg{j}", [128, 128]) for j in range(2)]
        for g in range(32):
            pr, g2 = g // 2, g % 2
            rows = slice(g2 * 64, g2 * 64 + 64)
            bi = mmpool.next()
            tr.op("pe", lambda e: e.matmul(ps_f[bi][:, 0:256], lhsT=Yre.ap[rows, pr].rearrange("p s q -> p (s q)"),
                                           rhs=Xre.ap[rows, pr].rearrange("p s q -> p (s q)"), start=True, stop=False),
                  reads=bk([Yre, Xre]), writes=[psk_f[bi]])
            tr.op("pe", lambda e: e.matmul(ps_f[bi][:, 0:256], lhsT=Yim.ap[rows, pr].rearrange("p s q -> p (s q)"),
                                           rhs=Xim.ap[rows, pr].rearrange("p s q -> p (s q)"), start=False, stop=True),
                  reads=bk([Yim, Xim]), writes=[psk_f[bi]])
            tT, dG = tmpT[g % 2], dg[g % 2]
            tr.op("dve", lambda e: e.tensor_tensor(out=tT.ap, in0=ps_f[bi][:, 0:256], in1=P["maskT"].ap, op=ALU.mult),
                  reads=[psk_f[bi], P["maskT"].k()], writes=bk([tT]))
            tr.op("pool", lambda e: e.tensor_tensor(out=dG.ap.rearrange("p (s q) -> p s q", s=8),
                                                    in0=P["identf"].ap.rearrange("p (s q) -> p s q", s=8),
                                                    in1=P["dbc"].ap[:, g, :].unsqueeze(1).to_broadcast([128, 8, 16]), op=ALU.mult),
                  reads=bk([P["identf"], P["dbc"]]), writes=bk([dG]))
            tr.op("dve", lambda e: e.tensor_tensor(out=toepb.ap[:, g, 0:128], in0=tT.ap[:, 0:128], in1=dG.ap, op=ALU.add),
                  reads=bk([tT, dG]), writes=[toepb.k(g, 0)])
            tr.op("act", lambda e: e.activation(out=toepb.ap[:, g, 128:256], in_=tT.ap[:, 128:256], func=AF.Copy),
                  reads=bk([tT]), writes=[toepb.k(g, 1)])
            if g >= 4 and bsn_ops:
                bsn_ops.pop(0)()
        tr.dma("sp", "s5st", toep_dr.ap(), toepb.ap, reads=[toepb.k(g, j) for g in range(32) for j in range(2)], writes=[("toep_dr",)])
        if debug:
            tr.dma("sp", "dbg", dbg["toep"], toepb.ap, reads=[toepb.k(g, j) for g in range(32) for j in range(2)])
        while bsn_ops:
            bsn_ops.pop(0)()
        ar.free(Xre, Xim, Yre, Yim, toepb, *tmpT, *dg)
        checkpoint("setup3")
        yield
        ar.free(o1q, o2q, bbre, bbim, EC, ES, nEC, nES, *[P[n] for n in ("bre", "bim", "cre", "cim", "dbc", "maskT", "identf", "kv", "lamre", "lamim", "logdt")])
        bpb = A("bpb", [128, 2, 16, 2, 128], BF16)
        n_ev = 0
        for j in range(2):
            for prb in range(4):
                pb_i = trpool.next()
                pt = ps_b[pb_i]
                for pr in range(4 * prb, 4 * prb + 4):
                    for ri in range(2):
                        col = ((pr % 4) * 2 + ri) * 128
                        tr.op("pe", lambda e: e.transpose(out=pt[:, col:col + 128], in_=bsnb.ap[:, ri, pr, j * 128:(j + 1) * 128],
                                                          identity=CT("identb")[:]),
                              reads=[bsnb.k(ri, pr // 4), CB("identb").k()], writes=[psk_b[pb_i]])
                dst = bpb.ap[:, j, 4 * prb:4 * prb + 4].rearrange("p a r n -> p (a r n)")
                if n_ev % 2 == 0:
                    tr.op("dve", lambda e: e.tensor_copy(out=dst, in_=pt[:, :]), reads=[psk_b[pb_i]], writes=[bpb.k(j, prb)])
                else:
                    tr.op("act", lambda e: e.activation(out=dst, in_=pt[:, :], func=AF.Copy), reads=[psk_b[pb_i]], writes=[bpb.k(j, prb)])
                n_ev += 1
        tr.dma("sp", "s5st", bp_dr.ap(), bpb.ap, reads=[bpb.k(j, gb) for j in range(2) for gb in range(4)], writes=[("bp_dr",)])
        ar.free(bsnb, bpb)
        checkpoint("setup4")
        yield
        lrd16, phi = A("lrd16", [128, 16]), A("phi", [128, 16])
        ts("dve", lrd16, lrd16.ap, lrd, lrd.ap, 16.0, None, ALU.mult)
        ts("dve", phi, phi.ap, lid, lid.ap, 16.0, None, ALU.mult)
        range_reduce(phi, [128, 16])
        act(rho16_b, rho16[:, :], lrd16, lrd16.ap, AF.Exp)
        tabs = A("tabs", [128, 3, 16, 128])
        angT = A("angT", [128, 16, 128])
        cib = P["cidx"].ap.unsqueeze(1).to_broadcast([128, 16, 128])
        tt("dve", tabs, tabs.ap[:, 0], lrd16, lrd16.ap.unsqueeze(2).to_broadcast([128, 16, 128]), P["cidx"], cib, ALU.mult)
        act(tabs, tabs.ap[:, 0], tabs, tabs.ap[:, 0], AF.Exp)
        tt("dve", angT, angT.ap, phi, phi.ap.unsqueeze(2).to_broadcast([128, 16, 128]), P["cidx"], cib, ALU.mult)
        range_reduce(angT, [128, 16, 128])
        sincos(angT, tabs, tabs.ap[:, 2], tabs, tabs.ap[:, 1], [128, 16, 128])
        tt("dve", a2k_b, a2k[:, 0, :], tabs, tabs.ap[:, 0, :, 127], tabs, tabs.ap[:, 1, :, 127], ALU.mult)
        tt("dve", a2k_b, a2k[:, 1, :], tabs, tabs.ap[:, 0, :, 127], tabs, tabs.ap[:, 2, :, 127], ALU.mult)
        s5h["tabs"] = tabs
        ar.free(angT, lrd16, phi, lr, dtt, lrd, lid, P["cidx"])

    setup_gen = s5_setup()
    next(setup_gen)

    def rest():
        xb = [ar.alloc(f"xb{i}", [128, D], F32) for i in range(3)]
        xnb = [ar.alloc(f"xnb{i}", [128, D], BF16) for i in range(2)]
        junk = ar.alloc("junk", [128, D], BF16)
        ssq_b, ssq = small("ssq", [128, NT])
        rstd_b, rstd = small("rstd", [128, NT])
        early = {0: (SL_F, C_F, 0), 2: (SL_F, C_F, 256), 4: (SL_Q, C_Q, 0), 6: (SL_Q, C_Q, 256), 8: (SL_I, C_I, 0), 10: (SL_I, C_I, 256)}
        for i in range(NT):
            tr.replay(5)
            if i in early:
                load_win_group(early[i][0], early[i][1], eng="dve", only=early[i][2])
            xt = xb[i % 3]
            tr.dma("sp", f"xb{i % 3}", xt.ap, x_d[i * 128:(i + 1) * 128, :], writes=[xt.k()])
            tr.op("act", lambda e: e.activation(out=junk.ap, in_=xt.ap, func=AF.Square, accum_out=ssq[:, i:i + 1]),
                  reads=[xt.k()], writes=[junk.k(), ssq_b.k(i)])
            tr.op("act", lambda e: e.activation(out=rstd[:, i:i + 1], in_=ssq[:, i:i + 1], func=AF.Ln, scale=1.0 / D, bias=epsc[:, 0:1]),
                  reads=[ssq_b.k(i), epsc_b.k()], writes=[rstd_b.k(i)])
            tr.op("act", lambda e: e.activation(out=rstd[:, i:i + 1], in_=rstd[:, i:i + 1], func=AF.Exp, scale=-0.5),
                  reads=[rstd_b.k(i)], writes=[rstd_b.k(i)])
            xn = xnb[i % 2]
            if i % 2:
                tr.op("dve", lambda e: e.tensor_scalar(out=xn.ap, in0=xt.ap, scalar1=rstd[:, i:i + 1], scalar2=None, op0=ALU.mult),
                      reads=[xt.k(), rstd_b.k(i)], writes=[xn.k()])
            else:
                tr.op("act", lambda e: e.activation(out=xn.ap, in_=xt.ap, func=AF.Copy, scale=rstd[:, i:i + 1]),
                      reads=[xt.k(), rstd_b.k(i)], writes=[xn.k()])
            pb_i = trpool.next()
            pt = ps_b[pb_i]
            for kc in range(8):
                tr.op("pe", lambda e: e.transpose(out=pt[:, kc * 128:(kc + 1) * 128], in_=xn.ap[:, kc * 128:(kc + 1) * 128],
                                                  identity=CT("identb")[:]),
                      reads=[xn.k(), CB("identb").k()], writes=[psk_b[pb_i]])
            tr.op("dve", lambda e: e.tensor_tensor(out=xnT.ap[:, :, i * 128:(i + 1) * 128],
                                                   in0=pt[:, :].rearrange("p (a b) -> p a b", a=8),
                                                   in1=CT("nw")[:, 0:8].unsqueeze(2).to_broadcast([128, 8, 128]), op=ALU.mult),
                  reads=[psk_b[pb_i], CB("nw").k()], writes=[xnT.k(i)])
        ar.free(*xb, *xnb, junk)
        tr.replay()
        checkpoint("phaseA")
        V = ar.alloc("V", [128, NT, 512], BF16)
        for i in range(NT):
            bi = mmpool.next()
            for kc in range(8):
                tr.op("pe", lambda e: e.matmul(ps_f[bi][:, :], lhsT=xnT.ap[:, kc, i * 128:(i + 1) * 128],
                                               rhs=wbf[SL_I].ap[:, kc, :], start=(kc == 0), stop=(kc == 7)),
                      reads=[wbf[SL_I].k(), xnT.k(i)], writes=[psk_f[bi]])
            tr.op("act", lambda e: e.activation(out=V.ap[:, i, :], in_=ps_f[bi][:, :], func=AF.Copy), reads=[psk_f[bi]], writes=[V.k(i)])
        ar.free(wbf[SL_I])
        next(setup_gen)
        checkpoint("setup")
        wbf.append(ar.alloc("wbf3", [128, 8, 512], BF16))
        load_win_group(SL_OG, C_OG)

        def proj_fm(wb, ct, tb):
            bi = mmpool.next()
            for kc in range(8):
                tr.op("pe", lambda e: e.matmul(ps_f[bi][:, :], lhsT=wb.ap[:, kc, ct * 128:(ct + 1) * 128],
                                               rhs=xnT.ap[:, kc, tb * 512:(tb + 1) * 512], start=(kc == 0), stop=(kc == 7)),
                      reads=[wb.k()] + [xnT.k(4 * tb + j) for j in range(4)], writes=[psk_f[bi]])
            return bi

        def proj_tm(wb, i):
            bi = mmpool.next()
            for kc in range(8):
                tr.op("pe", lambda e: e.matmul(ps_f[bi][:, :], lhsT=xnT.ap[:, kc, i * 128:(i + 1) * 128],
                                               rhs=wb.ap[:, kc, :], start=(kc == 0), stop=(kc == 7)),
                      reads=[wb.k(), xnT.k(i)], writes=[psk_f[bi]])
            return bi

        lb_b, lb = small("lb", [128, 4])
        oml_b, oml = small("oml", [128, 4])
        noml_b, noml = small("noml", [128, 4])
        tr.op("dve", lambda e: e.tensor_sub(out=lb[:, :], in0=CT("lbl")[:, 0, :], in1=CT("lbl")[:, 1, :]),
              reads=[CB("lbl").k()], writes=[lb_b.k()])
        sigmoid3(lb[:, :], [lb_b.k()], lb[:, :], [lb_b.k()], None)
        tr.op("dve", lambda e: e.tensor_scalar(out=oml[:, :], in0=lb[:, :], scalar1=-1.0, scalar2=1.0, op0=ALU.mult, op1=ALU.add),
              reads=[lb_b.k()], writes=[oml_b.k()])
        tr.op("dve", lambda e: e.tensor_scalar(out=noml[:, :], in0=lb[:, :], scalar1=-1.0, scalar2=None, op0=ALU.add),
              reads=[lb_b.k()], writes=[noml_b.k()])

        KdT = ar.alloc("KdT", [128, 4, TT], BF16)
        QdT = ar.alloc("QdT", [128, 4, TT], BF16)
        lastc_b, lastc = small("lastc", [128, 4, 32])
        el_b, el = small("el", [128, 4, 33])
        tmp = {n: [ar.alloc(f"t_{n}{j}", [128, 512], F32) for j in range(2)] for n in ["e1", "A", "lg", "e2"]}
        lnoml_b, lnoml = small("lnoml", [128, 4])
        tr.op("act", lambda e: e.activation(out=lnoml[:, :], in_=oml[:, :], func=AF.Ln), reads=[oml_b.k()], writes=[lnoml_b.k()])
        it = 0
        for h in range(4):
            for tb in range(NB):
                j = it % 2
                it += 1
                T = {n: tmp[n][j] for n in tmp}
                pf = proj_fm(wbf[SL_F], h, tb)
                pq = proj_fm(wbf[SL_Q], h, tb)
                tr.op("act", lambda e: e.activation(out=T["e1"].ap, in_=ps_f[pf][:, :], func=AF.Exp, scale=-1.0), reads=[psk_f[pf]], writes=[T["e1"].k()])
                tr.op("act", lambda e: e.activation(out=T["A"].ap, in_=T["e1"].ap, func=AF.Ln, bias=1.0), reads=[T["e1"].k()], writes=[T["A"].k()])
                tr.op("act", lambda e: e.activation(out=T["lg"].ap, in_=T["e1"].ap, func=AF.Ln, scale=lb[:, h:h + 1], bias=1.0),
                      reads=[T["e1"].k(), lb_b.k()], writes=[T["lg"].k()])
                tr.op("act", lambda e: e.activation(out=T["e2"].ap, in_=ps_f[pq][:, :], func=AF.Exp, scale=-1.0), reads=[psk_f[pq]], writes=[T["e2"].k()])
                tr.op("act", lambda e: e.activation(out=T["e2"].ap, in_=T["e2"].ap, func=AF.Ln, bias=1.0), reads=[T["e2"].k()], writes=[T["e2"].k()])
                tr.op("dve", lambda e: e.tensor_sub(out=T["lg"].ap, in0=T["lg"].ap, in1=T["A"].ap), reads=[T["lg"].k(), T["A"].k()], writes=[T["lg"].k()])
                tr.op("dve", lambda e: e.tensor_tensor_scan(out=T["lg"].ap, data0=CT("mask512")[:, :], data1=T["lg"].ap, initial=0.0,
                                                            op0=ALU.mult, op1=ALU.add),
                      reads=[T["lg"].k(), CB("mask512").k()], writes=[T["lg"].k()])
                tr.op("dve", lambda e: e.tensor_copy(out=lastc[:, h, tb * 8:(tb + 1) * 8], in_=T["lg"].ap[:, 63:512:64]),
                      reads=[T["lg"].k()], writes=[lastc_b.k(h, tb)])
                tr.op("dve", lambda e: e.tensor_add(out=T["A"].ap, in0=T["A"].ap, in1=T["lg"].ap), reads=[T["lg"].k(), T["A"].k()], writes=[T["A"].k()])
                tr.op("dve", lambda e: e.tensor_tensor(out=T["A"].ap, in0=ps_f[pf][:, :], in1=T["A"].ap, op=ALU.add),
                      reads=[psk_f[pf], T["A"].k()], writes=[T["A"].k()])
                tr.op("act", lambda e: e.activation(out=KdT.ap[:, h, tb * 512:(tb + 1) * 512], in_=T["A"].ap, func=AF.Exp, scale=-1.0,
                                                    bias=lnoml[:, h:h + 1]),
                      reads=[T["A"].k(), lnoml_b.k()], writes=[KdT.k(h, tb)])
                tr.op("dve", lambda e: e.tensor_sub(out=T["e2"].ap, in0=T["lg"].ap, in1=T["e2"].ap), reads=[T["lg"].k(), T["e2"].k()], writes=[T["e2"].k()])
                tr.op("act", lambda e: e.activation(out=T["e2"].ap, in_=T["e2"].ap, func=AF.Exp), reads=[T["e2"].k()], writes=[T["e2"].k()])
                tr.op("dve", lambda e: e.tensor_tensor(out=QdT.ap[:, h, tb * 512:(tb + 1) * 512], in0=ps_f[pq][:, :], in1=T["e2"].ap, op=ALU.mult),
                      reads=[psk_f[pq], T["e2"].k()], writes=[QdT.k(h, tb)])
        load_win_group(SL_ZB, C_ZB)
        load_win_group(SL_U, C_U)
        for n in tmp:
            ar.free(*tmp[n])
        checkpoint("step1")
        KdTM = ar.alloc("KdTM", [128, NT, 512], BF16)
        allc = [lastc_b.k(h, tb) for h in range(4) for tb in range(NB)]
        tr.op("dve", lambda e: e.memset(el[:, :, 0:1], 1.0), writes=[el_b.k()])
        tr.op("act", lambda e: e.activation(out=el[:, :, 1:33], in_=lastc[:, :, :], func=AF.Exp), reads=allc + [el_b.k()], writes=[el_b.k()])
        pk_b, pk = small("pk", [128, 516])
        tr.op("dve", lambda e: e.reduce_sum(out=pk[:, 512:516], in_=lastc[:, :, :], axis=AX.X), reads=allc, writes=[pk_b.k("d")])
        sfx_b, sfx = small("sfx", [128, 4, 32])
        ones_b, ones = small("ones32", [128, 32])
        tr.op("dve", lambda e: e.memset(ones[:, :], 1.0), writes=[ones_b.k()])
        for h in range(4):
            tr.op("dve", lambda e: e.tensor_tensor_scan(out=sfx[:, h, :], data0=ones[:, :], data1=lastc[:, h, :], initial=0.0,
                                                        op0=ALU.mult, op1=ALU.add), reads=allc + [ones_b.k()], writes=[sfx_b.k()])
        tr.op("dve", lambda e: e.tensor_sub(out=sfx[:, :, :], in0=lastc[:, :, :], in1=sfx[:, :, :]), reads=allc + [sfx_b.k()], writes=[sfx_b.k()])
        tr.op("dve", lambda e: e.tensor_tensor(out=sfx[:, :, :], in0=sfx[:, :, :], in1=pk[:, 512:516].unsqueeze(2).to_broadcast([128, 4, 32]), op=ALU.add),
              reads=[sfx_b.k(), pk_b.k("d")], writes=[sfx_b.k()])
        tr.op("act", lambda e: e.activation(out=sfx[:, :, :], in_=sfx[:, :, :], func=AF.Exp), reads=[sfx_b.k()], writes=[sfx_b.k()])
        tr.op("act", lambda e: e.activation(out=pk[:, 512:516], in_=pk[:, 512:516], func=AF.Exp), reads=[pk_b.k("d"), sfx_b.k()], writes=[pk_b.k("d")])

        for i in range(NT):
            pb_i = trpool.next()
            pt = ps_b[pb_i]
            for h in range(4):
                tr.op("pe", lambda e: e.transpose(out=pt[:, h * 128:(h + 1) * 128], in_=KdT.ap[:, h, i * 128:(i + 1) * 128],
                                                  identity=CT("identb")[:]),
                      reads=[KdT.k(h, i // 4), CB("identb").k()], writes=[psk_b[pb_i]])
            if i % 2:
                tr.op("dve", lambda e: e.tensor_copy(out=KdTM.ap[:, i, :], in_=pt[:, 0:512]), reads=[psk_b[pb_i]], writes=[KdTM.k(i)])
            else:
                tr.op("act", lambda e: e.activation(out=KdTM.ap[:, i, :], in_=pt[:, 0:512], func=AF.Copy), reads=[psk_b[pb_i]], writes=[KdTM.k(i)])
        KdPh = [ar.alloc(f"KdP{j}", [128, TT], BF16) for j in range(2)]
        KdPTMh = [ar.alloc(f"KdPTM{j}", [128, NT, 128], BF16) for j in range(2)]
        sbank = 4
        for h in range(4):
            kp, kpt = KdPh[h % 2], KdPTMh[h % 2]
            tr.op("pool", lambda e: e.tensor_tensor(out=kp.ap.rearrange("p (c t) -> p c t", t=64),
                                                    in0=KdT.ap[:, h, :].rearrange("p (c t) -> p c t", t=64),
                                                    in1=sfx[:, h, :].unsqueeze(2).to_broadcast([128, 32, 64]), op=ALU.mult),
                  reads=[KdT.k(h, tb) for tb in range(NB)] + [sfx_b.k()], writes=[kp.k()])
            for half in range(2):
                pb_i = trpool.next()
                pt = ps_b[pb_i]
                for ii in range(8):
                    i = 8 * half + ii
                    tr.op("pe", lambda e: e.transpose(out=pt[:, ii * 128:(ii + 1) * 128], in_=kp.ap[:, i * 128:(i + 1) * 128],
                                                      identity=CT("identb")[:]),
                          reads=[kp.k(), CB("identb").k()], writes=[psk_b[pb_i]])
                if half:
                    tr.op("dve", lambda e: e.tensor_copy(out=kpt.ap[:, 8 * half:8 * half + 8, :], in_=pt[:, :].rearrange("p (a b) -> p a b", a=8)),
                          reads=[psk_b[pb_i]], writes=[kpt.k(half)])
                else:
                    tr.op("act", lambda e: e.activation(out=kpt.ap[:, 8 * half:8 * half + 8, :], in_=pt[:, :].rearrange("p (a b) -> p a b", a=8), func=AF.Copy),
                          reads=[psk_b[pb_i]], writes=[kpt.k(half)])
            for i in range(NT):
                tr.op("pe", lambda e: e.matmul(ps_f[sbank][:, h * 128:(h + 1) * 128], lhsT=kpt.ap[:, i, :],
                                               rhs=V.ap[:, i, h * 128:(h + 1) * 128], start=(i == 0), stop=(i == NT - 1)),
                      reads=[kpt.k(i // 8), V.k(i)], writes=[psk_f[sbank]])
        tr.op("dve", lambda e: e.tensor_copy(out=pk[:, 0:512], in_=ps_f[sbank][:, :]), reads=[psk_f[sbank]], writes=[pk_b.k("s")])
        ar.free(*KdPh, *KdPTMh)
        checkpoint("pass1")
        pk_keys = [pk_b.k("d"), pk_b.k("s")]
        agi_k, ago_k = ("agi_h",), ("ago_h",)
        tr.dma("pool", "agh", agi_h[:, :], pk[:, :], reads=pk_keys, writes=[agi_k])
        tr.collective("cc", lambda e: e.collective_compute("AllGather", ALU.bypass, replica_groups=[[0, 1, 2, 3], [4, 5, 6, 7]],
                                                           ins=[agi_h.ap().opt()], outs=[ago_h.ap().opt()]),
                      reads=[agi_k], writes=[ago_k])
        next(setup_gen)
        gath = ar.alloc("gath", [128, 4, 516], F32)
        tr.dma("sp", "agh2", gath.ap, ago_h[:, :].rearrange("(j p) f -> p j f", p=128), reads=[ago_k], writes=[gath.k()])
        Sin_b, Sin = small("Sin", [128, 4, 128])
        ft = ar.alloc("ft", [128, 4, 128], F32)
        tr.op("dve", lambda e: e.memset(Sin[:, :, :], 0.0), writes=[Sin_b.k()])
        for j in range(3):
            tr.op("dve", lambda e: e.tensor_tensor(out=ft.ap, in0=Sin[:, :, :],
                                                   in1=gath.ap[:, j, 512:516].unsqueeze(2).to_broadcast([128, 4, 128]), op=ALU.mult),
                  reads=[Sin_b.k(), gath.k()], writes=[ft.k()])
            tr.op("dve", lambda e: e.tensor_add(out=ft.ap, in0=ft.ap, in1=gath.ap[:, j, 0:512].rearrange("p (h v) -> p h v", h=4)),
                  reads=[ft.k(), gath.k()], writes=[ft.k()])
            tr.op("dve", lambda e: e.tensor_sub(out=ft.ap, in0=ft.ap, in1=Sin[:, :, :]), reads=[ft.k(), Sin_b.k()], writes=[ft.k()])
            tr.op("dve", lambda e: e.scalar_tensor_tensor(out=Sin[:, :, :], in0=ft.ap, scalar=CT("use")[:, j:j + 1], in1=Sin[:, :, :],
                                                          op0=ALU.mult, op1=ALU.add),
                  reads=[ft.k(), Sin_b.k(), CB("use").k()], writes=[Sin_b.k()])
        ar.free(ft, gath)
        if debug:
            tr.dma("sp", "dbg", dbg["sin"], Sin[:, :, :], reads=[Sin_b.k()])

        checkpoint("fold")
        U_b = ar.alloc("U", [128, 2, 4, 128], F32)
        U = U_b.ap
        Sbf_b = ar.alloc("Sbf", [128, 4, 4, 128], BF16)
        Sbf = Sbf_b.ap
        ybT = ar.alloc("ybT", [128, 4, TT], BF16)
        p2 = {n: [ar.alloc(f"p2_{n}{j}", [128, 512], F32) for j in range(k)] for n, k in (("sog", 3), ("szb", 4), ("go", 3))}
        scb = [ar.alloc(f"scb{j}", [128, 4, 128], BF16) for j in range(2)]
        ybt = [ar.alloc(f"ybt{j}", [128, 512], BF16) for j in range(2)]
        junk2 = ar.alloc("junk2", [128, 128], BF16)
        ss_b, ss = small("ss", [128, NT, 4])
        ps_u = [ps_b[0][:, :].bitcast(F32), ps_b[1][:, :].bitcast(F32)]

        def upd_mm(c):
            i, hh = c // 2, c % 2
            for h in range(4):
                tr.op("pe", lambda e: e.matmul(ps_u[hh][:, h * 128:(h + 1) * 128],
                                               lhsT=KdTM.ap[hh * 64:(hh + 1) * 64, i, h * 128:(h + 1) * 128],
                                               rhs=V.ap[hh * 64:(hh + 1) * 64, i, h * 128:(h + 1) * 128], start=True, stop=True),
                      reads=[KdTM.k(i), V.k(i)], writes=[psk_b[hh]])

        def upd_state(c):
            hh = c % 2
            for h in range(4):
                tr.op("dve", lambda e: e.scalar_tensor_tensor(out=U[:, hh, h, :], in0=U[:, 1 - hh, h, :], scalar=el[:, h, c:c + 1],
                                                              in1=ps_u[hh][:, h * 128:(h + 1) * 128], op0=ALU.mult, op1=ALU.add),
                      reads=[U_b.k(1 - hh, h), el_b.k(), psk_b[hh]], writes=[U_b.k(hh, h)])
            slot = (c + 1) % 4
            for h in range(4):
                tr.op("pool", lambda e: e.tensor_scalar(out=Sbf[:, h, slot, :], in0=U[:, hh, h, :], scalar1=el[:, h, c + 1:c + 2], scalar2=0.0, op0=ALU.mult, op1=ALU.add),
                      reads=[U_b.k(hh, h), el_b.k()], writes=[Sbf_b.k(slot, h)])

        def st_proj(i):
            r = {}
            for nm, wb in (("og", wbf[SL_OG]), ("zb", wbf[SL_ZB])):
                bi = p2pool.next()
                for kc in range(8):
                    tr.op("pe", lambda e: e.matmul(ps_f[bi][:, :], lhsT=xnT.ap[:, kc, i * 128:(i + 1) * 128],
                                                   rhs=wb.ap[:, kc, :], start=(kc == 0), stop=(kc == 7)),
                          reads=[wb.k(), xnT.k(i)], writes=[psk_f[bi]])
                r[nm] = bi
            pbank[i] = r

        def st_gate_act(i):
            pog, pzb = pbank[i]["og"], pbank[i]["zb"]
            sigmoid3(p2["sog"][i % 3].ap, [p2["sog"][i % 3].k()], ps_f[pog][:, :], [psk_f[pog]], None)
            sigmoid3(p2["szb"][i % 4].ap, [p2["szb"][i % 4].k()], ps_f[pzb][:, :], [psk_f[pzb]], None)

        def st_gate_dve(i):
            pzb = pbank[i]["zb"]
            tr.op("dve", lambda e: e.tensor_tensor(out=p2["szb"][i % 4].ap, in0=ps_f[pzb][:, :], in1=p2["szb"][i % 4].ap, op=ALU.mult),
                  reads=[psk_f[pzb], p2["szb"][i % 4].k()], writes=[p2["szb"][i % 4].k()])

        def st_b1(i):
            obi = 4 + i % 2
            go = p2["go"][i % 3]
            tr.op("dve", lambda e: e.tensor_mul(out=go.ap, in0=ps_f[obi][:, :], in1=p2["sog"][i % 3].ap),
                  reads=[psk_f[obi], p2["sog"][i % 3].k()], writes=[go.k()])
            for h in range(4):
                tr.op("act", lambda e: e.activation(out=junk2.ap, in_=go.ap[:, h * 128:(h + 1) * 128], func=AF.Square,
                                                    accum_out=ss[:, i, h:h + 1]),
                      reads=[go.k()], writes=[junk2.k(), ss_b.k(i, h)])

        def st_b2(i):
            go = p2["go"][i % 3]
            ssk = [ss_b.k(i, h) for h in range(4)]
            rstd_act(ss[:, i, :], ssk, 1.0 / 128)
            for h in range(4):
                cols = slice(h * 128, (h + 1) * 128)
                tr.op("dve", lambda e: e.scalar_tensor_tensor(out=ybt[i % 2].ap[:, cols], in0=go.ap[:, cols], scalar=ss[:, i, h:h + 1],
                                                              in1=p2["szb"][i % 4].ap[:, cols], op0=ALU.mult, op1=ALU.mult),
                      reads=[go.k(), p2["szb"][i % 4].k()] + ssk, writes=[ybt[i % 2].k()])

        def st_b3(i):
            pt = ps_f[3][:, :].bitcast(BF16)
            for h in range(4):
                c0 = (i % 2) * 512 + h * 128
                tr.op("pe", lambda e: e.transpose(out=pt[:, c0:c0 + 128], in_=ybt[i % 2].ap[:, h * 128:(h + 1) * 128],
                                                  identity=CT("identb")[:]),
                      reads=[ybt[i % 2].k(), CB("identb").k()], writes=[("ps3h", i % 2)])

        def st_b4(i):
            pt = ps_f[3][:, :].bitcast(BF16)
            c0 = (i % 2) * 512
            tr.op("act", lambda e: e.activation(out=ybT.ap[:, :, i * 128:(i + 1) * 128],
                                                in_=pt[:, c0:c0 + 512].rearrange("p (a b) -> p a b", a=4), func=AF.Copy),
                  reads=[("ps3h", i % 2)], writes=[ybT.k(i)])

        def st_scores(i):
            sbi = 4 + i % 2
            for h in range(4):
                tr.op("pe", lambda e: e.matmul(ps_f[sbi][:, h * 128:(h + 1) * 128], lhsT=KdT.ap[:, h, i * 128:(i + 1) * 128],
                                               rhs=QdT.ap[:, h, i * 128:(i + 1) * 128], start=True, stop=True),
                      reads=[KdT.k(h, i // 4), QdT.k(h, i // 4)], writes=[psk_f[sbi]])
            upd_mm(2 * i)
            upd_mm(2 * i + 1)

        def st_state(i):
            sbi = 4 + i % 2
            tr.op("dve", lambda e: e.tensor_tensor(out=scb[i % 2].ap, in0=ps_f[sbi][:, :].rearrange("p (h t) -> p h t", h=4),
                                                   in1=CT("maskbc")[:, :].unsqueeze(1).to_broadcast([128, 4, 128]), op=ALU.mult),
                  reads=[psk_f[sbi], CB("maskbc").k()], writes=[scb[i % 2].k()])
            upd_state(2 * i)
            upd_state(2 * i + 1)

        def st_o(i):
            sbi = 4 + i % 2
            s0, s1 = (2 * i) % 4, (2 * i + 1) % 4
            for h in range(4):
                cols = slice(h * 128, (h + 1) * 128)
                tr.op("pe", lambda e: e.matmul(ps_f[sbi][:, cols], lhsT=scb[i % 2].ap[:, h, :], rhs=V.ap[:, i, cols], start=True, stop=False),
                      reads=[scb[i % 2].k(), V.k(i)], writes=[psk_f[sbi]])
                tr.op("pe", lambda e: e.matmul(ps_f[sbi][0:64, cols], lhsT=QdT.ap[:, h, i * 128:i * 128 + 64], rhs=Sbf[:, h, s0, :],
                                               start=False, stop=True),
                      reads=[QdT.k(h, i // 4), Sbf_b.k(s0, h)], writes=[psk_f[sbi]])
                tr.op("pe", lambda e: e.matmul(ps_f[sbi][64:128, cols], lhsT=QdT.ap[:, h, i * 128 + 64:(i + 1) * 128], rhs=Sbf[:, h, s1, :],
                                               start=False, stop=True),
                      reads=[QdT.k(h, i // 4), Sbf_b.k(s1, h)], writes=[psk_f[sbi]])

        p2pool = PsumPool([0, 1, 2])
        pbank = {}
        tr.op("dve", lambda e: e.tensor_copy(out=U[:, 1, :, :], in_=Sin[:, :, :]), reads=[Sin_b.k()], writes=[U_b.k(1, h) for h in range(4)])
        tr.op("act", lambda e: e.activation(out=Sbf[:, :, 0, :], in_=Sin[:, :, :], func=AF.Copy), reads=[Sin_b.k()], writes=[Sbf_b.k(0, h) for h in range(4)])
        ok = lambda t: 0 <= t < NT
        st_proj(0)
        st_gate_act(0)
        st_gate_dve(0)
        for i in range(NT + 4):
            if ok(i + 1):
                st_proj(i + 1)
            if ok(i - 1):
                st_b1(i - 1)
            if ok(i - 2):
                st_b2(i - 2)
            if ok(i):
                st_scores(i)
            if ok(i + 1):
                st_gate_act(i + 1)
            if ok(i):
                st_state(i)
            if ok(i + 1):
                st_gate_dve(i + 1)
            if ok(i):
                st_o(i)
            if ok(i - 3):
                st_b3(i - 3)
            if ok(i - 4):
                st_b4(i - 4)
        for n in p2:
            ar.free(*p2[n])
        ar.free(*scb, *ybt, junk2, KdT, QdT, KdTM, V, U_b, Sbf_b)

        if debug:
            tr.dma("sp", "dbg", dbg["yb"], ybT.ap, reads=[ybT.k(i) for i in range(NT)])

        checkpoint("hgrn")
        allpool = PsumPool([0, 1, 2, 3, 4, 5])
        ar.free(wbf[0], wbf[3])
        wbf[SL_ZA] = ar.alloc("wbf2b", [128, 8, 512], BF16)
        load_win_group(SL_ZA, C_ZA)
        for _ in setup_gen:
            pass
        Bp = ar.alloc("Bp", [128, 2, 16, 256], BF16)
        tabs = s5h["tabs"]
        Toep = ar.alloc("Toep", [128, 32, 256], BF16)
        Cp = ar.alloc("Cp", [128, 2, 16, 256], BF16)
        tr.dma("sp", "s5ld", Bp.ap, bp_dr.ap().rearrange("p j a r n -> p j a (r n)"), reads=[("bp_dr",)], writes=[Bp.k()])
        tr.dma("sp", "s5ld", Toep.ap, toep_dr.ap(), reads=[("toep_dr",)], writes=[Toep.k()])
        tr.dma("sp", "s5ld", Cp.ap, cp_dr.ap(), reads=[("cp_dr",)], writes=[Cp.k()])
        for b_ in (Bp, Toep, Cp):
            tr.res[b_.k()] = {"w": ("s5ld", tr.cnt["s5ld"]), "r": dict(b_.fence)}
        ucm = ar.alloc("ucm", [128, 32, 16, 16], BF16)
        for s_ in range(16):
            bi = allpool.next()
            for kc in range(8):
                tr.op("pe", lambda e: e.matmul(ps_f[bi][:, :], lhsT=xnT.ap[:, kc, s_:TT:16], rhs=wbf[SL_U].ap[:, kc, :],
                                               start=(kc == 0), stop=(kc == 7)),
                      reads=[wbf[SL_U].k()] + [xnT.k(i) for i in range(NT)], writes=[psk_f[bi]])
            if True:
                tr.op("act", lambda e: e.activation(out=ucm.ap[:, :, s_, :], in_=ps_f[bi][:, :].rearrange("p (g q) -> p g q", g=32), func=AF.Copy),
                      reads=[psk_f[bi]], writes=[ucm.k(s_)])
            else:
                tr.op("dve", lambda e: e.tensor_copy(out=ucm.ap[:, :, s_, :], in_=ps_f[bi][:, :].rearrange("p (g q) -> p g q", g=32)),
                      reads=[psk_f[bi]], writes=[ucm.k(s_)])
        ar.free(wbf[1])
        Ub = ar.alloc("U", [128, 2, 32, 128], BF16)
        n_ev = 0
        for j in range(2):
            for gb in range(4):
                pb_i = trpool.next()
                pt = ps_b[pb_i]
                for g in range(8 * gb, 8 * gb + 8):
                    tr.op("pe", lambda e: e.transpose(out=pt[:, (g % 8) * 128:(g % 8 + 1) * 128], in_=ucm.ap[:, g, 8 * j:8 * j + 8, :].rearrange("p s q -> p (s q)"),
                                                      identity=CT("identb")[:]),
                          reads=[ucm.k(s_) for s_ in range(8 * j, 8 * j + 8)] + [CB("identb").k()], writes=[psk_b[pb_i]])
                if n_ev % 2 == 0:
                    tr.op("dve", lambda e: e.tensor_copy(out=Ub.ap[:, j, 8 * gb:8 * gb + 8, :], in_=pt[:, :].rearrange("p (a b) -> p a b", a=8)),
                          reads=[psk_b[pb_i]], writes=[Ub.k(j, gb)])
                else:
                    tr.op("act", lambda e: e.activation(out=Ub.ap[:, j, 8 * gb:8 * gb + 8, :], in_=pt[:, :].rearrange("p (a b) -> p a b", a=8), func=AF.Copy),
                          reads=[psk_b[pb_i]], writes=[Ub.k(j, gb)])
                n_ev += 1
        ar.free(ucm)
        Gre = ar.alloc("Gre", [128, 16, 128], F32)
        Gim = ar.alloc("Gim", [128, 16, 128], F32)
        rt = [ar.alloc(f"rt{j}", [128, 512], F32) for j in range(4)]
        for q in range(4):
            bre_i, bim_i = allpool.next(), allpool.next()
            for pr in range(4 * q, 4 * q + 4):
                for g2 in range(2):
                    g = 2 * pr + g2
                    for ri, bi in ((0, bre_i), (1, bim_i)):
                        for j in range(2):
                            tr.op("pe", lambda e: e.matmul(ps_f[bi][g2 * 64:(g2 + 1) * 64, (pr % 4) * 128:(pr % 4 + 1) * 128],
                                                           lhsT=Bp.ap[:, j, pr, ri * 128 + g2 * 64:ri * 128 + (g2 + 1) * 64], rhs=Ub.ap[:, j, g, :],
                                                           start=(j == 0), stop=(j == 1)),
                                  reads=[Bp.k(), Ub.k(j, g // 8)], writes=[psk_f[bi]])
            cosq = tabs.ap[:, 1, 4 * q:4 * q + 4, :].rearrange("p a b -> p (a b)")
            sinq = tabs.ap[:, 2, 4 * q:4 * q + 4, :].rearrange("p a b -> p (a b)")
            gre_q = Gre.ap[:, 4 * q:4 * q + 4, :].rearrange("p a b -> p (a b)")
            gim_q = Gim.ap[:, 4 * q:4 * q + 4, :].rearrange("p a b -> p (a b)")
            tr.op("dve", lambda e: e.tensor_tensor(out=rt[0].ap, in0=ps_f[bre_i][:, :], in1=cosq, op=ALU.mult), reads=[psk_f[bre_i], tabs.k()], writes=[rt[0].k()])
            tr.op("dve", lambda e: e.tensor_tensor(out=rt[1].ap, in0=ps_f[bim_i][:, :], in1=sinq, op=ALU.mult), reads=[psk_f[bim_i], tabs.k()], writes=[rt[1].k()])
            tr.op("dve", lambda e: e.tensor_tensor(out=rt[2].ap, in0=ps_f[bim_i][:, :], in1=cosq, op=ALU.mult), reads=[psk_f[bim_i], tabs.k()], writes=[rt[2].k()])
            tr.op("dve", lambda e: e.tensor_tensor(out=rt[3].ap, in0=ps_f[bre_i][:, :], in1=sinq, op=ALU.mult), reads=[psk_f[bre_i], tabs.k()], writes=[rt[3].k()])
            tr.op("pool", lambda e: e.tensor_tensor(out=gre_q, in0=rt[0].ap, in1=rt[1].ap, op=ALU.add), reads=[rt[0].k(), rt[1].k()], writes=[Gre.k(q)])
            tr.op("pool", lambda e: e.tensor_tensor(out=gim_q, in0=rt[2].ap, in1=rt[3].ap, op=ALU.subtract), reads=[rt[2].k(), rt[3].k()], writes=[Gim.k(q)])
        ar.free(Bp)
        for pr in range(16):
            for Gb in (Gre, Gim):
                tr.op("dve", lambda e: e.tensor_tensor_scan(out=Gb.ap[:, pr, :], data0=rho16[:, pr:pr + 1].to_broadcast([128, 128]), data1=Gb.ap[:, pr, :],
                                                            initial=0.0, op0=ALU.mult, op1=ALU.add),
                      reads=[Gb.k(pr // 4), rho16_b.k()], writes=[Gb.k(pr // 4)])
        pk2_b, pk2 = small("pk2", [128, 2, 16])
        e1_b, e1 = small("e1", [128, 16])
        e2_b, e2 = small("e2", [128, 16])
        gk = [Gre.k(q) for q in range(4)] + [Gim.k(q) for q in range(4)]
        cos127, sin127 = tabs.ap[:, 1, :, 127], tabs.ap[:, 2, :, 127]
        gre127, gim127 = Gre.ap[:, :, 127], Gim.ap[:, :, 127]
        tr.op("dve", lambda e: e.tensor_tensor(out=e1[:, :], in0=cos127, in1=gre127, op=ALU.mult), reads=gk + [tabs.k()], writes=[e1_b.k()])
        tr.op("dve", lambda e: e.tensor_tensor(out=e2[:, :], in0=sin127, in1=gim127, op=ALU.mult), reads=gk + [tabs.k()], writes=[e2_b.k()])
        tr.op("dve", lambda e: e.tensor_tensor(out=pk2[:, 0, :], in0=e1[:, :], in1=e2[:, :], op=ALU.subtract), reads=[e1_b.k(), e2_b.k()], writes=[pk2_b.k(0)])
        tr.op("dve", lambda e: e.tensor_tensor(out=e1[:, :], in0=cos127, in1=gim127, op=ALU.mult), reads=gk + [tabs.k()], writes=[e1_b.k()])
        tr.op("dve", lambda e: e.tensor_tensor(out=e2[:, :], in0=sin127, in1=gre127, op=ALU.mult), reads=gk + [tabs.k()], writes=[e2_b.k()])
        tr.op("dve", lambda e: e.tensor_tensor(out=pk2[:, 1, :], in0=e1[:, :], in1=e2[:, :], op=ALU.add), reads=[e1_b.k(), e2_b.k()], writes=[pk2_b.k(1)])
        tr.dma("pool", "ags", agi_s[:, :], pk2[:, :, :].rearrange("p a b -> p (a b)"), reads=[pk2_b.k(0), pk2_b.k(1)], writes=[("agi_s",)])
        tr.collective("cc", lambda e: e.collective_compute("AllGather", ALU.bypass, replica_groups=[[0, 1, 2, 3], [4, 5, 6, 7]],
                                                           ins=[agi_s.ap().opt()], outs=[ago_s.ap().opt()]),
                      reads=[("agi_s",)], writes=[("ago_s",)])
        g2_b, g2t = small("gath2", [128, 4, 2, 16])
        tr.dma("sp", "ags2", g2t[:, :, :, :].rearrange("p j a b -> p j (a b)"), ago_s[:, :].rearrange("(j p) f -> p j f", p=128),
               reads=[("ago_s",)], writes=[g2_b.k()])
        hin_b, hin = small("hin", [128, 2, 16])
        nw_b, nwt = small("hnew", [128, 2, 16])
        tr.op("dve", lambda e: e.memset(hin[:, :, :], 0.0), writes=[hin_b.k()])
        for j in range(3):
            R_ = [hin_b.k(), a2k_b.k(), g2_b.k(), e1_b.k(), e2_b.k(), nw_b.k()]
            tr.op("dve", lambda e: e.tensor_tensor(out=e1[:, :], in0=a2k[:, 0, :], in1=hin[:, 0, :], op=ALU.mult), reads=R_, writes=[e1_b.k()])
            tr.op("dve", lambda e: e.tensor_tensor(out=e2[:, :], in0=a2k[:, 1, :], in1=hin[:, 1, :], op=ALU.mult), reads=R_, writes=[e2_b.k()])
            tr.op("dve", lambda e: e.tensor_tensor(out=nwt[:, 0, :], in0=e1[:, :], in1=e2[:, :], op=ALU.subtract), reads=R_, writes=[nw_b.k()])
            tr.op("dve", lambda e: e.tensor_tensor(out=e1[:, :], in0=a2k[:, 0, :], in1=hin[:, 1, :], op=ALU.mult), reads=R_, writes=[e1_b.k()])
            tr.op("dve", lambda e: e.tensor_tensor(out=e2[:, :], in0=a2k[:, 1, :], in1=hin[:, 0, :], op=ALU.mult), reads=R_, writes=[e2_b.k()])
            tr.op("dve", lambda e: e.tensor_tensor(out=nwt[:, 1, :], in0=e1[:, :], in1=e2[:, :], op=ALU.add), reads=R_, writes=[nw_b.k()])
            tr.op("dve", lambda e: e.tensor_tensor(out=nwt[:, :, :], in0=nwt[:, :, :], in1=g2t[:, j, :, :], op=ALU.add), reads=R_, writes=[nw_b.k()])
            tr.op("dve", lambda e: e.tensor_tensor(out=nwt[:, :, :], in0=nwt[:, :, :], in1=hin[:, :, :], op=ALU.subtract), reads=R_, writes=[nw_b.k()])
            tr.op("dve", lambda e: e.scalar_tensor_tensor(out=hin[:, :, :], in0=nwt[:, :, :], scalar=CT("use")[:, j:j + 1], in1=hin[:, :, :],
                                                          op0=ALU.mult, op1=ALU.add), reads=R_ + [CB("use").k()], writes=[hin_b.k()])
        if debug:
            tr.dma("sp", "dbg", dbg["hin"], hin[:, :, :], reads=[hin_b.k()])
        checkpoint("s5local")
        HreB = ar.alloc("HreB", [128, 16, 130], BF16)
        HimB = ar.alloc("HimB", [128, 16, 130], BF16)
        rpow = tabs.ap[:, 0]
        cosT, sinT = tabs.ap[:, 1], tabs.ap[:, 2]
        big = [ar.alloc(f"big{j}", [128, 8, 128], F32) for j in range(2)]
        for hf in range(2):
            ps_ = slice(8 * hf, 8 * hf + 8)
            qk = [2 * hf, 2 * hf + 1]
            for ri, Gb in ((0, Gre), (1, Gim)):
                tr.op("dve", lambda e: e.tensor_tensor(out=big[0].ap, in0=rpow[:, ps_, :], in1=hin[:, ri, ps_].unsqueeze(2).to_broadcast([128, 8, 128]), op=ALU.mult),
                      reads=[tabs.k(), hin_b.k()], writes=[big[0].k()])
                tr.op("dve", lambda e: e.tensor_tensor(out=Gb.ap[:, ps_, :], in0=Gb.ap[:, ps_, :], in1=big[0].ap, op=ALU.add),
                      reads=[big[0].k()] + [Gb.k(q) for q in qk], writes=[Gb.k(q) for q in qk])
            gkh = [Gre.k(q) for q in qk] + [Gim.k(q) for q in qk]
            tr.op("dve", lambda e: e.tensor_tensor(out=big[0].ap, in0=cosT[:, ps_, :], in1=Gre.ap[:, ps_, :], op=ALU.mult), reads=gkh + [tabs.k()], writes=[big[0].k()])
            tr.op("dve", lambda e: e.tensor_tensor(out=big[1].ap, in0=sinT[:, ps_, :], in1=Gim.ap[:, ps_, :], op=ALU.mult), reads=gkh + [tabs.k()], writes=[big[1].k()])
            tr.op("dve", lambda e: e.tensor_tensor(out=HreB.ap[:, ps_, 1:129], in0=big[0].ap, in1=big[1].ap, op=ALU.subtract),
                  reads=[big[0].k(), big[1].k()], writes=[HreB.k()])
            tr.op("dve", lambda e: e.tensor_tensor(out=big[0].ap, in0=cosT[:, ps_, :], in1=Gim.ap[:, ps_, :], op=ALU.mult), reads=gkh + [tabs.k()], writes=[big[0].k()])
            tr.op("dve", lambda e: e.tensor_tensor(out=big[1].ap, in0=sinT[:, ps_, :], in1=Gre.ap[:, ps_, :], op=ALU.mult), reads=gkh + [tabs.k()], writes=[big[1].k()])
            tr.op("dve", lambda e: e.tensor_tensor(out=HimB.ap[:, ps_, 1:129], in0=big[0].ap, in1=big[1].ap, op=ALU.add),
                  reads=[big[0].k(), big[1].k()], writes=[HimB.k()])
        tr.op("act", lambda e: e.activation(out=HreB.ap[:, :, 0], in_=hin[:, 0, :], func=AF.Copy), reads=[hin_b.k(), HreB.k()], writes=[HreB.k()])
        tr.op("act", lambda e: e.activation(out=HimB.ap[:, :, 0], in_=hin[:, 1, :], func=AF.Copy), reads=[hin_b.k(), HimB.k()], writes=[HimB.k()])
        ar.free(*big, *rt, Gre, Gim, tabs)
        ycm = ar.alloc("ycm", [128, 16, 32, 16], BF16)
        for pr in range(16):
            bi = allpool.next()
            for g2 in range(2):
                g = 2 * pr + g2
                rows = slice(g2 * 64, g2 * 64 + 64)
                o_ = ps_f[bi][:, g2 * 256:(g2 + 1) * 256]
                tr.op("pe", lambda e: e.matmul(o_, lhsT=Ub.ap[:, 0, g, :], rhs=Toep.ap[:, g, :], start=True, stop=False),
                      reads=[Ub.k(0, g // 8), Toep.k()], writes=[psk_f[bi]])
                tr.op("pe", lambda e: e.matmul(ps_f[bi][:, g2 * 256 + 128:(g2 + 1) * 256], lhsT=Ub.ap[:, 1, g, :], rhs=Toep.ap[:, g, 0:128],
                                               start=False, stop=False),
                      reads=[Ub.k(1, g // 8), Toep.k()], writes=[psk_f[bi]])
                tr.op("pe", lambda e: e.matmul(o_, lhsT=HreB.ap[rows, pr, 0:128], rhs=Cp.ap[rows, 0, pr, :], start=False, stop=False),
                      reads=[HreB.k(), Cp.k()], writes=[psk_f[bi]])
                tr.op("pe", lambda e: e.matmul(o_, lhsT=HimB.ap[rows, pr, 0:128], rhs=Cp.ap[rows, 1, pr, :], start=False, stop=True),
                      reads=[HimB.k(), Cp.k()], writes=[psk_f[bi]])
            tr.op("act", lambda e: e.activation(out=ycm.ap[:, :, 2 * pr:2 * pr + 2, :], in_=ps_f[bi][:, :].rearrange("p (g s q) -> p s g q", g=2, s=16),
                                                func=AF.Gelu_apprx_tanh),
                  reads=[psk_f[bi]], writes=[ycm.k(pr)])
        ar.free(Toep, Cp, Ub, HreB, HimB)
        yFM = ar.alloc("yFM", [128, 4, TT], BF16)
        n_ev = 0
        for q in range(4):
            for sb_ in range(2):
                pb_i = trpool.next()
                pt = ps_b[pb_i]
                for s8 in range(8):
                    s_ = 8 * sb_ + s8
                    tr.op("pe", lambda e: e.transpose(out=pt[:, s8 * 128:(s8 + 1) * 128], in_=ycm.ap[:, s_, 8 * q:8 * q + 8, :].rearrange("p g q -> p (g q)"),
                                                      identity=CT("identb")[:]),
                          reads=[ycm.k(pr) for pr in range(4 * q, 4 * q + 4)] + [CB("identb").k()], writes=[psk_b[pb_i]])
                dst = yFM.ap[:, q, :].rearrange("p (c s) -> p s c", s=16)[:, 8 * sb_:8 * sb_ + 8, :]
                src = pt[:, :].rearrange("p (a b) -> p a b", a=8)
                if n_ev % 2 == 0:
                    tr.op("dve", lambda e: e.tensor_copy(out=dst, in_=src), reads=[psk_b[pb_i]], writes=[yFM.k(q, sb_)])
                else:
                    tr.op("act", lambda e: e.activation(out=dst, in_=src, func=AF.Copy), reads=[psk_b[pb_i]], writes=[yFM.k(q, sb_)])
                n_ev += 1
        ar.free(ycm)
        if debug:
            tr.dma("sp", "dbg", dbg["yfm"], yFM.ap, reads=[yFM.k(q, s) for q in range(4) for s in range(2)])
        yaFM = ar.alloc("yaFM", [128, 4, TT], BF16)
        wglu = ar.alloc("wglu", [128, 4, 512], BF16)
        load_weight_cols(wglu, lambda c0: wglu.ap[:, :, c0:c0 + 256], wglu_d, 4, 0, 512)
        gt = {n: [ar.alloc(f"g_{n}{j}", [128, 512], F32) for j in range(2)] for n in ["sg", "sz"]}
        wga = ar.alloc("wga", [128, 8, 1024], BF16)
        wgb = ar.alloc("wgb", [128, 8, 1024], BF16)
        wpa = ar.alloc("wpa", [128, 4, 1024], BF16)
        wpb = ar.alloc("wpb", [128, 4, 1024], BF16)
        wout = ar.alloc("wout", [128, 8, 1024], BF16)
        load_weight_cols(wga, lambda c0: wga.ap[:, :, c0:c0 + 256], w_in_d, 8, C_GA, 1024)
        load_weight_cols(wgb, lambda c0: wgb.ap[:, :, c0:c0 + 256], w_in_d, 8, C_GB, 1024)
        load_weight_cols(wpa, lambda c0: wpa.ap[:, :, c0:c0 + 256], wpa_d, 4, 0, 1024)
        wpb32 = ar.alloc("wpb32", [128, 4, 1024], F32)
        tr.dma("sp", "wpb32", wpb32.ap, wpb_d.rearrange("(kc p) c -> p kc c", p=128), writes=[wpb32.k()])
        tr.op("pool", lambda e: e.tensor_tensor(out=wpb.ap, in0=wpb32.ap, in1=CT("hnw")[:, 0:4].unsqueeze(2).to_broadcast([128, 4, 1024]), op=ALU.mult),
              reads=[wpb32.k(), CB("hnw").k()], writes=[wpb.k()])
        ar.free(wpb32)
        load_weight_cols(wout, lambda c0: wout.ap[:, :, c0:c0 + 256], wout_d, 8, 0, 1024)
        nbg_b, nbg = small("nbglu", [128, 4])
        tr.op("dve", lambda e: e.tensor_scalar(out=nbg[:, :], in0=CT("bglu")[:, :], scalar1=-1.0, scalar2=None, op0=ALU.mult),
              reads=[CB("bglu").k()], writes=[nbg_b.k()])
        yk = [yFM.k(q, s) for q in range(4) for s in range(2)]
        it = 0
        for ct in range(4):
            for tb in range(NB):
                j = it % 2
                it += 1
                bg = allpool.next()
                for kc in range(4):
                    tr.op("pe", lambda e: e.matmul(ps_f[bg][:, :], lhsT=wglu.ap[:, kc, ct * 128:(ct + 1) * 128], rhs=yFM.ap[:, kc, tb * 512:(tb + 1) * 512],
                                                   start=(kc == 0), stop=(kc == 3)),
                          reads=[wglu.k()] + yk, writes=[psk_f[bg]])
                bz = allpool.next()
                for kc in range(8):
                    tr.op("pe", lambda e: e.matmul(ps_f[bz][:, :], lhsT=wbf[SL_ZA].ap[:, kc, ct * 128:(ct + 1) * 128],
                                                   rhs=xnT.ap[:, kc, tb * 512:(tb + 1) * 512], start=(kc == 0), stop=(kc == 7)),
                          reads=[wbf[SL_ZA].k()] + [xnT.k(4 * tb + jj) for jj in range(4)], writes=[psk_f[bz]])
                G = {n: gt[n][j] for n in gt}
                tr.op("act", lambda e: e.activation(out=G["sg"].ap, in_=ps_f[bg][:, :], func=AF.Exp, scale=-1.0, bias=nbg[:, ct:ct + 1]),
                      reads=[psk_f[bg], nbg_b.k()], writes=[G["sg"].k()])
                tr.op("act", lambda e: e.activation(out=G["sg"].ap, in_=G["sg"].ap, func=AF.Ln, bias=1.0), reads=[G["sg"].k()], writes=[G["sg"].k()])
                tr.op("act", lambda e: e.activation(out=G["sz"].ap, in_=ps_f[bz][:, :], func=AF.Exp, scale=-1.0), reads=[psk_f[bz]], writes=[G["sz"].k()])
                tr.op("act", lambda e: e.activation(out=G["sz"].ap, in_=G["sz"].ap, func=AF.Ln, bias=1.0), reads=[G["sz"].k()], writes=[G["sz"].k()])
                tr.op("dve", lambda e: e.tensor_add(out=G["sg"].ap, in0=G["sg"].ap, in1=G["sz"].ap), reads=[G["sg"].k(), G["sz"].k()], writes=[G["sg"].k()])
                tr.op("act", lambda e: e.activation(out=G["sg"].ap, in_=G["sg"].ap, func=AF.Exp, scale=-1.0), reads=[G["sg"].k()], writes=[G["sg"].k()])
                tr.op("dve", lambda e: e.tensor_tensor(out=G["sg"].ap, in0=ps_f[bz][:, :], in1=G["sg"].ap, op=ALU.mult),
                      reads=[psk_f[bz], G["sg"].k()], writes=[G["sg"].k()])
                tr.op("dve", lambda e: e.tensor_tensor(out=yaFM.ap[:, ct, tb * 512:(tb + 1) * 512], in0=yFM.ap[:, ct, tb * 512:(tb + 1) * 512], in1=G["sg"].ap, op=ALU.mult),
                      reads=yk + [G["sg"].k()], writes=[yaFM.k(ct, tb)])
        for n in gt:
            ar.free(*gt[n])
        ar.free(yFM, wglu)
        if debug:
            tr.dma("sp", "dbg", dbg["ya"], yaFM.ap, reads=[yaFM.k(c, t) for c in range(4) for t in range(NB)])
        checkpoint("s5")

        ar.free(wbf[SL_ZA])
        fnw = ar.alloc("fnw", [128, D], F32)
        tr.dma("sp", "fnw", fnw.ap, fnw_d, writes=[fnw.k()])
        mg = [ar.alloc(f"mg{j}", [128, 8, 512], BF16) for j in range(2)]
        mt = {n: [ar.alloc(f"m_{n}{j}", [128, 512], F32) for j in range(2)] for n in ["sa", "sb"]}
        xr = [ar.alloc(f"xr{j}", [128, D], F32) for j in range(2)]
        hb = [ar.alloc(f"hb{j}", [128, D], F32) for j in range(2)]
        junk3 = ar.alloc("junk3", [128, 512], BF16)
        ss2_b, ss2 = small("ss2", [128, NT, 2])
        r2_b, r2 = small("r2", [128, NT])
        yak = lambda tb: [yaFM.k(c, tb) for c in range(4)]
        ybk = lambda tb: [ybT.k(4 * tb + jj) for jj in range(4)]
        xk = lambda tb: [xnT.k(4 * tb + jj) for jj in range(4)]
        it = 0
        for tb in range(NB):
            M = mg[tb % 2]
            for dt_ in range(8):
                j = it % 2
                it += 1
                T_ = {n: mt[n][j] for n in mt}
                cs = slice(dt_ * 128, (dt_ + 1) * 128)
                ts_ = slice(tb * 512, (tb + 1) * 512)
                bpa, bpb_, bga, bgb = allpool.next(), allpool.next(), allpool.next(), allpool.next()
                for kc in range(8):
                    tr.op("pe", lambda e: e.matmul(ps_f[bga][:, :], lhsT=wga.ap[:, kc, cs], rhs=xnT.ap[:, kc, ts_], start=(kc == 0), stop=(kc == 7)),
                          reads=[wga.k()] + xk(tb), writes=[psk_f[bga]])
                for kc in range(8):
                    tr.op("pe", lambda e: e.matmul(ps_f[bgb][:, :], lhsT=wgb.ap[:, kc, cs], rhs=xnT.ap[:, kc, ts_], start=(kc == 0), stop=(kc == 7)),
                          reads=[wgb.k()] + xk(tb), writes=[psk_f[bgb]])
                for kc in range(4):
                    tr.op("pe", lambda e: e.matmul(ps_f[bpa][:, :], lhsT=wpa.ap[:, kc, cs], rhs=yaFM.ap[:, kc, ts_], start=(kc == 0), stop=(kc == 3)),
                          reads=[wpa.k()] + yak(tb), writes=[psk_f[bpa]])
                for kc in range(4):
                    tr.op("pe", lambda e: e.matmul(ps_f[bpb_][:, :], lhsT=wpb.ap[:, kc, cs], rhs=ybT.ap[:, kc, ts_], start=(kc == 0), stop=(kc == 3)),
                          reads=[wpb.k()] + ybk(tb), writes=[psk_f[bpb_]])
                sigmoid3(T_["sa"].ap, [T_["sa"].k()], ps_f[bga][:, :], [psk_f[bga]], None)
                sigmoid3(T_["sb"].ap, [T_["sb"].k()], ps_f[bgb][:, :], [psk_f[bgb]], None)
                tr.op("dve", lambda e: e.tensor_tensor(out=T_["sa"].ap, in0=ps_f[bpa][:, :], in1=T_["sa"].ap, op=ALU.mult),
                      reads=[psk_f[bpa], T_["sa"].k()], writes=[T_["sa"].k()])
                tr.op("dve", lambda e: e.tensor_tensor(out=T_["sb"].ap, in0=ps_f[bpb_][:, :], in1=T_["sb"].ap, op=ALU.mult),
                      reads=[psk_f[bpb_], T_["sb"].k()], writes=[T_["sb"].k()])
                tr.op("pool", lambda e: e.tensor_tensor(out=M.ap[:, dt_, :], in0=T_["sa"].ap, in1=T_["sb"].ap, op=ALU.add),
                      reads=[T_["sa"].k(), T_["sb"].k()], writes=[M.k(dt_)])
            for il in range(4):
                i = 4 * tb + il
                X_, H_ = xr[i % 2], hb[i % 2]
                tr.dma("pool", f"xr{i % 2}", X_.ap, x_d[i * 128:(i + 1) * 128, :], writes=[X_.k()])
                for half in range(2):
                    bo = allpool.next()
                    hs = slice(half * 512, (half + 1) * 512)
                    for kc in range(8):
                        tr.op("pe", lambda e: e.matmul(ps_f[bo][:, :], lhsT=M.ap[:, kc, il * 128:(il + 1) * 128], rhs=wout.ap[:, kc, hs],
                                                       start=(kc == 0), stop=(kc == 7)),
                              reads=[M.k(kc), wout.k()], writes=[psk_f[bo]])
                    tr.op("dve", lambda e: e.tensor_tensor(out=H_.ap[:, hs], in0=ps_f[bo][:, :], in1=X_.ap[:, hs], op=ALU.add),
                          reads=[psk_f[bo], X_.k()], writes=[H_.k(half)])
                    tr.op("act", lambda e: e.activation(out=junk3.ap, in_=H_.ap[:, hs], func=AF.Square, accum_out=ss2[:, i, half:half + 1]),
                          reads=[H_.k(half)], writes=[junk3.k(), ss2_b.k(i, half)])
                rk = [ss2_b.k(i, 0), ss2_b.k(i, 1)]
                tr.op("dve", lambda e: e.tensor_tensor(out=r2[:, i:i + 1], in0=ss2[:, i, 0:1], in1=ss2[:, i, 1:2], op=ALU.add), reads=rk, writes=[r2_b.k(i)])
                rstd_act(r2[:, i:i + 1], [r2_b.k(i)], 1.0 / D)
                tr.op("dve", lambda e: e.scalar_tensor_tensor(out=H_.ap, in0=H_.ap, scalar=r2[:, i:i + 1], in1=fnw.ap, op0=ALU.mult, op1=ALU.mult),
                      reads=[H_.k(0), H_.k(1), r2_b.k(i), fnw.k()], writes=[H_.k(0), H_.k(1)])
                tr.dma("sp", f"ob{i % 2}", out_d[i * 128:(i + 1) * 128, :], H_.ap, reads=[H_.k(0), H_.k(1)])


    try:
        rest()
    except _Stop:
        pass
    tr.final_wait("sp")
    print("instr counts", tr.ninstr)
    return nc


def make_inputs(inputs):
    f32 = np.float32
    x = np.asarray(inputs["x"], f32)
    per_core = []
    common = {
        "w_in": np.ascontiguousarray(inputs["w_in"][0], f32),
        "nw": np.ascontiguousarray(np.asarray(inputs["norm_w"][0], f32).reshape(8, 128).T),
        "lbl": np.ascontiguousarray(np.asarray(inputs["hgrn_lb_logits"], f32).reshape(2, 4, 128).transpose(2, 0, 1)),
        "hnw": np.ascontiguousarray(np.asarray(inputs["hgrn_norm_w"][0], f32).reshape(4, 128).T),
        "identb": np.eye(128, dtype=f32).astype(ml_dtypes.bfloat16),
        "w_proj_b": np.ascontiguousarray(inputs["w_proj_b"][0], f32),
        "w_proj_a": np.ascontiguousarray(inputs["w_proj_a"][0], f32),
        "w_out": np.ascontiguousarray(inputs["w_out"][0], f32),
        "w_glu": np.ascontiguousarray(inputs["ssm_w_glu"][0], f32),
        "bglu": np.ascontiguousarray(np.asarray(inputs["ssm_b_glu"][0], f32).reshape(4, 128).T),
        "fnw": np.ascontiguousarray(np.broadcast_to(np.asarray(inputs["final_norm_w"], f32).reshape(1, D), (128, D))),
    }
    sn = lambda a: np.ascontiguousarray(np.asarray(a, f32).reshape(16, 2, 64).transpose(1, 2, 0).reshape(128, 16))
    common["lamre"] = sn(inputs["ssm_lambda_re"][0])
    common["lamim"] = sn(inputs["ssm_lambda_im"][0])
    ld = np.asarray(inputs["ssm_log_dt"][0], f32).reshape(16, 2).T
    common["logdt"] = np.ascontiguousarray(np.repeat(ld[:, None, :], 64, axis=1).reshape(128, 16))
    bsn = lambda a: np.ascontiguousarray(np.asarray(a, f32).reshape(16, 2, 64, 16).transpose(1, 2, 0, 3).reshape(128, 16, 16))
    common["bre"] = bsn(inputs["ssm_b_re"][0])
    common["bim"] = bsn(inputs["ssm_b_im"][0])
    csn = lambda a: np.ascontiguousarray(np.asarray(a, f32).reshape(16, 2, 16, 64).transpose(1, 3, 0, 2).reshape(128, 16, 16))
    common["cre"] = csn(inputs["ssm_c_re"][0])
    common["cim"] = csn(inputs["ssm_c_im"][0])
    common["dbc"] = np.ascontiguousarray(np.broadcast_to(np.asarray(inputs["ssm_d"][0], f32)[None], (128, 32, 16)))
    kvv = np.concatenate([-np.arange(1, 9), np.arange(0, 17), np.arange(15, -1, -1)]).astype(f32)
    common["kv"] = np.ascontiguousarray(np.broadcast_to(kvv[None], (128, 41)))
    common["cidx"] = np.ascontiguousarray(np.broadcast_to(np.arange(1, 129, dtype=f32)[None], (128, 128)))
    common["identf"] = np.eye(128, dtype=f32)
    rr_ = np.arange(128)[:, None] // 16
    cc_ = np.arange(256)[None, :] // 16
    common["maskT"] = (cc_ >= rr_).astype(f32)
    s = np.arange(128)[:, None]
    t = np.arange(128)[None, :]
    common["maskbc"] = ((s // 64 == t // 64) & (t >= s)).astype(f32)
    m = np.ones((128, 512), f32)
    m[:, 0::64] = 0.0
    common["mask512"] = m
    for r in range(8):
        b, k = r // 4, r % 4
        d = dict(common)
        d["x"] = np.ascontiguousarray(x[b, k * TT:(k + 1) * TT, :])
        use = np.zeros((128, 4), f32)
        use[:, :k] = 1.0
        d["use"] = use
        per_core.append(d)
    return per_core


def kernel(**inputs):
    nc = build_nc()
    in_maps = make_inputs(inputs)
    res = run_bass_kernel_spmd(nc, in_maps, core_ids=list(range(8)))
    out = np.zeros((2, 8192, D), np.float32)
    for r in range(8):
        b, k = r // 4, r % 4
        out[b, k * TT:(k + 1) * TT, :] = res.results[r]["out"]
    return out
```

```python
import math
import numpy as np
import ml_dtypes
import concourse.bass as bass
import concourse.mybir as mybir
from concourse.bass_utils import run_bass_kernel_spmd

F32 = mybir.dt.float32
BF16 = mybir.dt.bfloat16
I32 = mybir.dt.int32
AF = mybir.ActivationFunctionType
ALU = mybir.AluOpType
AX = mybir.AxisListType

TT = 2048
D = 1024
NT = TT // 128
NB = TT // 512
EPS = 1e-6
TWO_PI = 2.0 * math.pi

C_U, C_ZA, C_Q, C_F, C_I, C_OG, C_ZB, C_GA, C_GB = 0, 512, 1024, 1536, 2048, 2560, 3072, 3584, 4608


class Buf:
    def __init__(self, name, fence):
        self.name = name
        self.fence = fence

    def k(self, *idx):
        return (self, idx)


class Tracker:
    def __init__(self, nc):
        self.nc = nc
        self.eng = {"pe": nc.tensor, "act": nc.scalar, "dve": nc.vector, "pool": nc.gpsimd, "sp": nc.sync}
        self.semh = {}
        self.cnt = {}
        for e in self.eng:
            self.semh[e] = nc.semaphore("s_" + e).__enter__()
            self.cnt[e] = 0
        self.waited = {e: {} for e in self.eng}
        self.res = {}
        self.same_engine_sync = True
        self.ninstr = {e: 0 for e in self.eng}
        self.pending = []
        self.defer = None

    def dma_sem(self, name):
        if name not in self.semh:
            self.semh[name] = self.nc.semaphore("d_" + name).__enter__()
            self.cnt[name] = 0
        return name

    def _state(self, key):
        st = self.res.get(key)
        if st is None:
            fence = key[0].fence if isinstance(key[0], Buf) else {}
            st = {"w": None, "r": dict(fence)}
            self.res[key] = st
        return st

    def _collect(self, reads, writes):
        evs = {}

        def add(sk, v):
            if evs.get(sk, 0) < v:
                evs[sk] = v

        for k in reads:
            st = self._state(k)
            if st["w"]:
                add(*st["w"])
        for k in writes:
            st = self._state(k)
            if st["w"]:
                add(*st["w"])
            for sk, v in st["r"].items():
                add(sk, v)
        return evs

    def _wait(self, eng, evs):
        for sk, v in evs.items():
            if sk == eng and (eng == "pe" or eng == "sp" or not self.same_engine_sync):
                continue
            if self.waited[eng].get(sk, 0) < v:
                self.eng[eng].wait_ge(self.semh[sk], v)
                self.waited[eng][sk] = v

    def _record(self, ev, reads, writes):
        for k in reads:
            st = self._state(k)
            if st["r"].get(ev[0], 0) < ev[1]:
                st["r"][ev[0]] = ev[1]
        for k in writes:
            self.res[k] = {"w": ev, "r": {}}

    def stop_defer(self):
        self.pending = self.defer
        self.defer = None

    def replay(self, n=None):
        q = self.pending
        assert self.defer is None
        k = len(q) if n is None else min(n, len(q))
        for ent in q[:k]:
            if ent[0] == "op":
                self.op(*ent[1:])
            else:
                ent[1].free(*ent[2])
        self.pending = q[k:]
        return len(self.pending)

    def op(self, eng, fn, reads=(), writes=()):
        if self.defer is not None:
            self.defer.append(("op", eng, fn, list(reads), list(writes)))
            return
        self._wait(eng, self._collect(reads, writes))
        ins = fn(self.eng[eng])
        self.cnt[eng] += 1
        self.ninstr[eng] += 1
        ins.then_inc(self.semh[eng], 1)
        self._record((eng, self.cnt[eng]), reads, writes)

    def dma(self, queue, semname, out, in_, reads=(), writes=(), **kw):
        self.dma_sem(semname)
        self._wait(queue, self._collect(reads, writes))
        ins = self.eng[queue].dma_start(out=out, in_=in_, **kw)
        self.cnt[semname] += 16
        ins.then_inc(self.semh[semname], 16)
        self._record((semname, self.cnt[semname]), reads, writes)

    def collective(self, semname, fn, reads=(), writes=()):
        self.dma_sem(semname)
        self._wait("pool", self._collect(reads, writes))
        ins = fn(self.eng["pool"])
        self.cnt[semname] += 1
        ins.then_inc(self.semh[semname], 1)
        self._record((semname, self.cnt[semname]), reads, writes)

    def retire(self, bufs):
        fence = {}
        for key, st in self.res.items():
            if isinstance(key[0], Buf) and key[0] in bufs:
                if st["w"] and fence.get(st["w"][0], 0) < st["w"][1]:
                    fence[st["w"][0]] = st["w"][1]
                for sk, v in st["r"].items():
                    if fence.get(sk, 0) < v:
                        fence[sk] = v
        for b in bufs:
            for sk, v in b.fence.items():
                if fence.get(sk, 0) < v:
                    fence[sk] = v
        return fence

    def final_wait(self, eng):
        for sk, v in self.cnt.items():
            if v > 0 and sk != eng and self.waited[eng].get(sk, 0) < v:
                self.eng[eng].wait_ge(self.semh[sk], v)
                self.waited[eng][sk] = v


class Arena:
    def __init__(self, nc, tr, nbytes):
        self.tr = tr
        self.n = nbytes
        self.t = nc.sbuf_tensor("arena", [128, nbytes // 2], BF16).__enter__()
        self.live = []
        self.retired = []

    def alloc(self, name, shape, dtype):
        esz = 4 if dtype in (F32, I32) else 2
        nel = int(np.prod(shape[1:]))
        nb = (nel * esz + 63) // 64 * 64
        pos = 0
        for s, e, _ in sorted(self.live, key=lambda z: z[0]):
            if pos + nb <= s:
                break
            pos = max(pos, e)
        assert pos + nb <= self.n, f"arena full allocating {name} {shape}: live={[(b.name, s, e) for s, e, b in self.live]}"
        fence = {}
        keep = []
        for s, e, f in self.retired:
            if s < pos + nb and pos < e:
                for sk, v in f.items():
                    if fence.get(sk, 0) < v:
                        fence[sk] = v
                if not (pos <= s and e <= pos + nb):
                    keep.append((s, e, f))
            else:
                keep.append((s, e, f))
        self.retired = keep
        b = Buf(name, fence)
        self.live.append((pos, pos + nb, b))
        ap = self.t[:, pos // 2: pos // 2 + nel * esz // 2]
        if esz == 4:
            ap = ap.bitcast(dtype)
        elif dtype != BF16:
            ap = ap.bitcast(dtype)
        if len(shape) > 2:
            names = [f"d{i}" for i in range(len(shape) - 1)]
            ap = ap.rearrange(f"p ({' '.join(names)}) -> p {' '.join(names)}", **{n: v for n, v in zip(names[:-1], shape[1:-1])})
        b.ap = ap
        b.shape = shape
        return b

    def free(self, *bufs):
        if self.tr.defer is not None:
            self.tr.defer.append(("free", self, bufs))
            return
        fence = self.tr.retire(set(bufs))
        for b in bufs:
            ent = [z for z in self.live if z[2] is b]
            assert ent, b.name
            self.live.remove(ent[0])
            self.retired.append((ent[0][0], ent[0][1], fence))


class PsumPool:
    def __init__(self, banks):
        self.banks = banks
        self.i = 0

    def next(self):
        b = self.banks[self.i % len(self.banks)]
        self.i += 1
        return b


def build_nc(debug=None):
    nc = bass.Bass("TRN2", target_bir_lowering=False)
    tr = Tracker(nc)
    dbg = {}

    def din(name, shape, dt=F32):
        return nc.dram_tensor(name, list(shape), dt, kind="ExternalInput").ap()

    x_d = din("x", [TT, D])
    w_in_d = din("w_in", [D, 5632])
    nw_d = din("nw", [128, 8])
    lbl_d = din("lbl", [128, 2, 4])
    hnw_d = din("hnw", [128, 4])
    use_d = din("use", [128, 4])
    identb_d = din("identb", [128, 128], BF16)
    maskbc_d = din("maskbc", [128, 128])
    mask512_d = din("mask512", [128, 512])
    wpb_d = din("w_proj_b", [512, D])
    wpa_d = din("w_proj_a", [512, D])
    wout_d = din("w_out", [D, D])
    wglu_d = din("w_glu", [512, 512])
    bglu_d = din("bglu", [128, 4])
    fnw_d = din("fnw", [128, D])
    lamre_d = din("lamre", [128, 16])
    lamim_d = din("lamim", [128, 16])
    logdt_d = din("logdt", [128, 16])
    bre_d = din("bre", [128, 16, 16])
    bim_d = din("bim", [128, 16, 16])
    cre_d = din("cre", [128, 16, 16])
    cim_d = din("cim", [128, 16, 16])
    dbc_d = din("dbc", [128, 32, 16])
    kv_d = din("kv", [128, 41])
    cidx_d = din("cidx", [128, 128])
    identf_d = din("identf", [128, 128])
    maskT_d = din("maskT", [128, 256])
    out_d = nc.dram_tensor("out", [TT, D], F32, kind="ExternalOutput").ap()
    toep_dr = nc.dram_tensor("toep_dr", [128, 32, 256], BF16)
    cp_dr = nc.dram_tensor("cp_dr", [128, 2, 16, 256], BF16)
    bp_dr = nc.dram_tensor("bp_dr", [128, 2, 16, 2, 128], BF16)
    tab_dr = nc.dram_tensor("tab_dr", [128, 3, 16, 128], F32)
    agi_s = nc.dram_tensor("agi_s", [128, 32], F32)
    ago_s = nc.dram_tensor("ago_s", [4 * 128, 32], F32)
    if debug:
        dbg["yb"] = nc.dram_tensor("dbg_yb", [128, 4, TT], BF16, kind="ExternalOutput").ap()
        dbg["sin"] = nc.dram_tensor("dbg_sin", [128, 4, 128], F32, kind="ExternalOutput").ap()
        dbg["ya"] = nc.dram_tensor("dbg_ya", [128, 4, TT], BF16, kind="ExternalOutput").ap()
        dbg["toep"] = nc.dram_tensor("dbg_toep", [128, 32, 256], BF16, kind="ExternalOutput").ap()
        dbg["hin"] = nc.dram_tensor("dbg_hin", [128, 2, 16], F32, kind="ExternalOutput").ap()
        dbg["yfm"] = nc.dram_tensor("dbg_yfm", [128, 4, TT], BF16, kind="ExternalOutput").ap()
    agi_h = nc.dram_tensor("agi_h", [128, 516], F32)
    ago_h = nc.dram_tensor("ago_h", [4 * 128, 516], F32)

    ar = Arena(nc, tr, 192 * 1024)
    stop_at = debug.get("stop") if isinstance(debug, dict) else None
    if isinstance(debug, dict) and debug.get("nosync"):
        tr.same_engine_sync = False

    class _Stop(Exception):
        pass

    def checkpoint(name):
        if stop_at == name:
            raise _Stop()

    def sb(name, shape, dt=F32):
        t = nc.sbuf_tensor(name, list(shape), dt).__enter__()
        b = Buf(name, {})
        b.ap = t[:] if False else t
        return b, t

    cst = {}
    for name, shape, dt, src in [
        ("nw", [128, 8], F32, nw_d), ("lbl", [128, 2, 4], F32, lbl_d), ("hnw", [128, 4], F32, hnw_d),
        ("use", [128, 4], F32, use_d), ("identb", [128, 128], BF16, identb_d),
        ("maskbc", [128, 128], F32, maskbc_d), ("mask512", [128, 512], F32, mask512_d),
        ("bglu", [128, 4], F32, bglu_d),
    ]:
        b, t = sb("c_" + name, shape, dt)
        cst[name] = (b, t)
        tr.dma("sp", "const", t[:], src, writes=[b.k()])
    for name in cst:
        tr.res[cst[name][0].k()] = {"w": ("const", tr.cnt["const"]), "r": {}}
    CB = lambda n: cst[n][0]
    CT = lambda n: cst[n][1]

    def small(name, shape, dt=F32):
        b, t = sb(name, shape, dt)
        return b, t

    ps_f = [nc.psum_tensor(f"psf{i}", [128, 512], F32).__enter__() for i in range(6)]
    ps_b = [nc.psum_tensor(f"psb{i}", [128, 1024], BF16).__enter__() for i in range(2)]
    psk_f = [("psf", i) for i in range(6)]
    psk_b = [("psb", i) for i in range(2)]
    mmpool = PsumPool([0, 1, 2, 3])
    smpool = PsumPool([4, 5])
    trpool = PsumPool([0, 1])

    xnT = ar.alloc("xnT", [128, 8, TT], BF16)
    wbf = [ar.alloc(f"wbf{i}", [128, 8, 512], BF16) for i in range(3)]
    def load_weight_cols(dst_buf, dst_ap_fn, src_d, kcs, col0, ncols, scale=None, dst_keys=None, eng="pool", only=None):
        assert scale is None
        for c0 in range(0, ncols, 256):
            if only is not None and c0 != only:
                continue
            src = src_d[:, col0 + c0: col0 + c0 + 256].rearrange("(kc p) c -> p kc c", p=128)
            keys = dst_keys if dst_keys is not None else [dst_buf.k()]
            tr.dma("pool", "w_" + dst_buf.name, dst_ap_fn(c0), src, writes=keys)

    def load_win_group(slot, col0, eng="pool", **kw):
        wb = wbf[slot]
        load_weight_cols(wb, lambda c0: wb.ap[:, :, c0:c0 + 256], w_in_d, 8, col0, 512, eng=eng, **kw)

    SL_F, SL_Q, SL_I, SL_OG, SL_ZB, SL_U, SL_ZA = 0, 1, 2, 3, 0, 1, 2

    def sigmoid3(dst_ap, dst_keys, src_ap, src_keys, tmp, nbias=None, nbias_keys=()):
        dk = list(dst_keys)
        if nbias is None:
            tr.op("act", lambda e: e.activation(out=dst_ap, in_=src_ap, func=AF.Exp, scale=-1.0), reads=list(src_keys), writes=dk)
        else:
            tr.op("act", lambda e: e.activation(out=dst_ap, in_=src_ap, func=AF.Exp, scale=-1.0, bias=nbias),
                  reads=list(src_keys) + list(nbias_keys), writes=dk)
        tr.op("act", lambda e: e.activation(out=dst_ap, in_=dst_ap, func=AF.Ln, bias=1.0), reads=dk, writes=dk)
        tr.op("act", lambda e: e.activation(out=dst_ap, in_=dst_ap, func=AF.Exp, scale=-1.0), reads=dk, writes=dk)

    def sigmoid_dve(dst_ap, dst_keys, src_ap, src_keys):
        dk = list(dst_keys)
        tr.op("act", lambda e: e.activation(out=dst_ap, in_=src_ap, func=AF.Exp, scale=-1.0), reads=list(src_keys), writes=dk)
        tr.op("dve", lambda e: e.tensor_scalar(out=dst_ap, in0=dst_ap, scalar1=1.0, scalar2=None, op0=ALU.add), reads=dk, writes=dk)
        tr.op("dve", lambda e: e.reciprocal(out=dst_ap, in_=dst_ap), reads=dk, writes=dk)

    def rstd_act(ap, keys, inv_n):
        tr.op("act", lambda e: e.activation(out=ap, in_=ap, func=AF.Ln, scale=inv_n, bias=epsc[:, 0:1]), reads=list(keys) + [epsc_b.k()], writes=list(keys))
        tr.op("act", lambda e: e.activation(out=ap, in_=ap, func=AF.Exp, scale=-0.5), reads=list(keys), writes=list(keys))

    epsc_b, epsc = small("epsc", [128, 1])
    tr.op("dve", lambda e: e.memset(epsc[:, :], EPS), writes=[epsc_b.k()])

    a2k_b, a2k = small("a2k", [128, 2, 16])
    rho16_b, rho16 = small("rho16", [128, 16])
    PI = math.pi

    def bk(bs):
        return [b.k() for b in bs]

    s5h = {}

    def s5_setup():
        A = lambda n, shp, dt=F32: ar.alloc("s5_" + n, shp, dt)
        P = {}
        for n, shp, src in [("lamre", [128, 16], lamre_d), ("lamim", [128, 16], lamim_d), ("logdt", [128, 16], logdt_d),
                            ("bre", [128, 16, 16], bre_d), ("bim", [128, 16, 16], bim_d), ("cre", [128, 16, 16], cre_d),
                            ("cim", [128, 16, 16], cim_d), ("dbc", [128, 32, 16], dbc_d), ("kv", [128, 41], kv_d),
                            ("cidx", [128, 128], cidx_d), ("identf", [128, 128], identf_d), ("maskT", [128, 256], maskT_d)]:
            b = A(n, shp)
            tr.dma("sp", "const2", b.ap, src, writes=[b.k()])
            P[n] = b
        for b in P.values():
            tr.res[b.k()] = {"w": ("const2", tr.cnt["const2"]), "r": {}}
        tr.defer = []

        def tt(eng, out_b, out_ap, a_b, a_ap, b_b, b_ap, op):
            tr.op(eng, lambda e: e.tensor_tensor(out=out_ap, in0=a_ap, in1=b_ap, op=op), reads=bk([a_b, b_b]), writes=bk([out_b]))

        def ts(eng, out_b, out_ap, a_b, a_ap, s1, s2, op0, op1=None):
            if op1 is None:
                tr.op(eng, lambda e: e.tensor_scalar(out=out_ap, in0=a_ap, scalar1=s1, scalar2=None, op0=op0), reads=bk([a_b]), writes=bk([out_b]))
            else:
                tr.op(eng, lambda e: e.tensor_scalar(out=out_ap, in0=a_ap, scalar1=s1, scalar2=s2, op0=op0, op1=op1), reads=bk([a_b]), writes=bk([out_b]))

        def act(out_b, out_ap, a_b, a_ap, func, **kw):
            tr.op("act", lambda e: e.activation(out=out_ap, in_=a_ap, func=func, **kw), reads=bk([a_b]), writes=bk([out_b]))

        def range_reduce(ang_b, shape):
            ti = A("rr_i", shape, I32)
            tf = A("rr_f", shape, F32)
            ts("dve", ti, ti.ap, ang_b, ang_b.ap, 1.0 / TWO_PI, None, ALU.mult)
            tr.op("dve", lambda e: e.tensor_copy(out=tf.ap, in_=ti.ap), reads=bk([ti]), writes=bk([tf]))
            tr.op("dve", lambda e: e.scalar_tensor_tensor(out=ang_b.ap, in0=tf.ap, scalar=-TWO_PI, in1=ang_b.ap, op0=ALU.mult, op1=ALU.add),
                  reads=bk([tf, ang_b]), writes=bk([ang_b]))
            ts("dve", ang_b, ang_b.ap, ang_b, ang_b.ap, -PI, PI, ALU.max, ALU.min)
            ar.free(ti, tf)

        def sincos(r_b, sin_b, sin_ap, cos_b, cos_ap, shape):
            ab = A("sc_ab", shape)
            act(sin_b, sin_ap, r_b, r_b.ap, AF.Sin)
            act(ab, ab.ap, r_b, r_b.ap, AF.Sin, scale=0.5)
            tt("dve", ab, ab.ap, ab, ab.ap, ab, ab.ap, ALU.mult)
            tr.op("dve", lambda e: e.tensor_scalar(out=cos_ap, in0=ab.ap, scalar1=-2.0, scalar2=1.0, op0=ALU.mult, op1=ALU.add),
                  reads=[ab.k()], writes=[cos_b.k()])
            ar.free(ab)

        hpi_b, hpi = small("hpi", [128, 1])

        NK = 41
        lr, dtt, lrd, lid = A("lr", [128, 16]), A("dt", [128, 16]), A("lrd", [128, 16]), A("lid", [128, 16])
        ts("dve", lr, lr.ap, P["lamre"], P["lamre"].ap, -1e-4, None, ALU.min)
        act(dtt, dtt.ap, P["logdt"], P["logdt"].ap, AF.Exp)
        tt("dve", lrd, lrd.ap, lr, lr.ap, dtt, dtt.ap, ALU.mult)
        tt("dve", lid, lid.ap, P["lamim"], P["lamim"].ap, dtt, dtt.ap, ALU.mult)
        E, ang = A("E", [128, 16, NK]), A("ang", [128, 16, NK])
        kvb = P["kv"].ap.unsqueeze(1).to_broadcast([128, 16, NK])
        tt("dve", E, E.ap, lrd, lrd.ap.unsqueeze(2).to_broadcast([128, 16, NK]), P["kv"], kvb, ALU.mult)
        tt("dve", ang, ang.ap, lid, lid.ap.unsqueeze(2).to_broadcast([128, 16, NK]), P["kv"], kvb, ALU.mult)
        act(E, E.ap, E, E.ap, AF.Exp)
        range_reduce(ang, [128, 16, NK])
        Sn, Cs = A("Sn", [128, 16, NK]), A("Cs", [128, 16, NK])
        sincos(ang, Sn, Sn.ap, Cs, Cs.ap, [128, 16, NK])
        EC, ES, nEC, nES = A("EC", [128, 16, NK]), A("ES", [128, 16, NK]), A("nEC", [128, 16, NK]), A("nES", [128, 16, NK])
        tt("dve", EC, EC.ap, E, E.ap, Cs, Cs.ap, ALU.mult)
        tt("dve", ES, ES.ap, E, E.ap, Sn, Sn.ap, ALU.mult)
        ts("dve", nEC, nEC.ap, EC, EC.ap, -1.0, None, ALU.mult)
        ts("dve", nES, nES.ap, ES, ES.ap, -1.0, None, ALU.mult)
        ar.free(E, ang, Sn, Cs)
        checkpoint("setup1")
        den, t0, nr, cfr, cfi = A("den", [128, 16]), A("t0", [128, 16]), A("nr", [128, 16]), A("cfr", [128, 16]), A("cfi", [128, 16])
        li = P["lamim"]
        abre, abim = EC.ap[:, :, 9], ES.ap[:, :, 9]
        tt("dve", t0, t0.ap, lr, lr.ap, lr, lr.ap, ALU.mult)
        tt("dve", den, den.ap, li, li.ap, li, li.ap, ALU.mult)
        tt("dve", den, den.ap, den, den.ap, t0, t0.ap, ALU.add)
        tr.op("dve", lambda e: e.reciprocal(out=den.ap, in_=den.ap), reads=bk([den]), writes=bk([den]))
        ts("dve", nr, nr.ap, EC, abre, -1.0, None, ALU.add)
        tt("dve", cfr, cfr.ap, nr, nr.ap, lr, lr.ap, ALU.mult)
        tt("dve", t0, t0.ap, ES, abim, li, li.ap, ALU.mult)
        tt("dve", cfr, cfr.ap, cfr, cfr.ap, t0, t0.ap, ALU.add)
        tt("dve", cfr, cfr.ap, cfr, cfr.ap, den, den.ap, ALU.mult)
        tt("dve", cfi, cfi.ap, ES, abim, lr, lr.ap, ALU.mult)
        tt("dve", t0, t0.ap, nr, nr.ap, li, li.ap, ALU.mult)
        tt("dve", cfi, cfi.ap, cfi, cfi.ap, t0, t0.ap, ALU.subtract)
        tt("dve", cfi, cfi.ap, cfi, cfi.ap, den, den.ap, ALU.mult)
        bbre, bbim, t1s, t2s = A("bbre", [128, 16, 16]), A("bbim", [128, 16, 16]), A("t1s", [128, 16, 16]), A("t2s", [128, 16, 16])
        cb = lambda b: b.ap.unsqueeze(2).to_broadcast([128, 16, 16])
        tt("dve", t1s, t1s.ap, cfr, cb(cfr), P["bre"], P["bre"].ap, ALU.mult)
        tt("dve", t2s, t2s.ap, cfi, cb(cfi), P["bim"], P["bim"].ap, ALU.mult)
        tt("dve", bbre, bbre.ap, t1s, t1s.ap, t2s, t2s.ap, ALU.subtract)
        tt("dve", t1s, t1s.ap, cfr, cb(cfr), P["bim"], P["bim"].ap, ALU.mult)
        tt("dve", t2s, t2s.ap, cfi, cb(cfi), P["bre"], P["bre"].ap, ALU.mult)
        tt("dve", bbim, bbim.ap, t1s, t1s.ap, t2s, t2s.ap, ALU.add)
        ar.free(den, t0, nr, cfr, cfi, t1s, t2s)
        checkpoint("setup1b")

        def outer(eng, out_b, out_ap, pw_b, lo, ns, vec_b):
            tr.op(eng, lambda e: e.tensor_tensor(out=out_ap, in0=pw_b.ap[:, :, lo:lo + ns].unsqueeze(3).to_broadcast([128, 16, ns, 16]),
                                                 in1=vec_b.ap.unsqueeze(2).to_broadcast([128, 16, ns, 16]), op=ALU.mult),
                  reads=bk([pw_b, vec_b]), writes=bk([out_b]))

        tr.stop_defer()
        yield
        Xre, Xim, o1, o1d = A("Xre", [128, 16, 16, 16]), A("Xim", [128, 16, 16, 16]), A("o1", [128, 16, 16, 16]), A("o1d", [128, 16, 16, 16])
        Yre, Yim = A("Yre", [128, 16, 8, 16]), A("Yim", [128, 16, 8, 16])
        outer("pool", Xre, Xre.ap, EC, 9, 16, P["cre"])
        outer("pool", o1, o1.ap, ES, 9, 16, P["cim"])
        tt("pool", Xre, Xre.ap, Xre, Xre.ap, o1, o1.ap, ALU.subtract)
        outer("dve", Xim, Xim.ap, nEC, 9, 16, P["cim"])
        outer("dve", o1d, o1d.ap, nES, 9, 16, P["cre"])
        tt("dve", Xim, Xim.ap, Xim, Xim.ap, o1d, o1d.ap, ALU.add)
        o1y = o1d.ap[:, :, 0:8, :]
        outer("dve", Yre, Yre.ap, EC, 0, 8, bbre)
        outer("dve", o1d, o1y, ES, 0, 8, bbim)
        tt("dve", Yre, Yre.ap, Yre, Yre.ap, o1d, o1y, ALU.subtract)
        outer("dve", Yim, Yim.ap, EC, 0, 8, bbim)
        outer("dve", o1d, o1y, ES, 0, 8, bbre)
        tt("dve", Yim, Yim.ap, Yim, Yim.ap, o1d, o1y, ALU.add)
        ar.free(o1, o1d)
        cpb = A("cpb", [128, 2, 16, 256], BF16)
        act(cpb, cpb.ap[:, 0], Xre, Xre.ap.rearrange("p a s q -> p a (s q)"), AF.Copy)
        act(cpb, cpb.ap[:, 1], Xim, Xim.ap.rearrange("p a s q -> p a (s q)"), AF.Copy)
        tr.dma("sp", "s5st", cp_dr.ap(), cpb.ap, reads=bk([cpb]), writes=[("cp_dr",)])
        ar.free(cpb)
        checkpoint("setup2")
        bsnb = A("bsnb", [128, 2, 16, 256], BF16)
        o1q, o2q = A("o1q", [128, 4, 16, 16]), A("o2q", [128, 4, 16, 16])
        bsn_ops = []

        def outer_q(out_b, pw_b, vec_b, q):
            bsn_ops.append(lambda: tr.op("dve", lambda e: e.tensor_tensor(
                out=out_b.ap, in0=pw_b.ap[:, 4 * q:4 * q + 4, 25:41].unsqueeze(3).to_broadcast([128, 4, 16, 16]),
                in1=vec_b.ap[:, 4 * q:4 * q + 4, :].unsqueeze(2).to_broadcast([128, 4, 16, 16]), op=ALU.mult),
                reads=bk([pw_b, vec_b]), writes=bk([out_b])))

        def comb_q(ri, q, op):
            fl = lambda b: b.ap.rearrange("p a s q -> p a (s q)")
            bsn_ops.append(lambda: tr.op("dve", lambda e: e.tensor_tensor(out=bsnb.ap[:, ri, 4 * q:4 * q + 4, :], in0=fl(o1q), in1=fl(o2q), op=op),
                                         reads=bk([o1q, o2q]), writes=[bsnb.k(ri, q)]))

        for q in range(4):
            outer_q(o1q, EC, bbre, q)
            outer_q(o2q, ES, bbim, q)
            comb_q(0, q, ALU.subtract)
            outer_q(o1q, EC, bbim, q)
            outer_q(o2q, ES, bbre, q)
            comb_q(1, q, ALU.add)
        toepb = A("toepb", [128, 32, 256], BF16)
        tmpT = [A(f"tmpT{j}", [128, 256]) for j in range(2)]
        dg = [A(f"dg{j}", [128, 128]) for j in range(2)]
        for g in range(32):
            pr, g2 = g // 2, g % 2
            rows = slice(g2 * 64, g2 * 64 + 64)
            bi = mmpool.next()
            tr.op("pe", lambda e: e.matmul(ps_f[bi][:, 0:256], lhsT=Yre.ap[rows, pr].rearrange("p s q -> p (s q)"),
                                           rhs=Xre.ap[rows, pr].rearrange("p s q -> p (s q)"), start=True, stop=False),
                  reads=bk([Yre, Xre]), writes=[psk_f[bi]])
            tr.op("pe", lambda e: e.matmul(ps_f[bi][:, 0:256], lhsT=Yim.ap[rows, pr].rearrange("p s q -> p (s q)"),
                                           rhs=Xim.ap[rows, pr].rearrange("p s q -> p (s q)"), start=False, stop=True),
                  reads=bk([Yim, Xim]), writes=[psk_f[bi]])
            tT, dG = tmpT[g % 2], dg[g % 2]
            tr.op("dve", lambda e: e.tensor_tensor(out=tT.ap, in0=ps_f[bi][:, 0:256], in1=P["maskT"].ap, op=ALU.mult),
                  reads=[psk_f[bi], P["maskT"].k()], writes=bk([tT]))
            tr.op("pool", lambda e: e.tensor_tensor(out=dG.ap.rearrange("p (s q) -> p s q", s=8),
                                                    in0=P["identf"].ap.rearrange("p (s q) -> p s q", s=8),
                                                    in1=P["dbc"].ap[:, g, :].unsqueeze(1).to_broadcast([128, 8, 16]), op=ALU.mult),
                  reads=bk([P["identf"], P["dbc"]]), writes=bk([dG]))
            tr.op("dve", lambda e: e.tensor_tensor(out=toepb.ap[:, g, 0:128], in0=tT.ap[:, 0:128], in1=dG.ap, op=ALU.add),
                  reads=bk([tT, dG]), writes=[toepb.k(g, 0)])
            tr.op("act", lambda e: e.activation(out=toepb.ap[:, g, 128:256], in_=tT.ap[:, 128:256], func=AF.Copy),
                  reads=bk([tT]), writes=[toepb.k(g, 1)])
            if g >= 4 and bsn_ops:
                bsn_ops.pop(0)()
        tr.dma("sp", "s5st", toep_dr.ap(), toepb.ap, reads=[toepb.k(g, j) for g in range(32) for j in range(2)], writes=[("toep_dr",)])
        if debug:
            tr.dma("sp", "dbg", dbg["toep"], toepb.ap, reads=[toepb.k(g, j) for g in range(32) for j in range(2)])
        while bsn_ops:
            bsn_ops.pop(0)()
        ar.free(Xre, Xim, Yre, Yim, toepb, *tmpT, *dg)
        checkpoint("setup3")
        yield
        ar.free(o1q, o2q, bbre, bbim, EC, ES, nEC, nES, *[P[n] for n in ("bre", "bim", "cre", "cim", "dbc", "maskT", "identf", "kv", "lamre", "lamim", "logdt")])
        bpb = A("bpb", [128, 2, 16, 2, 128], BF16)
        n_ev = 0
        for j in range(2):
            for prb in range(4):
                pb_i = trpool.next()
                pt = ps_b[pb_i]
                for pr in range(4 * prb, 4 * prb + 4):
                    for ri in range(2):
                        col = ((pr % 4) * 2 + ri) * 128
                        tr.op("pe", lambda e: e.transpose(out=pt[:, col:col + 128], in_=bsnb.ap[:, ri, pr, j * 128:(j + 1) * 128],
                                                          identity=CT("identb")[:]),
                              reads=[bsnb.k(ri, pr // 4), CB("identb").k()], writes=[psk_b[pb_i]])
                dst = bpb.ap[:, j, 4 * prb:4 * prb + 4].rearrange("p a r n -> p (a r n)")
                if n_ev % 2 == 0:
                    tr.op("dve", lambda e: e.tensor_copy(out=dst, in_=pt[:, :]), reads=[psk_b[pb_i]], writes=[bpb.k(j, prb)])
                else:
                    tr.op("act", lambda e: e.activation(out=dst, in_=pt[:, :], func=AF.Copy), reads=[psk_b[pb_i]], writes=[bpb.k(j, prb)])
                n_ev += 1
        tr.dma("sp", "s5st", bp_dr.ap(), bpb.ap, reads=[bpb.k(j, gb) for j in range(2) for gb in range(4)], writes=[("bp_dr",)])
        ar.free(bsnb, bpb)
        checkpoint("setup4")
        yield
        lrd16, phi = A("lrd16", [128, 16]), A("phi", [128, 16])
        ts("dve", lrd16, lrd16.ap, lrd, lrd.ap, 16.0, None, ALU.mult)
        ts("dve", phi, phi.ap, lid, lid.ap, 16.0, None, ALU.mult)
        range_reduce(phi, [128, 16])
        act(rho16_b, rho16[:, :], lrd16, lrd16.ap, AF.Exp)
        tabs = A("tabs", [128, 3, 16, 128])
        angT = A("angT", [128, 16, 128])
        cib = P["cidx"].ap.unsqueeze(1).to_broadcast([128, 16, 128])
        tt("dve", tabs, tabs.ap[:, 0], lrd16, lrd16.ap.unsqueeze(2).to_broadcast([128, 16, 128]), P["cidx"], cib, ALU.mult)
        act(tabs, tabs.ap[:, 0], tabs, tabs.ap[:, 0], AF.Exp)
        tt("dve", angT, angT.ap, phi, phi.ap.unsqueeze(2).to_broadcast([128, 16, 128]), P["cidx"], cib, ALU.mult)
        range_reduce(angT, [128, 16, 128])
        sincos(angT, tabs, tabs.ap[:, 2], tabs, tabs.ap[:, 1], [128, 16, 128])
        tt("dve", a2k_b, a2k[:, 0, :], tabs, tabs.ap[:, 0, :, 127], tabs, tabs.ap[:, 1, :, 127], ALU.mult)
        tt("dve", a2k_b, a2k[:, 1, :], tabs, tabs.ap[:, 0, :, 127], tabs, tabs.ap[:, 2, :, 127], ALU.mult)
        s5h["tabs"] = tabs
        ar.free(angT, lrd16, phi, lr, dtt, lrd, lid, P["cidx"])

    setup_gen = s5_setup()
    next(setup_gen)

    def rest():
        xb = [ar.alloc(f"xb{i}", [128, D], F32) for i in range(3)]
        xnb = [ar.alloc(f"xnb{i}", [128, D], BF16) for i in range(2)]
        junk = ar.alloc("junk", [128, D], BF16)
        ssq_b, ssq = small("ssq", [128, NT])
        rstd_b, rstd = small("rstd", [128, NT])
        early = {0: (SL_F, C_F, 0), 2: (SL_F, C_F, 256), 4: (SL_Q, C_Q, 0), 6: (SL_Q, C_Q, 256), 8: (SL_I, C_I, 0), 10: (SL_I, C_I, 256)}
        for i in range(NT):
            tr.replay(5)
            if i in early:
                load_win_group(early[i][0], early[i][1], eng="dve", only=early[i][2])
            xt = xb[i % 3]
            tr.dma("sp", f"xb{i % 3}", xt.ap, x_d[i * 128:(i + 1) * 128, :], writes=[xt.k()])
            tr.op("act", lambda e: e.activation(out=junk.ap, in_=xt.ap, func=AF.Square, accum_out=ssq[:, i:i + 1]),
                  reads=[xt.k()], writes=[junk.k(), ssq_b.k(i)])
            tr.op("act", lambda e: e.activation(out=rstd[:, i:i + 1], in_=ssq[:, i:i + 1], func=AF.Ln, scale=1.0 / D, bias=epsc[:, 0:1]),
                  reads=[ssq_b.k(i), epsc_b.k()], writes=[rstd_b.k(i)])
            tr.op("act", lambda e: e.activation(out=rstd[:, i:i + 1], in_=rstd[:, i:i + 1], func=AF.Exp, scale=-0.5),
                  reads=[rstd_b.k(i)], writes=[rstd_b.k(i)])
            xn = xnb[i % 2]
            if i % 2:
                tr.op("dve", lambda e: e.tensor_scalar(out=xn.ap, in0=xt.ap, scalar1=rstd[:, i:i + 1], scalar2=None, op0=ALU.mult),
                      reads=[xt.k(), rstd_b.k(i)], writes=[xn.k()])
            else:
                tr.op("act", lambda e: e.activation(out=xn.ap, in_=xt.ap, func=AF.Copy, scale=rstd[:, i:i + 1]),
                      reads=[xt.k(), rstd_b.k(i)], writes=[xn.k()])
            pb_i = trpool.next()
            pt = ps_b[pb_i]
            for kc in range(8):
                tr.op("pe", lambda e: e.transpose(out=pt[:, kc * 128:(kc + 1) * 128], in_=xn.ap[:, kc * 128:(kc + 1) * 128],
                                                  identity=CT("identb")[:]),
                      reads=[xn.k(), CB("identb").k()], writes=[psk_b[pb_i]])
            tr.op("dve", lambda e: e.tensor_tensor(out=xnT.ap[:, :, i * 128:(i + 1) * 128],
                                                   in0=pt[:, :].rearrange("p (a b) -> p a b", a=8),
                                                   in1=CT("nw")[:, 0:8].unsqueeze(2).to_broadcast([128, 8, 128]), op=ALU.mult),
                  reads=[psk_b[pb_i], CB("nw").k()], writes=[xnT.k(i)])
        ar.free(*xb, *xnb, junk)
        tr.replay()
        checkpoint("phaseA")
        V = ar.alloc("V", [128, NT, 512], BF16)
        for i in range(NT):
            bi = mmpool.next()
            for kc in range(8):
                tr.op("pe", lambda e: e.matmul(ps_f[bi][:, :], lhsT=xnT.ap[:, kc, i * 128:(i + 1) * 128],
                                               rhs=wbf[SL_I].ap[:, kc, :], start=(kc == 0), stop=(kc == 7)),
                      reads=[wbf[SL_I].k(), xnT.k(i)], writes=[psk_f[bi]])
            tr.op("act", lambda e: e.activation(out=V.ap[:, i, :], in_=ps_f[bi][:, :], func=AF.Copy), reads=[psk_f[bi]], writes=[V.k(i)])
        ar.free(wbf[SL_I])
        next(setup_gen)
        checkpoint("setup")
        wbf.append(ar.alloc("wbf3", [128, 8, 512], BF16))
        load_win_group(SL_OG, C_OG)

        def proj_fm(wb, ct, tb):
            bi = mmpool.next()
            for kc in range(8):
                tr.op("pe", lambda e: e.matmul(ps_f[bi][:, :], lhsT=wb.ap[:, kc, ct * 128:(ct + 1) * 128],
                                               rhs=xnT.ap[:, kc, tb * 512:(tb + 1) * 512], start=(kc == 0), stop=(kc == 7)),
                      reads=[wb.k()] + [xnT.k(4 * tb + j) for j in range(4)], writes=[psk_f[bi]])
            return bi

        def proj_tm(wb, i):
            bi = mmpool.next()
            for kc in range(8):
                tr.op("pe", lambda e: e.matmul(ps_f[bi][:, :], lhsT=xnT.ap[:, kc, i * 128:(i + 1) * 128],
                                               rhs=wb.ap[:, kc, :], start=(kc == 0), stop=(kc == 7)),
                      reads=[wb.k(), xnT.k(i)], writes=[psk_f[bi]])
            return bi

        lb_b, lb = small("lb", [128, 4])
        oml_b, oml = small("oml", [128, 4])
        noml_b, noml = small("noml", [128, 4])
        tr.op("dve", lambda e: e.tensor_sub(out=lb[:, :], in0=CT("lbl")[:, 0, :], in1=CT("lbl")[:, 1, :]),
              reads=[CB("lbl").k()], writes=[lb_b.k()])
        sigmoid3(lb[:, :], [lb_b.k()], lb[:, :], [lb_b.k()], None)
        tr.op("dve", lambda e: e.tensor_scalar(out=oml[:, :], in0=lb[:, :], scalar1=-1.0, scalar2=1.0, op0=ALU.mult, op1=ALU.add),
              reads=[lb_b.k()], writes=[oml_b.k()])
        tr.op("dve", lambda e: e.tensor_scalar(out=noml[:, :], in0=lb[:, :], scalar1=-1.0, scalar2=None, op0=ALU.add),
              reads=[lb_b.k()], writes=[noml_b.k()])

        KdT = ar.alloc("KdT", [128, 4, TT], BF16)
        QdT = ar.alloc("QdT", [128, 4, TT], BF16)
        lastc_b, lastc = small("lastc", [128, 4, 32])
        el_b, el = small("el", [128, 4, 33])
        tmp = {n: [ar.alloc(f"t_{n}{j}", [128, 512], F32) for j in range(2)] for n in ["e1", "A", "lg", "e2"]}
        lnoml_b, lnoml = small("lnoml", [128, 4])
        tr.op("act", lambda e: e.activation(out=lnoml[:, :], in_=oml[:, :], func=AF.Ln), reads=[oml_b.k()], writes=[lnoml_b.k()])
        it = 0
        for h in range(4):
            for tb in range(NB):
                j = it % 2
                it += 1
                T = {n: tmp[n][j] for n in tmp}
                pf = proj_fm(wbf[SL_F], h, tb)
                pq = proj_fm(wbf[SL_Q], h, tb)
                tr.op("act", lambda e: e.activation(out=T["e1"].ap, in_=ps_f[pf][:, :], func=AF.Exp, scale=-1.0), reads=[psk_f[pf]], writes=[T["e1"].k()])
                tr.op("act", lambda e: e.activation(out=T["A"].ap, in_=T["e1"].ap, func=AF.Ln, bias=1.0), reads=[T["e1"].k()], writes=[T["A"].k()])
                tr.op("act", lambda e: e.activation(out=T["lg"].ap, in_=T["e1"].ap, func=AF.Ln, scale=lb[:, h:h + 1], bias=1.0),
                      reads=[T["e1"].k(), lb_b.k()], writes=[T["lg"].k()])
                tr.op("act", lambda e: e.activation(out=T["e2"].ap, in_=ps_f[pq][:, :], func=AF.Exp, scale=-1.0), reads=[psk_f[pq]], writes=[T["e2"].k()])
                tr.op("act", lambda e: e.activation(out=T["e2"].ap, in_=T["e2"].ap, func=AF.Ln, bias=1.0), reads=[T["e2"].k()], writes=[T["e2"].k()])
                tr.op("dve", lambda e: e.tensor_sub(out=T["lg"].ap, in0=T["lg"].ap, in1=T["A"].ap), reads=[T["lg"].k(), T["A"].k()], writes=[T["lg"].k()])
                tr.op("dve", lambda e: e.tensor_tensor_scan(out=T["lg"].ap, data0=CT("mask512")[:, :], data1=T["lg"].ap, initial=0.0,
                                                            op0=ALU.mult, op1=ALU.add),
                      reads=[T["lg"].k(), CB("mask512").k()], writes=[T["lg"].k()])
                tr.op("dve", lambda e: e.tensor_copy(out=lastc[:, h, tb * 8:(tb + 1) * 8], in_=T["lg"].ap[:, 63:512:64]),
                      reads=[T["lg"].k()], writes=[lastc_b.k(h, tb)])
                tr.op("dve", lambda e: e.tensor_add(out=T["A"].ap, in0=T["A"].ap, in1=T["lg"].ap), reads=[T["lg"].k(), T["A"].k()], writes=[T["A"].k()])
                tr.op("dve", lambda e: e.tensor_tensor(out=T["A"].ap, in0=ps_f[pf][:, :], in1=T["A"].ap, op=ALU.add),
                      reads=[psk_f[pf], T["A"].k()], writes=[T["A"].k()])
                tr.op("act", lambda e: e.activation(out=KdT.ap[:, h, tb * 512:(tb + 1) * 512], in_=T["A"].ap, func=AF.Exp, scale=-1.0,
                                                    bias=lnoml[:, h:h + 1]),
                      reads=[T["A"].k(), lnoml_b.k()], writes=[KdT.k(h, tb)])
                tr.op("dve", lambda e: e.tensor_sub(out=T["e2"].ap, in0=T["lg"].ap, in1=T["e2"].ap), reads=[T["lg"].k(), T["e2"].k()], writes=[T["e2"].k()])
                tr.op("act", lambda e: e.activation(out=T["e2"].ap, in_=T["e2"].ap, func=AF.Exp), reads=[T["e2"].k()], writes=[T["e2"].k()])
                tr.op("dve", lambda e: e.tensor_tensor(out=QdT.ap[:, h, tb * 512:(tb + 1) * 512], in0=ps_f[pq][:, :], in1=T["e2"].ap, op=ALU.mult),
                      reads=[psk_f[pq], T["e2"].k()], writes=[QdT.k(h, tb)])
        load_win_group(SL_ZB, C_ZB)
        load_win_group(SL_U, C_U)
        for n in tmp:
            ar.free(*tmp[n])
        checkpoint("step1")
        KdTM = ar.alloc("KdTM", [128, NT, 512], BF16)
        allc = [lastc_b.k(h, tb) for h in range(4) for tb in range(NB)]
        tr.op("dve", lambda e: e.memset(el[:, :, 0:1], 1.0), writes=[el_b.k()])
        tr.op("act", lambda e: e.activation(out=el[:, :, 1:33], in_=lastc[:, :, :], func=AF.Exp), reads=allc + [el_b.k()], writes=[el_b.k()])
        pk_b, pk = small("pk", [128, 516])
        tr.op("dve", lambda e: e.reduce_sum(out=pk[:, 512:516], in_=lastc[:, :, :], axis=AX.X), reads=allc, writes=[pk_b.k("d")])
        sfx_b, sfx = small("sfx", [128, 4, 32])
        ones_b, ones = small("ones32", [128, 32])
        tr.op("dve", lambda e: e.memset(ones[:, :], 1.0), writes=[ones_b.k()])
        for h in range(4):
            tr.op("dve", lambda e: e.tensor_tensor_scan(out=sfx[:, h, :], data0=ones[:, :], data1=lastc[:, h, :], initial=0.0,
                                                        op0=ALU.mult, op1=ALU.add), reads=allc + [ones_b.k()], writes=[sfx_b.k()])
        tr.op("dve", lambda e: e.tensor_sub(out=sfx[:, :, :], in0=lastc[:, :, :], in1=sfx[:, :, :]), reads=allc + [sfx_b.k()], writes=[sfx_b.k()])
        tr.op("dve", lambda e: e.tensor_tensor(out=sfx[:, :, :], in0=sfx[:, :, :], in1=pk[:, 512:516].unsqueeze(2).to_broadcast([128, 4, 32]), op=ALU.add),
              reads=[sfx_b.k(), pk_b.k("d")], writes=[sfx_b.k()])
        tr.op("act", lambda e: e.activation(out=sfx[:, :, :], in_=sfx[:, :, :], func=AF.Exp), reads=[sfx_b.k()], writes=[sfx_b.k()])
        tr.op("act", lambda e: e.activation(out=pk[:, 512:516], in_=pk[:, 512:516], func=AF.Exp), reads=[pk_b.k("d"), sfx_b.k()], writes=[pk_b.k("d")])

        for i in range(NT):
            pb_i = trpool.next()
            pt = ps_b[pb_i]
            for h in range(4):
                tr.op("pe", lambda e: e.transpose(out=pt[:, h * 128:(h + 1) * 128], in_=KdT.ap[:, h, i * 128:(i + 1) * 128],
                                                  identity=CT("identb")[:]),
                      reads=[KdT.k(h, i // 4), CB("identb").k()], writes=[psk_b[pb_i]])
            if i % 2:
                tr.op("dve", lambda e: e.tensor_copy(out=KdTM.ap[:, i, :], in_=pt[:, 0:512]), reads=[psk_b[pb_i]], writes=[KdTM.k(i)])
            else:
                tr.op("act", lambda e: e.activation(out=KdTM.ap[:, i, :], in_=pt[:, 0:512], func=AF.Copy), reads=[psk_b[pb_i]], writes=[KdTM.k(i)])
        KdPh = [ar.alloc(f"KdP{j}", [128, TT], BF16) for j in range(2)]
        KdPTMh = [ar.alloc(f"KdPTM{j}", [128, NT, 128], BF16) for j in range(2)]
        sbank = 4
        for h in range(4):
            kp, kpt = KdPh[h % 2], KdPTMh[h % 2]
            tr.op("pool", lambda e: e.tensor_tensor(out=kp.ap.rearrange("p (c t) -> p c t", t=64),
                                                    in0=KdT.ap[:, h, :].rearrange("p (c t) -> p c t", t=64),
                                                    in1=sfx[:, h, :].unsqueeze(2).to_broadcast([128, 32, 64]), op=ALU.mult),
                  reads=[KdT.k(h, tb) for tb in range(NB)] + [sfx_b.k()], writes=[kp.k()])
            for half in range(2):
                pb_i = trpool.next()
                pt = ps_b[pb_i]
                for ii in range(8):
                    i = 8 * half + ii
                    tr.op("pe", lambda e: e.transpose(out=pt[:, ii * 128:(ii + 1) * 128], in_=kp.ap[:, i * 128:(i + 1) * 128],
                                                      identity=CT("identb")[:]),
                          reads=[kp.k(), CB("identb").k()], writes=[psk_b[pb_i]])
                if half:
                    tr.op("dve", lambda e: e.tensor_copy(out=kpt.ap[:, 8 * half:8 * half + 8, :], in_=pt[:, :].rearrange("p (a b) -> p a b", a=8)),
                          reads=[psk_b[pb_i]], writes=[kpt.k(half)])
                else:
                    tr.op("act", lambda e: e.activation(out=kpt.ap[:, 8 * half:8 * half + 8, :], in_=pt[:, :].rearrange("p (a b) -> p a b", a=8), func=AF.Copy),
                          reads=[psk_b[pb_i]], writes=[kpt.k(half)])
            for i in range(NT):
                tr.op("pe", lambda e: e.matmul(ps_f[sbank][:, h * 128:(h + 1) * 128], lhsT=kpt.ap[:, i, :],
                                               rhs=V.ap[:, i, h * 128:(h + 1) * 128], start=(i == 0), stop=(i == NT - 1)),
                      reads=[kpt.k(i // 8), V.k(i)], writes=[psk_f[sbank]])
        tr.op("dve", lambda e: e.tensor_copy(out=pk[:, 0:512], in_=ps_f[sbank][:, :]), reads=[psk_f[sbank]], writes=[pk_b.k("s")])
        ar.free(*KdPh, *KdPTMh)
        checkpoint("pass1")
        pk_keys = [pk_b.k("d"), pk_b.k("s")]
        agi_k, ago_k = ("agi_h",), ("ago_h",)
        tr.dma("pool", "agh", agi_h[:, :], pk[:, :], reads=pk_keys, writes=[agi_k])
        tr.collective("cc", lambda e: e.collective_compute("AllGather", ALU.bypass, replica_groups=[[0, 1, 2, 3], [4, 5, 6, 7]],
                                                           ins=[agi_h.ap().opt()], outs=[ago_h.ap().opt()]),
                      reads=[agi_k], writes=[ago_k])
        next(setup_gen)
        U_b = ar.alloc("U", [128, 2, 4, 128], F32)
        U = U_b.ap
        Sbf_b = ar.alloc("Sbf", [128, 4, 4, 128], BF16)
        Sbf = Sbf_b.ap
        ybT = ar.alloc("ybT", [128, 4, TT], BF16)
        p2 = {n: [ar.alloc(f"p2_{n}{j}", [128, 512], F32) for j in range(k)] for n, k in (("sog", 3), ("szb", 4), ("go", 3))}
        scb = [ar.alloc(f"scb{j}", [128, 4, 128], BF16) for j in range(2)]
        ybt = [ar.alloc(f"ybt{j}", [128, 512], BF16) for j in range(2)]
        junk2 = ar.alloc("junk2", [128, 128], BF16)
        ss_b, ss = small("ss", [128, NT, 4])
        ps_u = [ps_b[0][:, :].bitcast(F32), ps_b[1][:, :].bitcast(F32)]

        def upd_mm(c):
            i, hh = c // 2, c % 2
            for h in range(4):
                tr.op("pe", lambda e: e.matmul(ps_u[hh][:, h * 128:(h + 1) * 128],
                                               lhsT=KdTM.ap[hh * 64:(hh + 1) * 64, i, h * 128:(h + 1) * 128],
                                               rhs=V.ap[hh * 64:(hh + 1) * 64, i, h * 128:(h + 1) * 128], start=True, stop=True),
                      reads=[KdTM.k(i), V.k(i)], writes=[psk_b[hh]])

        def upd_state(c):
            hh = c % 2
            for h in range(4):
                tr.op("dve", lambda e: e.scalar_tensor_tensor(out=U[:, hh, h, :], in0=U[:, 1 - hh, h, :], scalar=el[:, h, c:c + 1],
                                                              in1=ps_u[hh][:, h * 128:(h + 1) * 128], op0=ALU.mult, op1=ALU.add),
                      reads=[U_b.k(1 - hh, h), el_b.k(), psk_b[hh]], writes=[U_b.k(hh, h)])
            slot = (c + 1) % 4
            for h in range(4):
                tr.op("pool", lambda e: e.tensor_scalar(out=Sbf[:, h, slot, :], in0=U[:, hh, h, :], scalar1=el[:, h, c + 1:c + 2], scalar2=0.0, op0=ALU.mult, op1=ALU.add),
                      reads=[U_b.k(hh, h), el_b.k()], writes=[Sbf_b.k(slot, h)])

        def st_proj(i):
            r = {}
            for nm, wb in (("og", wbf[SL_OG]), ("zb", wbf[SL_ZB])):
                bi = p2pool.next()
                for kc in range(8):
                    tr.op("pe", lambda e: e.matmul(ps_f[bi][:, :], lhsT=xnT.ap[:, kc, i * 128:(i + 1) * 128],
                                                   rhs=wb.ap[:, kc, :], start=(kc == 0), stop=(kc == 7)),
                          reads=[wb.k(), xnT.k(i)], writes=[psk_f[bi]])
                r[nm] = bi
            pbank[i] = r

        def st_gate_act(i):
            pog, pzb = pbank[i]["og"], pbank[i]["zb"]
            sigmoid3(p2["sog"][i % 3].ap, [p2["sog"][i % 3].k()], ps_f[pog][:, :], [psk_f[pog]], None)
            sigmoid3(p2["szb"][i % 4].ap, [p2["szb"][i % 4].k()], ps_f[pzb][:, :], [psk_f[pzb]], None)

        def st_gate_dve(i):
            pzb = pbank[i]["zb"]
            tr.op("dve", lambda e: e.tensor_tensor(out=p2["szb"][i % 4].ap, in0=ps_f[pzb][:, :], in1=p2["szb"][i % 4].ap, op=ALU.mult),
                  reads=[psk_f[pzb], p2["szb"][i % 4].k()], writes=[p2["szb"][i % 4].k()])

        def st_b1(i):
            obi = 4 + i % 2
            go = p2["go"][i % 3]
            tr.op("dve", lambda e: e.tensor_mul(out=go.ap, in0=ps_f[obi][:, :], in1=p2["sog"][i % 3].ap),
                  reads=[psk_f[obi], p2["sog"][i % 3].k()], writes=[go.k()])
            for h in range(4):
                tr.op("act", lambda e: e.activation(out=junk2.ap, in_=go.ap[:, h * 128:(h + 1) * 128], func=AF.Square,
                                                    accum_out=ss[:, i, h:h + 1]),
                      reads=[go.k()], writes=[junk2.k(), ss_b.k(i, h)])

        def st_b2(i):
            go = p2["go"][i % 3]
            ssk = [ss_b.k(i, h) for h in range(4)]
            rstd_act(ss[:, i, :], ssk, 1.0 / 128)
            for h in range(4):
                cols = slice(h * 128, (h + 1) * 128)
                tr.op("dve", lambda e: e.scalar_tensor_tensor(out=ybt[i % 2].ap[:, cols], in0=go.ap[:, cols], scalar=ss[:, i, h:h + 1],
                                                              in1=p2["szb"][i % 4].ap[:, cols], op0=ALU.mult, op1=ALU.mult),
                      reads=[go.k(), p2["szb"][i % 4].k()] + ssk, writes=[ybt[i % 2].k()])

        def st_b3(i):
            pt = ps_f[3][:, :].bitcast(BF16)
            for h in range(4):
                c0 = (i % 2) * 512 + h * 128
                tr.op("pe", lambda e: e.transpose(out=pt[:, c0:c0 + 128], in_=ybt[i % 2].ap[:, h * 128:(h + 1) * 128],
                                                  identity=CT("identb")[:]),
                      reads=[ybt[i % 2].k(), CB("identb").k()], writes=[("ps3h", i % 2)])

        def st_b4(i):
            pt = ps_f[3][:, :].bitcast(BF16)
            c0 = (i % 2) * 512
            tr.op("act", lambda e: e.activation(out=ybT.ap[:, :, i * 128:(i + 1) * 128],
                                                in_=pt[:, c0:c0 + 512].rearrange("p (a b) -> p a b", a=4), func=AF.Copy),
                  reads=[("ps3h", i % 2)], writes=[ybT.k(i)])

        def st_scores(i):
            sbi = 4 + i % 2
            for h in range(4):
                tr.op("pe", lambda e: e.matmul(ps_f[sbi][:, h * 128:(h + 1) * 128], lhsT=KdT.ap[:, h, i * 128:(i + 1) * 128],
                                               rhs=QdT.ap[:, h, i * 128:(i + 1) * 128], start=True, stop=True),
                      reads=[KdT.k(h, i // 4), QdT.k(h, i // 4)], writes=[psk_f[sbi]])
            upd_mm(2 * i)
            upd_mm(2 * i + 1)

        def st_state(i):
            sbi = 4 + i % 2
            tr.op("dve", lambda e: e.tensor_tensor(out=scb[i % 2].ap, in0=ps_f[sbi][:, :].rearrange("p (h t) -> p h t", h=4),
                                                   in1=CT("maskbc")[:, :].unsqueeze(1).to_broadcast([128, 4, 128]), op=ALU.mult),
                  reads=[psk_f[sbi], CB("maskbc").k()], writes=[scb[i % 2].k()])
            upd_state(2 * i)
            upd_state(2 * i + 1)

        def st_o(i):
            sbi = 4 + i % 2
            s0, s1 = (2 * i) % 4, (2 * i + 1) % 4
            for h in range(4):
                cols = slice(h * 128, (h + 1) * 128)
                tr.op("pe", lambda e: e.matmul(ps_f[sbi][:, cols], lhsT=scb[i % 2].ap[:, h, :], rhs=V.ap[:, i, cols], start=True, stop=False),
                      reads=[scb[i % 2].k(), V.k(i)], writes=[psk_f[sbi]])
                tr.op("pe", lambda e: e.matmul(ps_f[sbi][0:64, cols], lhsT=QdT.ap[:, h, i * 128:i * 128 + 64], rhs=Sbf[:, h, s0, :],
                                               start=False, stop=True),
                      reads=[QdT.k(h, i // 4), Sbf_b.k(s0, h)], writes=[psk_f[sbi]])
                tr.op("pe", lambda e: e.matmul(ps_f[sbi][64:128, cols], lhsT=QdT.ap[:, h, i * 128 + 64:(i + 1) * 128], rhs=Sbf[:, h, s1, :],
                                               start=False, stop=True),
                      reads=[QdT.k(h, i // 4), Sbf_b.k(s1, h)], writes=[psk_f[sbi]])

        p2pool = PsumPool([0, 1, 2])
        pbank = {}
        ok = lambda t: 0 <= t < NT
        st_proj(0)
        st_gate_act(0)
        st_gate_dve(0)
        gath = ar.alloc("gath", [128, 4, 516], F32)
        tr.dma("sp", "agh2", gath.ap, ago_h[:, :].rearrange("(j p) f -> p j f", p=128), reads=[ago_k], writes=[gath.k()])
        Sin_b, Sin = small("Sin", [128, 4, 128])
        ft = ar.alloc("ft", [128, 4, 128], F32)
        tr.op("dve", lambda e: e.memset(Sin[:, :, :], 0.0), writes=[Sin_b.k()])
        for j in range(3):
            tr.op("dve", lambda e: e.tensor_tensor(out=ft.ap, in0=Sin[:, :, :],
                                                   in1=gath.ap[:, j, 512:516].unsqueeze(2).to_broadcast([128, 4, 128]), op=ALU.mult),
                  reads=[Sin_b.k(), gath.k()], writes=[ft.k()])
            tr.op("dve", lambda e: e.tensor_add(out=ft.ap, in0=ft.ap, in1=gath.ap[:, j, 0:512].rearrange("p (h v) -> p h v", h=4)),
                  reads=[ft.k(), gath.k()], writes=[ft.k()])
            tr.op("dve", lambda e: e.tensor_sub(out=ft.ap, in0=ft.ap, in1=Sin[:, :, :]), reads=[ft.k(), Sin_b.k()], writes=[ft.k()])
            tr.op("dve", lambda e: e.scalar_tensor_tensor(out=Sin[:, :, :], in0=ft.ap, scalar=CT("use")[:, j:j + 1], in1=Sin[:, :, :],
                                                          op0=ALU.mult, op1=ALU.add),
                  reads=[ft.k(), Sin_b.k(), CB("use").k()], writes=[Sin_b.k()])
        ar.free(ft, gath)
        if debug:
            tr.dma("sp", "dbg", dbg["sin"], Sin[:, :, :], reads=[Sin_b.k()])

        checkpoint("fold")
        tr.op("dve", lambda e: e.tensor_copy(out=U[:, 1, :, :], in_=Sin[:, :, :]), reads=[Sin_b.k()], writes=[U_b.k(1, h) for h in range(4)])
        tr.op("act", lambda e: e.activation(out=Sbf[:, :, 0, :], in_=Sin[:, :, :], func=AF.Copy), reads=[Sin_b.k()], writes=[Sbf_b.k(0, h) for h in range(4)])
        for i in range(NT + 4):
            if ok(i + 1):
                st_proj(i + 1)
            if ok(i - 1):
                st_b1(i - 1)
            if ok(i - 2):
                st_b2(i - 2)
            if ok(i):
                st_scores(i)
            if ok(i + 1):
                st_gate_act(i + 1)
            if ok(i):
                st_state(i)
            if ok(i + 1):
                st_gate_dve(i + 1)
            if ok(i):
                st_o(i)
            if ok(i - 3):
                st_b3(i - 3)
            if ok(i - 4):
                st_b4(i - 4)
        for n in p2:
            ar.free(*p2[n])
        ar.free(*scb, *ybt, junk2, KdT, QdT, KdTM, V, U_b, Sbf_b)

        if debug:
            tr.dma("sp", "dbg", dbg["yb"], ybT.ap, reads=[ybT.k(i) for i in range(NT)])

        checkpoint("hgrn")
        allpool = PsumPool([0, 1, 2, 3, 4, 5])
        ar.free(wbf[0], wbf[3])
        wbf[SL_ZA] = ar.alloc("wbf2b", [128, 8, 512], BF16)
        load_win_group(SL_ZA, C_ZA)
        for _ in setup_gen:
            pass
        Bp = ar.alloc("Bp", [128, 2, 16, 256], BF16)
        tabs = s5h["tabs"]
        Toep = ar.alloc("Toep", [128, 32, 256], BF16)
        Cp = ar.alloc("Cp", [128, 2, 16, 256], BF16)
        tr.dma("sp", "s5ld", Bp.ap, bp_dr.ap().rearrange("p j a r n -> p j a (r n)"), reads=[("bp_dr",)], writes=[Bp.k()])
        tr.dma("sp", "s5ld", Toep.ap, toep_dr.ap(), reads=[("toep_dr",)], writes=[Toep.k()])
        tr.dma("sp", "s5ld", Cp.ap, cp_dr.ap(), reads=[("cp_dr",)], writes=[Cp.k()])
        for b_ in (Bp, Toep, Cp):
            tr.res[b_.k()] = {"w": ("s5ld", tr.cnt["s5ld"]), "r": dict(b_.fence)}
        ucm = ar.alloc("ucm", [128, 32, 16, 16], BF16)
        for s_ in range(16):
            bi = allpool.next()
            for kc in range(8):
                tr.op("pe", lambda e: e.matmul(ps_f[bi][:, :], lhsT=xnT.ap[:, kc, s_:TT:16], rhs=wbf[SL_U].ap[:, kc, :],
                                               start=(kc == 0), stop=(kc == 7)),
                      reads=[wbf[SL_U].k()] + [xnT.k(i) for i in range(NT)], writes=[psk_f[bi]])
            if s_ % 2 == 0:
                tr.op("act", lambda e: e.activation(out=ucm.ap[:, :, s_, :], in_=ps_f[bi][:, :].rearrange("p (g q) -> p g q", g=32), func=AF.Copy),
                      reads=[psk_f[bi]], writes=[ucm.k(s_)])
            else:
                tr.op("dve", lambda e: e.tensor_copy(out=ucm.ap[:, :, s_, :], in_=ps_f[bi][:, :].rearrange("p (g q) -> p g q", g=32)),
                      reads=[psk_f[bi]], writes=[ucm.k(s_)])
        ar.free(wbf[1])
        Ub = ar.alloc("U", [128, 2, 32, 128], BF16)
        n_ev = 0
        for j in range(2):
            for gb in range(4):
                pb_i = trpool.next()
                pt = ps_b[pb_i]
                for g in range(8 * gb, 8 * gb + 8):
                    tr.op("pe", lambda e: e.transpose(out=pt[:, (g % 8) * 128:(g % 8 + 1) * 128], in_=ucm.ap[:, g, 8 * j:8 * j + 8, :].rearrange("p s q -> p (s q)"),
                                                      identity=CT("identb")[:]),
                          reads=[ucm.k(s_) for s_ in range(8 * j, 8 * j + 8)] + [CB("identb").k()], writes=[psk_b[pb_i]])
                if n_ev % 2 == 0:
                    tr.op("dve", lambda e: e.tensor_copy(out=Ub.ap[:, j, 8 * gb:8 * gb + 8, :], in_=pt[:, :].rearrange("p (a b) -> p a b", a=8)),
                          reads=[psk_b[pb_i]], writes=[Ub.k(j, gb)])
                else:
                    tr.op("act", lambda e: e.activation(out=Ub.ap[:, j, 8 * gb:8 * gb + 8, :], in_=pt[:, :].rearrange("p (a b) -> p a b", a=8), func=AF.Copy),
                          reads=[psk_b[pb_i]], writes=[Ub.k(j, gb)])
                n_ev += 1
        ar.free(ucm)
        Gre = ar.alloc("Gre", [128, 16, 128], F32)
        Gim = ar.alloc("Gim", [128, 16, 128], F32)
        rt = [ar.alloc(f"rt{j}", [128, 512], F32) for j in range(4)]
        for q in range(4):
            bre_i, bim_i = allpool.next(), allpool.next()
            for pr in range(4 * q, 4 * q + 4):
                for g2 in range(2):
                    g = 2 * pr + g2
                    for ri, bi in ((0, bre_i), (1, bim_i)):
                        for j in range(2):
                            tr.op("pe", lambda e: e.matmul(ps_f[bi][g2 * 64:(g2 + 1) * 64, (pr % 4) * 128:(pr % 4 + 1) * 128],
                                                           lhsT=Bp.ap[:, j, pr, ri * 128 + g2 * 64:ri * 128 + (g2 + 1) * 64], rhs=Ub.ap[:, j, g, :],
                                                           start=(j == 0), stop=(j == 1)),
                                  reads=[Bp.k(), Ub.k(j, g // 8)], writes=[psk_f[bi]])
            cosq = tabs.ap[:, 1, 4 * q:4 * q + 4, :].rearrange("p a b -> p (a b)")
            sinq = tabs.ap[:, 2, 4 * q:4 * q + 4, :].rearrange("p a b -> p (a b)")
            gre_q = Gre.ap[:, 4 * q:4 * q + 4, :].rearrange("p a b -> p (a b)")
            gim_q = Gim.ap[:, 4 * q:4 * q + 4, :].rearrange("p a b -> p (a b)")
            tr.op("dve", lambda e: e.tensor_tensor(out=rt[0].ap, in0=ps_f[bre_i][:, :], in1=cosq, op=ALU.mult), reads=[psk_f[bre_i], tabs.k()], writes=[rt[0].k()])
            tr.op("dve", lambda e: e.tensor_tensor(out=rt[1].ap, in0=ps_f[bim_i][:, :], in1=sinq, op=ALU.mult), reads=[psk_f[bim_i], tabs.k()], writes=[rt[1].k()])
            tr.op("dve", lambda e: e.tensor_tensor(out=rt[2].ap, in0=ps_f[bim_i][:, :], in1=cosq, op=ALU.mult), reads=[psk_f[bim_i], tabs.k()], writes=[rt[2].k()])
            tr.op("dve", lambda e: e.tensor_tensor(out=rt[3].ap, in0=ps_f[bre_i][:, :], in1=sinq, op=ALU.mult), reads=[psk_f[bre_i], tabs.k()], writes=[rt[3].k()])
            tr.op("pool", lambda e: e.tensor_tensor(out=gre_q, in0=rt[0].ap, in1=rt[1].ap, op=ALU.add), reads=[rt[0].k(), rt[1].k()], writes=[Gre.k(q)])
            tr.op("pool", lambda e: e.tensor_tensor(out=gim_q, in0=rt[2].ap, in1=rt[3].ap, op=ALU.subtract), reads=[rt[2].k(), rt[3].k()], writes=[Gim.k(q)])
        ar.free(Bp)
        for pr in range(16):
            for Gb in (Gre, Gim):
                tr.op("dve", lambda e: e.tensor_tensor_scan(out=Gb.ap[:, pr, :], data0=rho16[:, pr:pr + 1].to_broadcast([128, 128]), data1=Gb.ap[:, pr, :],
                                                            initial=0.0, op0=ALU.mult, op1=ALU.add),
                      reads=[Gb.k(pr // 4), rho16_b.k()], writes=[Gb.k(pr // 4)])
        pk2_b, pk2 = small("pk2", [128, 2, 16])
        e1_b, e1 = small("e1", [128, 16])
        e2_b, e2 = small("e2", [128, 16])
        gk = [Gre.k(q) for q in range(4)] + [Gim.k(q) for q in range(4)]
        cos127, sin127 = tabs.ap[:, 1, :, 127], tabs.ap[:, 2, :, 127]
        gre127, gim127 = Gre.ap[:, :, 127], Gim.ap[:, :, 127]
        tr.op("dve", lambda e: e.tensor_tensor(out=e1[:, :], in0=cos127, in1=gre127, op=ALU.mult), reads=gk + [tabs.k()], writes=[e1_b.k()])
        tr.op("dve", lambda e: e.tensor_tensor(out=e2[:, :], in0=sin127, in1=gim127, op=ALU.mult), reads=gk + [tabs.k()], writes=[e2_b.k()])
        tr.op("dve", lambda e: e.tensor_tensor(out=pk2[:, 0, :], in0=e1[:, :], in1=e2[:, :], op=ALU.subtract), reads=[e1_b.k(), e2_b.k()], writes=[pk2_b.k(0)])
        tr.op("dve", lambda e: e.tensor_tensor(out=e1[:, :], in0=cos127, in1=gim127, op=ALU.mult), reads=gk + [tabs.k()], writes=[e1_b.k()])
        tr.op("dve", lambda e: e.tensor_tensor(out=e2[:, :], in0=sin127, in1=gre127, op=ALU.mult), reads=gk + [tabs.k()], writes=[e2_b.k()])
        tr.op("dve", lambda e: e.tensor_tensor(out=pk2[:, 1, :], in0=e1[:, :], in1=e2[:, :], op=ALU.add), reads=[e1_b.k(), e2_b.k()], writes=[pk2_b.k(1)])
        tr.dma("pool", "ags", agi_s[:, :], pk2[:, :, :].rearrange("p a b -> p (a b)"), reads=[pk2_b.k(0), pk2_b.k(1)], writes=[("agi_s",)])
        tr.collective("cc", lambda e: e.collective_compute("AllGather", ALU.bypass, replica_groups=[[0, 1, 2, 3], [4, 5, 6, 7]],
                                                           ins=[agi_s.ap().opt()], outs=[ago_s.ap().opt()]),
                      reads=[("agi_s",)], writes=[("ago_s",)])
        g2_b, g2t = small("gath2", [128, 4, 2, 16])
        tr.dma("sp", "ags2", g2t[:, :, :, :].rearrange("p j a b -> p j (a b)"), ago_s[:, :].rearrange("(j p) f -> p j f", p=128),
               reads=[("ago_s",)], writes=[g2_b.k()])
        hin_b, hin = small("hin", [128, 2, 16])
        nw_b, nwt = small("hnew", [128, 2, 16])
        tr.op("dve", lambda e: e.memset(hin[:, :, :], 0.0), writes=[hin_b.k()])
        for j in range(3):
            R_ = [hin_b.k(), a2k_b.k(), g2_b.k(), e1_b.k(), e2_b.k(), nw_b.k()]
            tr.op("dve", lambda e: e.tensor_tensor(out=e1[:, :], in0=a2k[:, 0, :], in1=hin[:, 0, :], op=ALU.mult), reads=R_, writes=[e1_b.k()])
            tr.op("dve", lambda e: e.tensor_tensor(out=e2[:, :], in0=a2k[:, 1, :], in1=hin[:, 1, :], op=ALU.mult), reads=R_, writes=[e2_b.k()])
            tr.op("dve", lambda e: e.tensor_tensor(out=nwt[:, 0, :], in0=e1[:, :], in1=e2[:, :], op=ALU.subtract), reads=R_, writes=[nw_b.k()])
            tr.op("dve", lambda e: e.tensor_tensor(out=e1[:, :], in0=a2k[:, 0, :], in1=hin[:, 1, :], op=ALU.mult), reads=R_, writes=[e1_b.k()])
            tr.op("dve", lambda e: e.tensor_tensor(out=e2[:, :], in0=a2k[:, 1, :], in1=hin[:, 0, :], op=ALU.mult), reads=R_, writes=[e2_b.k()])
            tr.op("dve", lambda e: e.tensor_tensor(out=nwt[:, 1, :], in0=e1[:, :], in1=e2[:, :], op=ALU.add), reads=R_, writes=[nw_b.k()])
            tr.op("dve", lambda e: e.tensor_tensor(out=nwt[:, :, :], in0=nwt[:, :, :], in1=g2t[:, j, :, :], op=ALU.add), reads=R_, writes=[nw_b.k()])
            tr.op("dve", lambda e: e.tensor_tensor(out=nwt[:, :, :], in0=nwt[:, :, :], in1=hin[:, :, :], op=ALU.subtract), reads=R_, writes=[nw_b.k()])
            tr.op("dve", lambda e: e.scalar_tensor_tensor(out=hin[:, :, :], in0=nwt[:, :, :], scalar=CT("use")[:, j:j + 1], in1=hin[:, :, :],
                                                          op0=ALU.mult, op1=ALU.add), reads=R_ + [CB("use").k()], writes=[hin_b.k()])
        if debug:
            tr.dma("sp", "dbg", dbg["hin"], hin[:, :, :], reads=[hin_b.k()])
        checkpoint("s5local")
        HreB = ar.alloc("HreB", [128, 16, 130], BF16)
        HimB = ar.alloc("HimB", [128, 16, 130], BF16)
        rpow = tabs.ap[:, 0]
        cosT, sinT = tabs.ap[:, 1], tabs.ap[:, 2]
        big = [ar.alloc(f"big{j}", [128, 8, 128], F32) for j in range(2)]
        for hf in range(2):
            ps_ = slice(8 * hf, 8 * hf + 8)
            qk = [2 * hf, 2 * hf + 1]
            for ri, Gb in ((0, Gre), (1, Gim)):
                tr.op("dve", lambda e: e.tensor_tensor(out=big[0].ap, in0=rpow[:, ps_, :], in1=hin[:, ri, ps_].unsqueeze(2).to_broadcast([128, 8, 128]), op=ALU.mult),
                      reads=[tabs.k(), hin_b.k()], writes=[big[0].k()])
                tr.op("dve", lambda e: e.tensor_tensor(out=Gb.ap[:, ps_, :], in0=Gb.ap[:, ps_, :], in1=big[0].ap, op=ALU.add),
                      reads=[big[0].k()] + [Gb.k(q) for q in qk], writes=[Gb.k(q) for q in qk])
            gkh = [Gre.k(q) for q in qk] + [Gim.k(q) for q in qk]
            tr.op("dve", lambda e: e.tensor_tensor(out=big[0].ap, in0=cosT[:, ps_, :], in1=Gre.ap[:, ps_, :], op=ALU.mult), reads=gkh + [tabs.k()], writes=[big[0].k()])
            tr.op("dve", lambda e: e.tensor_tensor(out=big[1].ap, in0=sinT[:, ps_, :], in1=Gim.ap[:, ps_, :], op=ALU.mult), reads=gkh + [tabs.k()], writes=[big[1].k()])
            tr.op("dve", lambda e: e.tensor_tensor(out=HreB.ap[:, ps_, 1:129], in0=big[0].ap, in1=big[1].ap, op=ALU.subtract),
                  reads=[big[0].k(), big[1].k()], writes=[HreB.k()])
            tr.op("dve", lambda e: e.tensor_tensor(out=big[0].ap, in0=cosT[:, ps_, :], in1=Gim.ap[:, ps_, :], op=ALU.mult), reads=gkh + [tabs.k()], writes=[big[0].k()])
            tr.op("dve", lambda e: e.tensor_tensor(out=big[1].ap, in0=sinT[:, ps_, :], in1=Gre.ap[:, ps_, :], op=ALU.mult), reads=gkh + [tabs.k()], writes=[big[1].k()])
            tr.op("dve", lambda e: e.tensor_tensor(out=HimB.ap[:, ps_, 1:129], in0=big[0].ap, in1=big[1].ap, op=ALU.add),
                  reads=[big[0].k(), big[1].k()], writes=[HimB.k()])
        tr.op("act", lambda e: e.activation(out=HreB.ap[:, :, 0], in_=hin[:, 0, :], func=AF.Copy), reads=[hin_b.k(), HreB.k()], writes=[HreB.k()])
        tr.op("act", lambda e: e.activation(out=HimB.ap[:, :, 0], in_=hin[:, 1, :], func=AF.Copy), reads=[hin_b.k(), HimB.k()], writes=[HimB.k()])
        ar.free(*big, *rt, Gre, Gim, tabs)
        ycm = ar.alloc("ycm", [128, 16, 32, 16], BF16)
        for pr in range(16):
            bi = allpool.next()
            for g2 in range(2):
                g = 2 * pr + g2
                rows = slice(g2 * 64, g2 * 64 + 64)
                o_ = ps_f[bi][:, g2 * 256:(g2 + 1) * 256]
                tr.op("pe", lambda e: e.matmul(o_, lhsT=Ub.ap[:, 0, g, :], rhs=Toep.ap[:, g, :], start=True, stop=False),
                      reads=[Ub.k(0, g // 8), Toep.k()], writes=[psk_f[bi]])
                tr.op("pe", lambda e: e.matmul(ps_f[bi][:, g2 * 256 + 128:(g2 + 1) * 256], lhsT=Ub.ap[:, 1, g, :], rhs=Toep.ap[:, g, 0:128],
                                               start=False, stop=False),
                      reads=[Ub.k(1, g // 8), Toep.k()], writes=[psk_f[bi]])
                tr.op("pe", lambda e: e.matmul(o_, lhsT=HreB.ap[rows, pr, 0:128], rhs=Cp.ap[rows, 0, pr, :], start=False, stop=False),
                      reads=[HreB.k(), Cp.k()], writes=[psk_f[bi]])
                tr.op("pe", lambda e: e.matmul(o_, lhsT=HimB.ap[rows, pr, 0:128], rhs=Cp.ap[rows, 1, pr, :], start=False, stop=True),
                      reads=[HimB.k(), Cp.k()], writes=[psk_f[bi]])
            tr.op("act", lambda e: e.activation(out=ycm.ap[:, :, 2 * pr:2 * pr + 2, :], in_=ps_f[bi][:, :].rearrange("p (g s q) -> p s g q", g=2, s=16),
                                                func=AF.Gelu_apprx_tanh),
                  reads=[psk_f[bi]], writes=[ycm.k(pr)])
        ar.free(Toep, Cp, Ub, HreB, HimB)
        yFM = ar.alloc("yFM", [128, 4, TT], BF16)
        n_ev = 0
        for q in range(4):
            for sb_ in range(2):
                pb_i = trpool.next()
                pt = ps_b[pb_i]
                for s8 in range(8):
                    s_ = 8 * sb_ + s8
                    tr.op("pe", lambda e: e.transpose(out=pt[:, s8 * 128:(s8 + 1) * 128], in_=ycm.ap[:, s_, 8 * q:8 * q + 8, :].rearrange("p g q -> p (g q)"),
                                                      identity=CT("identb")[:]),
                          reads=[ycm.k(pr) for pr in range(4 * q, 4 * q + 4)] + [CB("identb").k()], writes=[psk_b[pb_i]])
                dst = yFM.ap[:, q, :].rearrange("p (c s) -> p s c", s=16)[:, 8 * sb_:8 * sb_ + 8, :]
                src = pt[:, :].rearrange("p (a b) -> p a b", a=8)
                if n_ev % 2 == 0:
                    tr.op("dve", lambda e: e.tensor_copy(out=dst, in_=src), reads=[psk_b[pb_i]], writes=[yFM.k(q, sb_)])
                else:
                    tr.op("act", lambda e: e.activation(out=dst, in_=src, func=AF.Copy), reads=[psk_b[pb_i]], writes=[yFM.k(q, sb_)])
                n_ev += 1
        ar.free(ycm)
        if debug:
            tr.dma("sp", "dbg", dbg["yfm"], yFM.ap, reads=[yFM.k(q, s) for q in range(4) for s in range(2)])
        yaFM = ar.alloc("yaFM", [128, 4, TT], BF16)
        wglu = ar.alloc("wglu", [128, 4, 512], BF16)
        load_weight_cols(wglu, lambda c0: wglu.ap[:, :, c0:c0 + 256], wglu_d, 4, 0, 512)
        gt = {n: [ar.alloc(f"g_{n}{j}", [128, 512], F32) for j in range(2)] for n in ["sg", "sz"]}
        wga = ar.alloc("wga", [128, 8, 1024], BF16)
        wgb = ar.alloc("wgb", [128, 8, 1024], BF16)
        wpa = ar.alloc("wpa", [128, 4, 1024], BF16)
        wpb = ar.alloc("wpb", [128, 4, 1024], BF16)
        wout = ar.alloc("wout", [128, 8, 1024], BF16)
        load_weight_cols(wga, lambda c0: wga.ap[:, :, c0:c0 + 256], w_in_d, 8, C_GA, 1024)
        load_weight_cols(wgb, lambda c0: wgb.ap[:, :, c0:c0 + 256], w_in_d, 8, C_GB, 1024)
        load_weight_cols(wpa, lambda c0: wpa.ap[:, :, c0:c0 + 256], wpa_d, 4, 0, 1024)
        wpb32 = ar.alloc("wpb32", [128, 4, 1024], F32)
        tr.dma("sp", "wpb32", wpb32.ap, wpb_d.rearrange("(kc p) c -> p kc c", p=128), writes=[wpb32.k()])
        tr.op("pool", lambda e: e.tensor_tensor(out=wpb.ap, in0=wpb32.ap, in1=CT("hnw")[:, 0:4].unsqueeze(2).to_broadcast([128, 4, 1024]), op=ALU.mult),
              reads=[wpb32.k(), CB("hnw").k()], writes=[wpb.k()])
        ar.free(wpb32)
        load_weight_cols(wout, lambda c0: wout.ap[:, :, c0:c0 + 256], wout_d, 8, 0, 1024)
        nbg_b, nbg = small("nbglu", [128, 4])
        tr.op("dve", lambda e: e.tensor_scalar(out=nbg[:, :], in0=CT("bglu")[:, :], scalar1=-1.0, scalar2=None, op0=ALU.mult),
              reads=[CB("bglu").k()], writes=[nbg_b.k()])
        yk = [yFM.k(q, s) for q in range(4) for s in range(2)]
        it = 0
        for ct in range(4):
            for tb in range(NB):
                j = it % 2
                it += 1
                bg = allpool.next()
                for kc in range(4):
                    tr.op("pe", lambda e: e.matmul(ps_f[bg][:, :], lhsT=wglu.ap[:, kc, ct * 128:(ct + 1) * 128], rhs=yFM.ap[:, kc, tb * 512:(tb + 1) * 512],
                                                   start=(kc == 0), stop=(kc == 3)),
                          reads=[wglu.k()] + yk, writes=[psk_f[bg]])
                bz = allpool.next()
                for kc in range(8):
                    tr.op("pe", lambda e: e.matmul(ps_f[bz][:, :], lhsT=wbf[SL_ZA].ap[:, kc, ct * 128:(ct + 1) * 128],
                                                   rhs=xnT.ap[:, kc, tb * 512:(tb + 1) * 512], start=(kc == 0), stop=(kc == 7)),
                          reads=[wbf[SL_ZA].k()] + [xnT.k(4 * tb + jj) for jj in range(4)], writes=[psk_f[bz]])
                G = {n: gt[n][j] for n in gt}
                sigmoid3(G["sg"].ap, [G["sg"].k()], ps_f[bg][:, :], [psk_f[bg]], None, nbias=nbg[:, ct:ct + 1], nbias_keys=[nbg_b.k()])
                sigmoid3(G["sz"].ap, [G["sz"].k()], ps_f[bz][:, :], [psk_f[bz]], None)
                tr.op("dve", lambda e: e.tensor_tensor(out=G["sz"].ap, in0=ps_f[bz][:, :], in1=G["sz"].ap, op=ALU.mult),
                      reads=[psk_f[bz], G["sz"].k()], writes=[G["sz"].k()])
                tr.op("dve", lambda e: e.tensor_tensor(out=G["sg"].ap, in0=yFM.ap[:, ct, tb * 512:(tb + 1) * 512], in1=G["sg"].ap, op=ALU.mult),
                      reads=yk + [G["sg"].k()], writes=[G["sg"].k()])
                tr.op("dve", lambda e: e.tensor_tensor(out=yaFM.ap[:, ct, tb * 512:(tb + 1) * 512], in0=G["sg"].ap, in1=G["sz"].ap, op=ALU.mult),
                      reads=[G["sg"].k(), G["sz"].k()], writes=[yaFM.k(ct, tb)])
        for n in gt:
            ar.free(*gt[n])
        ar.free(yFM, wglu)
        if debug:
            tr.dma("sp", "dbg", dbg["ya"], yaFM.ap, reads=[yaFM.k(c, t) for c in range(4) for t in range(NB)])
        checkpoint("s5")

        ar.free(wbf[SL_ZA])
        fnw = ar.alloc("fnw", [128, D], F32)
        tr.dma("sp", "fnw", fnw.ap, fnw_d, writes=[fnw.k()])
        mg = [ar.alloc(f"mg{j}", [128, 8, 512], BF16) for j in range(2)]
        mt = {n: [ar.alloc(f"m_{n}{j}", [128, 512], F32) for j in range(2)] for n in ["sa", "sb"]}
        xr = [ar.alloc(f"xr{j}", [128, D], F32) for j in range(2)]
        hb = [ar.alloc(f"hb{j}", [128, D], F32) for j in range(2)]
        junk3 = ar.alloc("junk3", [128, 512], BF16)
        ss2_b, ss2 = small("ss2", [128, NT, 2])
        r2_b, r2 = small("r2", [128, NT])
        yak = lambda tb: [yaFM.k(c, tb) for c in range(4)]
        ybk = lambda tb: [ybT.k(4 * tb + jj) for jj in range(4)]
        xk = lambda tb: [xnT.k(4 * tb + jj) for jj in range(4)]
        it = 0
        for tb in range(NB):
            M = mg[tb % 2]
            for dt_ in range(8):
                j = it % 2
                it += 1
                T_ = {n: mt[n][j] for n in mt}
                cs = slice(dt_ * 128, (dt_ + 1) * 128)
                ts_ = slice(tb * 512, (tb + 1) * 512)
                bpa, bpb_, bga, bgb = allpool.next(), allpool.next(), allpool.next(), allpool.next()
                for kc in range(8):
                    tr.op("pe", lambda e: e.matmul(ps_f[bga][:, :], lhsT=wga.ap[:, kc, cs], rhs=xnT.ap[:, kc, ts_], start=(kc == 0), stop=(kc == 7)),
                          reads=[wga.k()] + xk(tb), writes=[psk_f[bga]])
                for kc in range(8):
                    tr.op("pe", lambda e: e.matmul(ps_f[bgb][:, :], lhsT=wgb.ap[:, kc, cs], rhs=xnT.ap[:, kc, ts_], start=(kc == 0), stop=(kc == 7)),
                          reads=[wgb.k()] + xk(tb), writes=[psk_f[bgb]])
                for kc in range(4):
                    tr.op("pe", lambda e: e.matmul(ps_f[bpa][:, :], lhsT=wpa.ap[:, kc, cs], rhs=yaFM.ap[:, kc, ts_], start=(kc == 0), stop=(kc == 3)),
                          reads=[wpa.k()] + yak(tb), writes=[psk_f[bpa]])
                for kc in range(4):
                    tr.op("pe", lambda e: e.matmul(ps_f[bpb_][:, :], lhsT=wpb.ap[:, kc, cs], rhs=ybT.ap[:, kc, ts_], start=(kc == 0), stop=(kc == 3)),
                          reads=[wpb.k()] + ybk(tb), writes=[psk_f[bpb_]])
                sigmoid3(T_["sa"].ap, [T_["sa"].k()], ps_f[bga][:, :], [psk_f[bga]], None)
                sigmoid3(T_["sb"].ap, [T_["sb"].k()], ps_f[bgb][:, :], [psk_f[bgb]], None)
                tr.op("dve", lambda e: e.tensor_tensor(out=T_["sa"].ap, in0=ps_f[bpa][:, :], in1=T_["sa"].ap, op=ALU.mult),
                      reads=[psk_f[bpa], T_["sa"].k()], writes=[T_["sa"].k()])
                tr.op("dve", lambda e: e.tensor_tensor(out=T_["sb"].ap, in0=ps_f[bpb_][:, :], in1=T_["sb"].ap, op=ALU.mult),
                      reads=[psk_f[bpb_], T_["sb"].k()], writes=[T_["sb"].k()])
                tr.op("pool", lambda e: e.tensor_tensor(out=M.ap[:, dt_, :], in0=T_["sa"].ap, in1=T_["sb"].ap, op=ALU.add),
                      reads=[T_["sa"].k(), T_["sb"].k()], writes=[M.k(dt_)])
            for il in range(4):
                i = 4 * tb + il
                X_, H_ = xr[i % 2], hb[i % 2]
                tr.dma("pool", f"xr{i % 2}", X_.ap, x_d[i * 128:(i + 1) * 128, :], writes=[X_.k()])
                for half in range(2):
                    bo = allpool.next()
                    hs = slice(half * 512, (half + 1) * 512)
                    for kc in range(8):
                        tr.op("pe", lambda e: e.matmul(ps_f[bo][:, :], lhsT=M.ap[:, kc, il * 128:(il + 1) * 128], rhs=wout.ap[:, kc, hs],
                                                       start=(kc == 0), stop=(kc == 7)),
                              reads=[M.k(kc), wout.k()], writes=[psk_f[bo]])
                    tr.op("dve", lambda e: e.tensor_tensor(out=H_.ap[:, hs], in0=ps_f[bo][:, :], in1=X_.ap[:, hs], op=ALU.add),
                          reads=[psk_f[bo], X_.k()], writes=[H_.k(half)])
                    tr.op("act", lambda e: e.activation(out=junk3.ap, in_=H_.ap[:, hs], func=AF.Square, accum_out=ss2[:, i, half:half + 1]),
                          reads=[H_.k(half)], writes=[junk3.k(), ss2_b.k(i, half)])
                rk = [ss2_b.k(i, 0), ss2_b.k(i, 1)]
                tr.op("dve", lambda e: e.tensor_tensor(out=r2[:, i:i + 1], in0=ss2[:, i, 0:1], in1=ss2[:, i, 1:2], op=ALU.add), reads=rk, writes=[r2_b.k(i)])
                rstd_act(r2[:, i:i + 1], [r2_b.k(i)], 1.0 / D)
                tr.op("dve", lambda e: e.scalar_tensor_tensor(out=H_.ap, in0=H_.ap, scalar=r2[:, i:i + 1], in1=fnw.ap, op0=ALU.mult, op1=ALU.mult),
                      reads=[H_.k(0), H_.k(1), r2_b.k(i), fnw.k()], writes=[H_.k(0), H_.k(1)])
                tr.dma("sp", f"ob{i % 2}", out_d[i * 128:(i + 1) * 128, :], H_.ap, reads=[H_.k(0), H_.k(1)])


    try:
        rest()
    except _Stop:
        pass
    tr.final_wait("sp")
    print("instr counts", tr.ninstr)
    return nc


def make_inputs(inputs):
    f32 = np.float32
    x = np.asarray(inputs["x"], f32)
    per_core = []
    common = {
        "w_in": np.ascontiguousarray(inputs["w_in"][0], f32),
        "nw": np.ascontiguousarray(np.asarray(inputs["norm_w"][0], f32).reshape(8, 128).T),
        "lbl": np.ascontiguousarray(np.asarray(inputs["hgrn_lb_logits"], f32).reshape(2, 4, 128).transpose(2, 0, 1)),
        "hnw": np.ascontiguousarray(np.asarray(inputs["hgrn_norm_w"][0], f32).reshape(4, 128).T),
        "identb": np.eye(128, dtype=f32).astype(ml_dtypes.bfloat16),
        "w_proj_b": np.ascontiguousarray(inputs["w_proj_b"][0], f32),
        "w_proj_a": np.ascontiguousarray(inputs["w_proj_a"][0], f32),
        "w_out": np.ascontiguousarray(inputs["w_out"][0], f32),
        "w_glu": np.ascontiguousarray(inputs["ssm_w_glu"][0], f32),
        "bglu": np.ascontiguousarray(np.asarray(inputs["ssm_b_glu"][0], f32).reshape(4, 128).T),
        "fnw": np.ascontiguousarray(np.broadcast_to(np.asarray(inputs["final_norm_w"], f32).reshape(1, D), (128, D))),
    }
    sn = lambda a: np.ascontiguousarray(np.asarray(a, f32).reshape(16, 2, 64).transpose(1, 2, 0).reshape(128, 16))
    common["lamre"] = sn(inputs["ssm_lambda_re"][0])
    common["lamim"] = sn(inputs["ssm_lambda_im"][0])
    ld = np.asarray(inputs["ssm_log_dt"][0], f32).reshape(16, 2).T
    common["logdt"] = np.ascontiguousarray(np.repeat(ld[:, None, :], 64, axis=1).reshape(128, 16))
    bsn = lambda a: np.ascontiguousarray(np.asarray(a, f32).reshape(16, 2, 64, 16).transpose(1, 2, 0, 3).reshape(128, 16, 16))
    common["bre"] = bsn(inputs["ssm_b_re"][0])
    common["bim"] = bsn(inputs["ssm_b_im"][0])
    csn = lambda a: np.ascontiguousarray(np.asarray(a, f32).reshape(16, 2, 16, 64).transpose(1, 3, 0, 2).reshape(128, 16, 16))
    common["cre"] = csn(inputs["ssm_c_re"][0])
    common["cim"] = csn(inputs["ssm_c_im"][0])
    common["dbc"] = np.ascontiguousarray(np.broadcast_to(np.asarray(inputs["ssm_d"][0], f32)[None], (128, 32, 16)))
    kvv = np.concatenate([-np.arange(1, 9), np.arange(0, 17), np.arange(15, -1, -1)]).astype(f32)
    common["kv"] = np.ascontiguousarray(np.broadcast_to(kvv[None], (128, 41)))
    common["cidx"] = np.ascontiguousarray(np.broadcast_to(np.arange(1, 129, dtype=f32)[None], (128, 128)))
    common["identf"] = np.eye(128, dtype=f32)
    rr_ = np.arange(128)[:, None] // 16
    cc_ = np.arange(256)[None, :] // 16
    common["maskT"] = (cc_ >= rr_).astype(f32)
    s = np.arange(128)[:, None]
    t = np.arange(128)[None, :]
    common["maskbc"] = ((s // 64 == t // 64) & (t >= s)).astype(f32)
    m = np.ones((128, 512), f32)
    m[:, 0::64] = 0.0
    common["mask512"] = m
    for r in range(8):
        b, k = r // 4, r % 4
        d = dict(common)
        d["x"] = np.ascontiguousarray(x[b, k * TT:(k + 1) * TT, :])
        use = np.zeros((128, 4), f32)
        use[:, :k] = 1.0
        d["use"] = use
        per_core.append(d)
    return per_core


def kernel(**inputs):
    nc = build_nc()
    in_maps = make_inputs(inputs)
    res = run_bass_kernel_spmd(nc, in_maps, core_ids=list(range(8)))
    out = np.zeros((2, 8192, D), np.float32)
    for r in range(8):
        b, k = r // 4, r % 4
        out[b, k * TT:(k + 1) * TT, :] = res.results[r]["out"]
    return out
```

```python
import math
import numpy as np
import ml_dtypes
import concourse.bass as bass
import concourse.mybir as mybir
from concourse.bass_utils import run_bass_kernel_spmd

F32 = mybir.dt.float32
BF16 = mybir.dt.bfloat16
I32 = mybir.dt.int32
AF = mybir.ActivationFunctionType
ALU = mybir.AluOpType
AX = mybir.AxisListType

TT = 2048
D = 1024
NT = TT // 128
NB = TT // 512
EPS = 1e-6
TWO_PI = 2.0 * math.pi

C_U, C_ZA, C_Q, C_F, C_I, C_OG, C_ZB, C_GA, C_GB = 0, 512, 1024, 1536, 2048, 2560, 3072, 3584, 4608


class Buf:
    def __init__(self, name, fence):
        self.name = name
        self.fence = fence

    def k(self, *idx):
        return (self, idx)


class Tracker:
    def __init__(self, nc):
        self.nc = nc
        self.eng = {"pe": nc.tensor, "act": nc.scalar, "dve": nc.vector, "pool": nc.gpsimd, "sp": nc.sync}
        self.semh = {}
        self.cnt = {}
        for e in self.eng:
            self.semh[e] = nc.semaphore("s_" + e).__enter__()
            self.cnt[e] = 0
        self.waited = {e: {} for e in self.eng}
        self.res = {}
        self.same_engine_sync = True
        self.ninstr = {e: 0 for e in self.eng}
        self.pending = []
        self.defer = None

    def dma_sem(self, name):
        if name not in self.semh:
            self.semh[name] = self.nc.semaphore("d_" + name).__enter__()
            self.cnt[name] = 0
        return name

    def _state(self, key):
        st = self.res.get(key)
        if st is None:
            fence = key[0].fence if isinstance(key[0], Buf) else {}
            st = {"w": None, "r": dict(fence)}
            self.res[key] = st
        return st

    def _collect(self, reads, writes):
        evs = {}

        def add(sk, v):
            if evs.get(sk, 0) < v:
                evs[sk] = v

        for k in reads:
            st = self._state(k)
            if st["w"]:
                add(*st["w"])
        for k in writes:
            st = self._state(k)
            if st["w"]:
                add(*st["w"])
            for sk, v in st["r"].items():
                add(sk, v)
        return evs

    def _wait(self, eng, evs):
        for sk, v in evs.items():
            if sk == eng and (eng == "pe" or eng == "sp" or not self.same_engine_sync):
                continue
            if self.waited[eng].get(sk, 0) < v:
                self.eng[eng].wait_ge(self.semh[sk], v)
                self.waited[eng][sk] = v

    def _record(self, ev, reads, writes):
        for k in reads:
            st = self._state(k)
            if st["r"].get(ev[0], 0) < ev[1]:
                st["r"][ev[0]] = ev[1]
        for k in writes:
            self.res[k] = {"w": ev, "r": {}}

    def stop_defer(self):
        self.pending = self.defer
        self.defer = None

    def replay(self, n=None):
        q = self.pending
        assert self.defer is None
        k = len(q) if n is None else min(n, len(q))
        for ent in q[:k]:
            if ent[0] == "op":
                self.op(*ent[1:])
            else:
                ent[1].free(*ent[2])
        self.pending = q[k:]
        return len(self.pending)

    def op(self, eng, fn, reads=(), writes=()):
        if self.defer is not None:
            self.defer.append(("op", eng, fn, list(reads), list(writes)))
            return
        self._wait(eng, self._collect(reads, writes))
        ins = fn(self.eng[eng])
        self.cnt[eng] += 1
        self.ninstr[eng] += 1
        ins.then_inc(self.semh[eng], 1)
        self._record((eng, self.cnt[eng]), reads, writes)

    def dma(self, queue, semname, out, in_, reads=(), writes=(), **kw):
        self.dma_sem(semname)
        self._wait(queue, self._collect(reads, writes))
        ins = self.eng[queue].dma_start(out=out, in_=in_, **kw)
        self.cnt[semname] += 16
        ins.then_inc(self.semh[semname], 16)
        self._record((semname, self.cnt[semname]), reads, writes)

    def collective(self, semname, fn, reads=(), writes=()):
        self.dma_sem(semname)
        self._wait("pool", self._collect(reads, writes))
        ins = fn(self.eng["pool"])
        self.cnt[semname] += 1
        ins.then_inc(self.semh[semname], 1)
        self._record((semname, self.cnt[semname]), reads, writes)

    def retire(self, bufs):
        fence = {}
        for key, st in self.res.items():
            if isinstance(key[0], Buf) and key[0] in bufs:
                if st["w"] and fence.get(st["w"][0], 0) < st["w"][1]:
                    fence[st["w"][0]] = st["w"][1]
                for sk, v in st["r"].items():
                    if fence.get(sk, 0) < v:
                        fence[sk] = v
        for b in bufs:
            for sk, v in b.fence.items():
                if fence.get(sk, 0) < v:
                    fence[sk] = v
        return fence

    def final_wait(self, eng):
        for sk, v in self.cnt.items():
            if v > 0 and sk != eng and self.waited[eng].get(sk, 0) < v:
                self.eng[eng].wait_ge(self.semh[sk], v)
                self.waited[eng][sk] = v


class Arena:
    def __init__(self, nc, tr, nbytes):
        self.tr = tr
        self.n = nbytes
        self.t = nc.sbuf_tensor("arena", [128, nbytes // 2], BF16).__enter__()
        self.live = []
        self.retired = []

    def alloc(self, name, shape, dtype):
        esz = 4 if dtype in (F32, I32) else 2
        nel = int(np.prod(shape[1:]))
        nb = (nel * esz + 63) // 64 * 64
        pos = 0
        for s, e, _ in sorted(self.live, key=lambda z: z[0]):
            if pos + nb <= s:
                break
            pos = max(pos, e)
        assert pos + nb <= self.n, f"arena full allocating {name} {shape}: live={[(b.name, s, e) for s, e, b in self.live]}"
        fence = {}
        keep = []
        for s, e, f in self.retired:
            if s < pos + nb and pos < e:
                for sk, v in f.items():
                    if fence.get(sk, 0) < v:
                        fence[sk] = v
                if not (pos <= s and e <= pos + nb):
                    keep.append((s, e, f))
            else:
                keep.append((s, e, f))
        self.retired = keep
        b = Buf(name, fence)
        self.live.append((pos, pos + nb, b))
        ap = self.t[:, pos // 2: pos // 2 + nel * esz // 2]
        if esz == 4:
            ap = ap.bitcast(dtype)
        elif dtype != BF16:
            ap = ap.bitcast(dtype)
        if len(shape) > 2:
            names = [f"d{i}" for i in range(len(shape) - 1)]
            ap = ap.rearrange(f"p ({' '.join(names)}) -> p {' '.join(names)}", **{n: v for n, v in zip(names[:-1], shape[1:-1])})
        b.ap = ap
        b.shape = shape
        return b

    def free(self, *bufs):
        if self.tr.defer is not None:
            self.tr.defer.append(("free", self, bufs))
            return
        fence = self.tr.retire(set(bufs))
        for b in bufs:
            ent = [z for z in self.live if z[2] is b]
            assert ent, b.name
            self.live.remove(ent[0])
            self.retired.append((ent[0][0], ent[0][1], fence))


class PsumPool:
    def __init__(self, banks):
        self.banks = banks
        self.i = 0

    def next(self):
        b = self.banks[self.i % len(self.banks)]
        self.i += 1
        return b


def build_nc(debug=None):
    nc = bass.Bass("TRN2", target_bir_lowering=False)
    tr = Tracker(nc)
    dbg = {}

    def din(name, shape, dt=F32):
        return nc.dram_tensor(name, list(shape), dt, kind="ExternalInput").ap()

    x_d = din("x", [TT, D])
    w_in_d = din("w_in", [D, 5632])
    nw_d = din("nw", [128, 8])
    lbl_d = din("lbl", [128, 2, 4])
    hnw_d = din("hnw", [128, 4])
    use_d = din("use", [128, 4])
    identb_d = din("identb", [128, 128], BF16)
    maskbc_d = din("maskbc", [128, 128])
    mask512_d = din("mask512", [128, 512])
    wpb_d = din("w_proj_b", [512, D])
    wpa_d = din("w_proj_a", [512, D])
    wout_d = din("w_out", [D, D])
    wglu_d = din("w_glu", [512, 512])
    bglu_d = din("bglu", [128, 4])
    fnw_d = din("fnw", [128, D])
    lamre_d = din("lamre", [128, 16])
    lamim_d = din("lamim", [128, 16])
    logdt_d = din("logdt", [128, 16])
    bre_d = din("bre", [128, 16, 16])
    bim_d = din("bim", [128, 16, 16])
    cre_d = din("cre", [128, 16, 16])
    cim_d = din("cim", [128, 16, 16])
    dbc_d = din("dbc", [128, 32, 16])
    kv_d = din("kv", [128, 41])
    cidx_d = din("cidx", [128, 128])
    identf_d = din("identf", [128, 128])
    maskT_d = din("maskT", [128, 256])
    out_d = nc.dram_tensor("out", [TT, D], F32, kind="ExternalOutput").ap()
    toep_dr = nc.dram_tensor("toep_dr", [128, 32, 256], BF16)
    cp_dr = nc.dram_tensor("cp_dr", [128, 2, 16, 256], BF16)
    bp_dr = nc.dram_tensor("bp_dr", [128, 2, 16, 2, 128], BF16)
    tab_dr = nc.dram_tensor("tab_dr", [128, 3, 16, 128], F32)
    agi_s = nc.dram_tensor("agi_s", [128, 32], F32)
    ago_s = nc.dram_tensor("ago_s", [4 * 128, 32], F32)
    if debug:
        dbg["yb"] = nc.dram_tensor("dbg_yb", [128, 4, TT], BF16, kind="ExternalOutput").ap()
        dbg["sin"] = nc.dram_tensor("dbg_sin", [128, 4, 128], F32, kind="ExternalOutput").ap()
        dbg["ya"] = nc.dram_tensor("dbg_ya", [128, 4, TT], BF16, kind="ExternalOutput").ap()
        dbg["toep"] = nc.dram_tensor("dbg_toep", [128, 32, 256], BF16, kind="ExternalOutput").ap()
        dbg["hin"] = nc.dram_tensor("dbg_hin", [128, 2, 16], F32, kind="ExternalOutput").ap()
        dbg["yfm"] = nc.dram_tensor("dbg_yfm", [128, 4, TT], BF16, kind="ExternalOutput").ap()
    agi_h = nc.dram_tensor("agi_h", [128, 516], F32)
    ago_h = nc.dram_tensor("ago_h", [4 * 128, 516], F32)

    ar = Arena(nc, tr, 192 * 1024)
    stop_at = debug.get("stop") if isinstance(debug, dict) else None
    if isinstance(debug, dict) and debug.get("nosync"):
        tr.same_engine_sync = False

    class _Stop(Exception):
        pass

    def checkpoint(name):
        if stop_at == name:
            raise _Stop()

    def sb(name, shape, dt=F32):
        t = nc.sbuf_tensor(name, list(shape), dt).__enter__()
        b = Buf(name, {})
        b.ap = t[:] if False else t
        return b, t

    cst = {}
    for name, shape, dt, src in [
        ("nw", [128, 8], F32, nw_d), ("lbl", [128, 2, 4], F32, lbl_d), ("hnw", [128, 4], F32, hnw_d),
        ("use", [128, 4], F32, use_d), ("identb", [128, 128], BF16, identb_d),
        ("maskbc", [128, 128], F32, maskbc_d), ("mask512", [128, 512], F32, mask512_d),
        ("bglu", [128, 4], F32, bglu_d),
    ]:
        b, t = sb("c_" + name, shape, dt)
        cst[name] = (b, t)
        tr.dma("sp", "const", t[:], src, writes=[b.k()])
    for name in cst:
        tr.res[cst[name][0].k()] = {"w": ("const", tr.cnt["const"]), "r": {}}
    CB = lambda n: cst[n][0]
    CT = lambda n: cst[n][1]

    def small(name, shape, dt=F32):
        b, t = sb(name, shape, dt)
        return b, t

    ps_f = [nc.psum_tensor(f"psf{i}", [128, 512], F32).__enter__() for i in range(6)]
    ps_b = [nc.psum_tensor(f"psb{i}", [128, 1024], BF16).__enter__() for i in range(2)]
    psk_f = [("psf", i) for i in range(6)]
    psk_b = [("psb", i) for i in range(2)]
    mmpool = PsumPool([0, 1, 2, 3])
    smpool = PsumPool([4, 5])
    trpool = PsumPool([0, 1])

    xnT = ar.alloc("xnT", [128, 8, TT], BF16)
    wbf = [ar.alloc(f"wbf{i}", [128, 8, 512], BF16) for i in range(3)]
    def load_weight_cols(dst_buf, dst_ap_fn, src_d, kcs, col0, ncols, scale=None, dst_keys=None, eng="pool", only=None):
        assert scale is None
        for c0 in range(0, ncols, 256):
            if only is not None and c0 != only:
                continue
            src = src_d[:, col0 + c0: col0 + c0 + 256].rearrange("(kc p) c -> p kc c", p=128)
            keys = dst_keys if dst_keys is not None else [dst_buf.k()]
            tr.dma("pool", "w_" + dst_buf.name, dst_ap_fn(c0), src, writes=keys)

    def load_win_group(slot, col0, eng="pool", **kw):
        wb = wbf[slot]
        load_weight_cols(wb, lambda c0: wb.ap[:, :, c0:c0 + 256], w_in_d, 8, col0, 512, eng=eng, **kw)

    SL_F, SL_Q, SL_I, SL_OG, SL_ZB, SL_U, SL_ZA = 0, 1, 2, 3, 0, 1, 2

    def sigmoid3(dst_ap, dst_keys, src_ap, src_keys, tmp, nbias=None, nbias_keys=()):
        dk = list(dst_keys)
        if nbias is None:
            tr.op("act", lambda e: e.activation(out=dst_ap, in_=src_ap, func=AF.Exp, scale=-1.0), reads=list(src_keys), writes=dk)
        else:
            tr.op("act", lambda e: e.activation(out=dst_ap, in_=src_ap, func=AF.Exp, scale=-1.0, bias=nbias),
                  reads=list(src_keys) + list(nbias_keys), writes=dk)
        tr.op("act", lambda e: e.activation(out=dst_ap, in_=dst_ap, func=AF.Ln, bias=1.0), reads=dk, writes=dk)
        tr.op("act", lambda e: e.activation(out=dst_ap, in_=dst_ap, func=AF.Exp, scale=-1.0), reads=dk, writes=dk)

    def sigmoid_dve(dst_ap, dst_keys, src_ap, src_keys):
        dk = list(dst_keys)
        tr.op("act", lambda e: e.activation(out=dst_ap, in_=src_ap, func=AF.Exp, scale=-1.0), reads=list(src_keys), writes=dk)
        tr.op("dve", lambda e: e.tensor_scalar(out=dst_ap, in0=dst_ap, scalar1=1.0, scalar2=None, op0=ALU.add), reads=dk, writes=dk)
        tr.op("dve", lambda e: e.reciprocal(out=dst_ap, in_=dst_ap), reads=dk, writes=dk)

    def rstd_act(ap, keys, inv_n):
        tr.op("act", lambda e: e.activation(out=ap, in_=ap, func=AF.Ln, scale=inv_n, bias=epsc[:, 0:1]), reads=list(keys) + [epsc_b.k()], writes=list(keys))
        tr.op("act", lambda e: e.activation(out=ap, in_=ap, func=AF.Exp, scale=-0.5), reads=list(keys), writes=list(keys))

    epsc_b, epsc = small("epsc", [128, 1])
    tr.op("dve", lambda e: e.memset(epsc[:, :], EPS), writes=[epsc_b.k()])

    a2k_b, a2k = small("a2k", [128, 2, 16])
    rho16_b, rho16 = small("rho16", [128, 16])
    PI = math.pi

    def bk(bs):
        return [b.k() for b in bs]

    s5h = {}

    def s5_setup():
        A = lambda n, shp, dt=F32: ar.alloc("s5_" + n, shp, dt)
        P = {}
        for n, shp, src in [("lamre", [128, 16], lamre_d), ("lamim", [128, 16], lamim_d), ("logdt", [128, 16], logdt_d),
                            ("bre", [128, 16, 16], bre_d), ("bim", [128, 16, 16], bim_d), ("cre", [128, 16, 16], cre_d),
                            ("cim", [128, 16, 16], cim_d), ("dbc", [128, 32, 16], dbc_d), ("kv", [128, 41], kv_d),
                            ("cidx", [128, 128], cidx_d), ("identf", [128, 128], identf_d), ("maskT", [128, 256], maskT_d)]:
            b = A(n, shp)
            tr.dma("sp", "const2", b.ap, src, writes=[b.k()])
            P[n] = b
        for b in P.values():
            tr.res[b.k()] = {"w": ("const2", tr.cnt["const2"]), "r": {}}
        tr.defer = []

        def tt(eng, out_b, out_ap, a_b, a_ap, b_b, b_ap, op):
            tr.op(eng, lambda e: e.tensor_tensor(out=out_ap, in0=a_ap, in1=b_ap, op=op), reads=bk([a_b, b_b]), writes=bk([out_b]))

        def ts(eng, out_b, out_ap, a_b, a_ap, s1, s2, op0, op1=None):
            if op1 is None:
                tr.op(eng, lambda e: e.tensor_scalar(out=out_ap, in0=a_ap, scalar1=s1, scalar2=None, op0=op0), reads=bk([a_b]), writes=bk([out_b]))
            else:
                tr.op(eng, lambda e: e.tensor_scalar(out=out_ap, in0=a_ap, scalar1=s1, scalar2=s2, op0=op0, op1=op1), reads=bk([a_b]), writes=bk([out_b]))

        def act(out_b, out_ap, a_b, a_ap, func, **kw):
            tr.op("act", lambda e: e.activation(out=out_ap, in_=a_ap, func=func, **kw), reads=bk([a_b]), writes=bk([out_b]))

        def range_reduce(ang_b, shape):
            ti = A("rr_i", shape, I32)
            tf = A("rr_f", shape, F32)
            ts("dve", ti, ti.ap, ang_b, ang_b.ap, 1.0 / TWO_PI, None, ALU.mult)
            tr.op("dve", lambda e: e.tensor_copy(out=tf.ap, in_=ti.ap), reads=bk([ti]), writes=bk([tf]))
            tr.op("dve", lambda e: e.scalar_tensor_tensor(out=ang_b.ap, in0=tf.ap, scalar=-TWO_PI, in1=ang_b.ap, op0=ALU.mult, op1=ALU.add),
                  reads=bk([tf, ang_b]), writes=bk([ang_b]))
            ts("dve", ang_b, ang_b.ap, ang_b, ang_b.ap, -PI, PI, ALU.max, ALU.min)
            ar.free(ti, tf)

        def sincos(r_b, sin_b, sin_ap, cos_b, cos_ap, shape):
            ab = A("sc_ab", shape)
            act(sin_b, sin_ap, r_b, r_b.ap, AF.Sin)
            act(ab, ab.ap, r_b, r_b.ap, AF.Sin, scale=0.5)
            tt("dve", ab, ab.ap, ab, ab.ap, ab, ab.ap, ALU.mult)
            tr.op("dve", lambda e: e.tensor_scalar(out=cos_ap, in0=ab.ap, scalar1=-2.0, scalar2=1.0, op0=ALU.mult, op1=ALU.add),
                  reads=[ab.k()], writes=[cos_b.k()])
            ar.free(ab)

        hpi_b, hpi = small("hpi", [128, 1])

        NK = 41
        lr, dtt, lrd, lid = A("lr", [128, 16]), A("dt", [128, 16]), A("lrd", [128, 16]), A("lid", [128, 16])
        ts("dve", lr, lr.ap, P["lamre"], P["lamre"].ap, -1e-4, None, ALU.min)
        act(dtt, dtt.ap, P["logdt"], P["logdt"].ap, AF.Exp)
        tt("dve", lrd, lrd.ap, lr, lr.ap, dtt, dtt.ap, ALU.mult)
        tt("dve", lid, lid.ap, P["lamim"], P["lamim"].ap, dtt, dtt.ap, ALU.mult)
        E, ang = A("E", [128, 16, NK]), A("ang", [128, 16, NK])
        kvb = P["kv"].ap.unsqueeze(1).to_broadcast([128, 16, NK])
        tt("dve", E, E.ap, lrd, lrd.ap.unsqueeze(2).to_broadcast([128, 16, NK]), P["kv"], kvb, ALU.mult)
        tt("dve", ang, ang.ap, lid, lid.ap.unsqueeze(2).to_broadcast([128, 16, NK]), P["kv"], kvb, ALU.mult)
        act(E, E.ap, E, E.ap, AF.Exp)
        range_reduce(ang, [128, 16, NK])
        Sn, Cs = A("Sn", [128, 16, NK]), A("Cs", [128, 16, NK])
        sincos(ang, Sn, Sn.ap, Cs, Cs.ap, [128, 16, NK])
        EC, ES, nEC, nES = A("EC", [128, 16, NK]), A("ES", [128, 16, NK]), A("nEC", [128, 16, NK]), A("nES", [128, 16, NK])
        tt("dve", EC, EC.ap, E, E.ap, Cs, Cs.ap, ALU.mult)
        tt("dve", ES, ES.ap, E, E.ap, Sn, Sn.ap, ALU.mult)
        ts("dve", nEC, nEC.ap, EC, EC.ap, -1.0, None, ALU.mult)
        ts("dve", nES, nES.ap, ES, ES.ap, -1.0, None, ALU.mult)
        ar.free(E, ang, Sn, Cs)
        checkpoint("setup1")
        den, t0, nr, cfr, cfi = A("den", [128, 16]), A("t0", [128, 16]), A("nr", [128, 16]), A("cfr", [128, 16]), A("cfi", [128, 16])
        li = P["lamim"]
        abre, abim = EC.ap[:, :, 9], ES.ap[:, :, 9]
        tt("dve", t0, t0.ap, lr, lr.ap, lr, lr.ap, ALU.mult)
        tt("dve", den, den.ap, li, li.ap, li, li.ap, ALU.mult)
        tt("dve", den, den.ap, den, den.ap, t0, t0.ap, ALU.add)
        tr.op("dve", lambda e: e.reciprocal(out=den.ap, in_=den.ap), reads=bk([den]), writes=bk([den]))
        ts("dve", nr, nr.ap, EC, abre, -1.0, None, ALU.add)
        tt("dve", cfr, cfr.ap, nr, nr.ap, lr, lr.ap, ALU.mult)
        tt("dve", t0, t0.ap, ES, abim, li, li.ap, ALU.mult)
        tt("dve", cfr, cfr.ap, cfr, cfr.ap, t0, t0.ap, ALU.add)
        tt("dve", cfr, cfr.ap, cfr, cfr.ap, den, den.ap, ALU.mult)
        tt("dve", cfi, cfi.ap, ES, abim, lr, lr.ap, ALU.mult)
        tt("dve", t0, t0.ap, nr, nr.ap, li, li.ap, ALU.mult)
        tt("dve", cfi, cfi.ap, cfi, cfi.ap, t0, t0.ap, ALU.subtract)
        tt("dve", cfi, cfi.ap, cfi, cfi.ap, den, den.ap, ALU.mult)
        bbre, bbim, t1s, t2s = A("bbre", [128, 16, 16]), A("bbim", [128, 16, 16]), A("t1s", [128, 16, 16]), A("t2s", [128, 16, 16])
        cb = lambda b: b.ap.unsqueeze(2).to_broadcast([128, 16, 16])
        tt("dve", t1s, t1s.ap, cfr, cb(cfr), P["bre"], P["bre"].ap, ALU.mult)
        tt("dve", t2s, t2s.ap, cfi, cb(cfi), P["bim"], P["bim"].ap, ALU.mult)
        tt("dve", bbre, bbre.ap, t1s, t1s.ap, t2s, t2s.ap, ALU.subtract)
        tt("dve", t1s, t1s.ap, cfr, cb(cfr), P["bim"], P["bim"].ap, ALU.mult)
        tt("dve", t2s, t2s.ap, cfi, cb(cfi), P["bre"], P["bre"].ap, ALU.mult)
        tt("dve", bbim, bbim.ap, t1s, t1s.ap, t2s, t2s.ap, ALU.add)
        ar.free(den, t0, nr, cfr, cfi, t1s, t2s)
        checkpoint("setup1b")

        def outer(eng, out_b, out_ap, pw_b, lo, ns, vec_b):
            tr.op(eng, lambda e: e.tensor_tensor(out=out_ap, in0=pw_b.ap[:, :, lo:lo + ns].unsqueeze(3).to_broadcast([128, 16, ns, 16]),
                                                 in1=vec_b.ap.unsqueeze(2).to_broadcast([128, 16, ns, 16]), op=ALU.mult),
                  reads=bk([pw_b, vec_b]), writes=bk([out_b]))

        tr.stop_defer()
        yield
        Xre, Xim, o1, o1d = A("Xre", [128, 16, 16, 16]), A("Xim", [128, 16, 16, 16]), A("o1", [128, 16, 16, 16]), A("o1d", [128, 16, 16, 16])
        Yre, Yim = A("Yre", [128, 16, 8, 16]), A("Yim", [128, 16, 8, 16])
        outer("pool", Xre, Xre.ap, EC, 9, 16, P["cre"])
        outer("pool", o1, o1.ap, ES, 9, 16, P["cim"])
        tt("pool", Xre, Xre.ap, Xre, Xre.ap, o1, o1.ap, ALU.subtract)
        outer("dve", Xim, Xim.ap, nEC, 9, 16, P["cim"])
        outer("dve", o1d, o1d.ap, nES, 9, 16, P["cre"])
        tt("dve", Xim, Xim.ap, Xim, Xim.ap, o1d, o1d.ap, ALU.add)
        o1y = o1d.ap[:, :, 0:8, :]
        outer("dve", Yre, Yre.ap, EC, 0, 8, bbre)
        outer("dve", o1d, o1y, ES, 0, 8, bbim)
        tt("dve", Yre, Yre.ap, Yre, Yre.ap, o1d, o1y, ALU.subtract)
        outer("dve", Yim, Yim.ap, EC, 0, 8, bbim)
        outer("dve", o1d, o1y, ES, 0, 8, bbre)
        tt("dve", Yim, Yim.ap, Yim, Yim.ap, o1d, o1y, ALU.add)
        ar.free(o1, o1d)
        cpb = A("cpb", [128, 2, 16, 256], BF16)
        act(cpb, cpb.ap[:, 0], Xre, Xre.ap.rearrange("p a s q -> p a (s q)"), AF.Copy)
        act(cpb, cpb.ap[:, 1], Xim, Xim.ap.rearrange("p a s q -> p a (s q)"), AF.Copy)
        tr.dma("sp", "s5st", cp_dr.ap(), cpb.ap, reads=bk([cpb]), writes=[("cp_dr",)])
        ar.free(cpb)
        checkpoint("setup2")
        bsnb = A("bsnb", [128, 2, 16, 256], BF16)
        o1q, o2q = A("o1q", [128, 4, 16, 16]), A("o2q", [128, 4, 16, 16])
        bsn_ops = []

        def outer_q(out_b, pw_b, vec_b, q):
            bsn_ops.append(lambda: tr.op("dve", lambda e: e.tensor_tensor(
                out=out_b.ap, in0=pw_b.ap[:, 4 * q:4 * q + 4, 25:41].unsqueeze(3).to_broadcast([128, 4, 16, 16]),
                in1=vec_b.ap[:, 4 * q:4 * q + 4, :].unsqueeze(2).to_broadcast([128, 4, 16, 16]), op=ALU.mult),
                reads=bk([pw_b, vec_b]), writes=bk([out_b])))

        def comb_q(ri, q, op):
            fl = lambda b: b.ap.rearrange("p a s q -> p a (s q)")
            bsn_ops.append(lambda: tr.op("dve", lambda e: e.tensor_tensor(out=bsnb.ap[:, ri, 4 * q:4 * q + 4, :], in0=fl(o1q), in1=fl(o2q), op=op),
                                         reads=bk([o1q, o2q]), writes=[bsnb.k(ri, q)]))

        for q in range(4):
            outer_q(o1q, EC, bbre, q)
            outer_q(o2q, ES, bbim, q)
            comb_q(0, q, ALU.subtract)
            outer_q(o1q, EC, bbim, q)
            outer_q(o2q, ES, bbre, q)
            comb_q(1, q, ALU.add)
        toepb = A("toepb", [128, 32, 256], BF16)
        tmpT = [A(f"tmpT{j}", [128, 256]) for j in range(2)]
        dg = [A(f"dg{j}", [128, 128]) for j in range(2)]
        for g in range(32):
            pr, g2 = g // 2, g % 2
            rows = slice(g2 * 64, g2 * 64 + 64)
            bi = mmpool.next()
            tr.op("pe", lambda e: e.matmul(ps_f[bi][:, 0:256], lhsT=Yre.ap[rows, pr].rearrange("p s q -> p (s q)"),
                                           rhs=Xre.ap[rows, pr].rearrange("p s q -> p (s q)"), start=True, stop=False),
                  reads=bk([Yre, Xre]), writes=[psk_f[bi]])
            tr.op("pe", lambda e: e.matmul(ps_f[bi][:, 0:256], lhsT=Yim.ap[rows, pr].rearrange("p s q -> p (s q)"),
                                           rhs=Xim.ap[rows, pr].rearrange("p s q -> p (s q)"), start=False, stop=True),
                  reads=bk([Yim, Xim]), writes=[psk_f[bi]])
            tT, dG = tmpT[g % 2], dg[g % 2]
            tr.op("dve", lambda e: e.tensor_tensor(out=tT.ap, in0=ps_f[bi][:, 0:256], in1=P["maskT"].ap, op=ALU.mult),
                  reads=[psk_f[bi], P["maskT"].k()], writes=bk([tT]))
            tr.op("pool", lambda e: e.tensor_tensor(out=dG.ap.rearrange("p (s q) -> p s q", s=8),
                                                    in0=P["identf"].ap.rearrange("p (s q) -> p s q", s=8),
                                                    in1=P["dbc"].ap[:, g, :].unsqueeze(1).to_broadcast([128, 8, 16]), op=ALU.mult),
                  reads=bk([P["identf"], P["dbc"]]), writes=bk([dG]))
            tr.op("dve", lambda e: e.tensor_tensor(out=toepb.ap[:, g, 0:128], in0=tT.ap[:, 0:128], in1=dG.ap, op=ALU.add),
                  reads=bk([tT, dG]), writes=[toepb.k(g, 0)])
            tr.op("act", lambda e: e.activation(out=toepb.ap[:, g, 128:256], in_=tT.ap[:, 128:256], func=AF.Copy),
                  reads=bk([tT]), writes=[toepb.k(g, 1)])
            if g >= 4 and bsn_ops:
                bsn_ops.pop(0)()
        tr.dma("sp", "s5st", toep_dr.ap(), toepb.ap, reads=[toepb.k(g, j) for g in range(32) for j in range(2)], writes=[("toep_dr",)])
        if debug:
            tr.dma("sp", "dbg", dbg["toep"], toepb.ap, reads=[toepb.k(g, j) for g in range(32) for j in range(2)])
        while bsn_ops:
            bsn_ops.pop(0)()
        ar.free(Xre, Xim, Yre, Yim, toepb, *tmpT, *dg)
        checkpoint("setup3")
        yield
        ar.free(o1q, o2q, bbre, bbim, EC, ES, nEC, nES, *[P[n] for n in ("bre", "bim", "cre", "cim", "dbc", "maskT", "identf", "kv", "lamre", "lamim", "logdt")])
        bpb = A("bpb", [128, 2, 16, 2, 128], BF16)
        n_ev = 0
        for j in range(2):
            for prb in range(4):
                pb_i = trpool.next()
                pt = ps_b[pb_i]
                for pr in range(4 * prb, 4 * prb + 4):
                    for ri in range(2):
                        col = ((pr % 4) * 2 + ri) * 128
                        tr.op("pe", lambda e: e.transpose(out=pt[:, col:col + 128], in_=bsnb.ap[:, ri, pr, j * 128:(j + 1) * 128],
                                                          identity=CT("identb")[:]),
                              reads=[bsnb.k(ri, pr // 4), CB("identb").k()], writes=[psk_b[pb_i]])
                dst = bpb.ap[:, j, 4 * prb:4 * prb + 4].rearrange("p a r n -> p (a r n)")
                if n_ev % 2 == 0:
                    tr.op("dve", lambda e: e.tensor_copy(out=dst, in_=pt[:, :]), reads=[psk_b[pb_i]], writes=[bpb.k(j, prb)])
                else:
                    tr.op("act", lambda e: e.activation(out=dst, in_=pt[:, :], func=AF.Copy), reads=[psk_b[pb_i]], writes=[bpb.k(j, prb)])
                n_ev += 1
        tr.dma("sp", "s5st", bp_dr.ap(), bpb.ap, reads=[bpb.k(j, gb) for j in range(2) for gb in range(4)], writes=[("bp_dr",)])
        ar.free(bsnb, bpb)
        checkpoint("setup4")
        yield
        lrd16, phi = A("lrd16", [128, 16]), A("phi", [128, 16])
        ts("dve", lrd16, lrd16.ap, lrd, lrd.ap, 16.0, None, ALU.mult)
        ts("dve", phi, phi.ap, lid, lid.ap, 16.0, None, ALU.mult)
        range_reduce(phi, [128, 16])
        act(rho16_b, rho16[:, :], lrd16, lrd16.ap, AF.Exp)
        tabs = A("tabs", [128, 3, 16, 128])
        angT = A("angT", [128, 16, 128])
        cib = P["cidx"].ap.unsqueeze(1).to_broadcast([128, 16, 128])
        tt("dve", tabs, tabs.ap[:, 0], lrd16, lrd16.ap.unsqueeze(2).to_broadcast([128, 16, 128]), P["cidx"], cib, ALU.mult)
        act(tabs, tabs.ap[:, 0], tabs, tabs.ap[:, 0], AF.Exp)
        tt("dve", angT, angT.ap, phi, phi.ap.unsqueeze(2).to_broadcast([128, 16, 128]), P["cidx"], cib, ALU.mult)
        range_reduce(angT, [128, 16, 128])
        sincos(angT, tabs, tabs.ap[:, 2], tabs, tabs.ap[:, 1], [128, 16, 128])
        tt("dve", a2k_b, a2k[:, 0, :], tabs, tabs.ap[:, 0, :, 127], tabs, tabs.ap[:, 1, :, 127], ALU.mult)
        tt("dve", a2k_b, a2k[:, 1, :], tabs, tabs.ap[:, 0, :, 127], tabs, tabs.ap[:, 2, :, 127], ALU.mult)
        s5h["tabs"] = tabs
        ar.free(angT, lrd16, phi, lr, dtt, lrd, lid, P["cidx"])

    setup_gen = s5_setup()
    next(setup_gen)

    def rest():
        xb = [ar.alloc(f"xb{i}", [128, D], F32) for i in range(5)]
        xnb = [ar.alloc(f"xnb{i}", [128, D], BF16) for i in range(3)]
        junk = ar.alloc("junk", [128, D], BF16)
        ssq_b, ssq = small("ssq", [128, NT])
        rstd_b, rstd = small("rstd", [128, NT])
        early = {0: (SL_F, C_F, 0), 2: (SL_F, C_F, 256), 4: (SL_Q, C_Q, 0), 6: (SL_Q, C_Q, 256), 8: (SL_I, C_I, 0), 10: (SL_I, C_I, 256)}
        for i in range(NT):
            tr.replay(5)
            if i in early:
                load_win_group(early[i][0], early[i][1], eng="dve", only=early[i][2])
            xt = xb[i % 5]
            tr.dma("sp", f"xb{i % 5}", xt.ap, x_d[i * 128:(i + 1) * 128, :], writes=[xt.k()])
            tr.op("act", lambda e: e.activation(out=junk.ap, in_=xt.ap, func=AF.Square, accum_out=ssq[:, i:i + 1]),
                  reads=[xt.k()], writes=[junk.k(), ssq_b.k(i)])
            tr.op("act", lambda e: e.activation(out=rstd[:, i:i + 1], in_=ssq[:, i:i + 1], func=AF.Ln, scale=1.0 / D, bias=epsc[:, 0:1]),
                  reads=[ssq_b.k(i), epsc_b.k()], writes=[rstd_b.k(i)])
            tr.op("act", lambda e: e.activation(out=rstd[:, i:i + 1], in_=rstd[:, i:i + 1], func=AF.Exp, scale=-0.5),
                  reads=[rstd_b.k(i)], writes=[rstd_b.k(i)])
            xn = xnb[i % 3]
            if i % 2:
                tr.op("dve", lambda e: e.tensor_scalar(out=xn.ap, in0=xt.ap, scalar1=rstd[:, i:i + 1], scalar2=None, op0=ALU.mult),
                      reads=[xt.k(), rstd_b.k(i)], writes=[xn.k()])
            else:
                tr.op("act", lambda e: e.activation(out=xn.ap, in_=xt.ap, func=AF.Copy, scale=rstd[:, i:i + 1]),
                      reads=[xt.k(), rstd_b.k(i)], writes=[xn.k()])
            pb_i = trpool.next()
            pt = ps_b[pb_i]
            for kc in range(8):
                tr.op("pe", lambda e: e.transpose(out=pt[:, kc * 128:(kc + 1) * 128], in_=xn.ap[:, kc * 128:(kc + 1) * 128],
                                                  identity=CT("identb")[:]),
                      reads=[xn.k(), CB("identb").k()], writes=[psk_b[pb_i]])
            tr.op("dve", lambda e: e.tensor_tensor(out=xnT.ap[:, :, i * 128:(i + 1) * 128],
                                                   in0=pt[:, :].rearrange("p (a b) -> p a b", a=8),
                                                   in1=CT("nw")[:, 0:8].unsqueeze(2).to_broadcast([128, 8, 128]), op=ALU.mult),
                  reads=[psk_b[pb_i], CB("nw").k()], writes=[xnT.k(i)])
        ar.free(*xb, *xnb, junk)
        tr.replay()
        checkpoint("phaseA")
        V = ar.alloc("V", [128, NT, 512], BF16)
        for i in range(NT):
            bi = mmpool.next()
            for kc in range(8):
                tr.op("pe", lambda e: e.matmul(ps_f[bi][:, :], lhsT=xnT.ap[:, kc, i * 128:(i + 1) * 128],
                                               rhs=wbf[SL_I].ap[:, kc, :], start=(kc == 0), stop=(kc == 7)),
                      reads=[wbf[SL_I].k(), xnT.k(i)], writes=[psk_f[bi]])
            tr.op("act", lambda e: e.activation(out=V.ap[:, i, :], in_=ps_f[bi][:, :], func=AF.Copy), reads=[psk_f[bi]], writes=[V.k(i)])
        ar.free(wbf[SL_I])
        next(setup_gen)
        checkpoint("setup")
        wbf.append(ar.alloc("wbf3", [128, 8, 512], BF16))
        load_win_group(SL_OG, C_OG)

        def proj_fm(wb, ct, tb):
            bi = mmpool.next()
            for kc in range(8):
                tr.op("pe", lambda e: e.matmul(ps_f[bi][:, :], lhsT=wb.ap[:, kc, ct * 128:(ct + 1) * 128],
                                               rhs=xnT.ap[:, kc, tb * 512:(tb + 1) * 512], start=(kc == 0), stop=(kc == 7)),
                      reads=[wb.k()] + [xnT.k(4 * tb + j) for j in range(4)], writes=[psk_f[bi]])
            return bi

        def proj_tm(wb, i):
            bi = mmpool.next()
            for kc in range(8):
                tr.op("pe", lambda e: e.matmul(ps_f[bi][:, :], lhsT=xnT.ap[:, kc, i * 128:(i + 1) * 128],
                                               rhs=wb.ap[:, kc, :], start=(kc == 0), stop=(kc == 7)),
                      reads=[wb.k(), xnT.k(i)], writes=[psk_f[bi]])
            return bi

        lb_b, lb = small("lb", [128, 4])
        oml_b, oml = small("oml", [128, 4])
        noml_b, noml = small("noml", [128, 4])
        tr.op("dve", lambda e: e.tensor_sub(out=lb[:, :], in0=CT("lbl")[:, 0, :], in1=CT("lbl")[:, 1, :]),
              reads=[CB("lbl").k()], writes=[lb_b.k()])
        sigmoid3(lb[:, :], [lb_b.k()], lb[:, :], [lb_b.k()], None)
        tr.op("dve", lambda e: e.tensor_scalar(out=oml[:, :], in0=lb[:, :], scalar1=-1.0, scalar2=1.0, op0=ALU.mult, op1=ALU.add),
              reads=[lb_b.k()], writes=[oml_b.k()])
        tr.op("dve", lambda e: e.tensor_scalar(out=noml[:, :], in0=lb[:, :], scalar1=-1.0, scalar2=None, op0=ALU.add),
              reads=[lb_b.k()], writes=[noml_b.k()])

        KdT = ar.alloc("KdT", [128, 4, TT], BF16)
        QdT = ar.alloc("QdT", [128, 4, TT], BF16)
        lastc_b, lastc = small("lastc", [128, 4, 32])
        el_b, el = small("el", [128, 4, 33])
        tmp = {n: [ar.alloc(f"t_{n}{j}", [128, 512], F32) for j in range(2)] for n in ["e1", "A", "lg", "e2"]}
        lnoml_b, lnoml = small("lnoml", [128, 4])
        tr.op("act", lambda e: e.activation(out=lnoml[:, :], in_=oml[:, :], func=AF.Ln), reads=[oml_b.k()], writes=[lnoml_b.k()])
        it = 0
        for h in range(4):
            for tb in range(NB):
                j = it % 2
                it += 1
                T = {n: tmp[n][j] for n in tmp}
                pf = proj_fm(wbf[SL_F], h, tb)
                pq = proj_fm(wbf[SL_Q], h, tb)
                tr.op("act", lambda e: e.activation(out=T["e1"].ap, in_=ps_f[pf][:, :], func=AF.Exp, scale=-1.0), reads=[psk_f[pf]], writes=[T["e1"].k()])
                tr.op("act", lambda e: e.activation(out=T["A"].ap, in_=T["e1"].ap, func=AF.Ln, bias=1.0), reads=[T["e1"].k()], writes=[T["A"].k()])
                tr.op("act", lambda e: e.activation(out=T["lg"].ap, in_=T["e1"].ap, func=AF.Ln, scale=lb[:, h:h + 1], bias=1.0),
                      reads=[T["e1"].k(), lb_b.k()], writes=[T["lg"].k()])
                tr.op("act", lambda e: e.activation(out=T["e2"].ap, in_=ps_f[pq][:, :], func=AF.Exp, scale=-1.0), reads=[psk_f[pq]], writes=[T["e2"].k()])
                tr.op("act", lambda e: e.activation(out=T["e2"].ap, in_=T["e2"].ap, func=AF.Ln, bias=1.0), reads=[T["e2"].k()], writes=[T["e2"].k()])
                tr.op("dve", lambda e: e.tensor_sub(out=T["lg"].ap, in0=T["lg"].ap, in1=T["A"].ap), reads=[T["lg"].k(), T["A"].k()], writes=[T["lg"].k()])
                tr.op("dve", lambda e: e.tensor_tensor_scan(out=T["lg"].ap, data0=CT("mask512")[:, :], data1=T["lg"].ap, initial=0.0,
                                                            op0=ALU.mult, op1=ALU.add),
                      reads=[T["lg"].k(), CB("mask512").k()], writes=[T["lg"].k()])
                tr.op("dve", lambda e: e.tensor_copy(out=lastc[:, h, tb * 8:(tb + 1) * 8], in_=T["lg"].ap[:, 63:512:64]),
                      reads=[T["lg"].k()], writes=[lastc_b.k(h, tb)])
                tr.op("dve", lambda e: e.tensor_add(out=T["A"].ap, in0=T["A"].ap, in1=T["lg"].ap), reads=[T["lg"].k(), T["A"].k()], writes=[T["A"].k()])
                tr.op("dve", lambda e: e.tensor_tensor(out=T["A"].ap, in0=ps_f[pf][:, :], in1=T["A"].ap, op=ALU.add),
                      reads=[psk_f[pf], T["A"].k()], writes=[T["A"].k()])
                tr.op("act", lambda e: e.activation(out=KdT.ap[:, h, tb * 512:(tb + 1) * 512], in_=T["A"].ap, func=AF.Exp, scale=-1.0,
                                                    bias=lnoml[:, h:h + 1]),
                      reads=[T["A"].k(), lnoml_b.k()], writes=[KdT.k(h, tb)])
                tr.op("dve", lambda e: e.tensor_sub(out=T["e2"].ap, in0=T["lg"].ap, in1=T["e2"].ap), reads=[T["lg"].k(), T["e2"].k()], writes=[T["e2"].k()])
                tr.op("act", lambda e: e.activation(out=T["e2"].ap, in_=T["e2"].ap, func=AF.Exp), reads=[T["e2"].k()], writes=[T["e2"].k()])
                tr.op("dve", lambda e: e.tensor_tensor(out=QdT.ap[:, h, tb * 512:(tb + 1) * 512], in0=ps_f[pq][:, :], in1=T["e2"].ap, op=ALU.mult),
                      reads=[psk_f[pq], T["e2"].k()], writes=[QdT.k(h, tb)])
        load_win_group(SL_ZB, C_ZB)
        load_win_group(SL_U, C_U)
        for n in tmp:
            ar.free(*tmp[n])
        checkpoint("step1")
        KdTM = ar.alloc("KdTM", [128, NT, 512], BF16)
        allc = [lastc_b.k(h, tb) for h in range(4) for tb in range(NB)]
        tr.op("dve", lambda e: e.memset(el[:, :, 0:1], 1.0), writes=[el_b.k()])
        tr.op("act", lambda e: e.activation(out=el[:, :, 1:33], in_=lastc[:, :, :], func=AF.Exp), reads=allc + [el_b.k()], writes=[el_b.k()])
        pk_b, pk = small("pk", [128, 516])
        tr.op("dve", lambda e: e.reduce_sum(out=pk[:, 512:516], in_=lastc[:, :, :], axis=AX.X), reads=allc, writes=[pk_b.k("d")])
        sfx_b, sfx = small("sfx", [128, 4, 32])
        ones_b, ones = small("ones32", [128, 32])
        tr.op("dve", lambda e: e.memset(ones[:, :], 1.0), writes=[ones_b.k()])
        for h in range(4):
            tr.op("dve", lambda e: e.tensor_tensor_scan(out=sfx[:, h, :], data0=ones[:, :], data1=lastc[:, h, :], initial=0.0,
                                                        op0=ALU.mult, op1=ALU.add), reads=allc + [ones_b.k()], writes=[sfx_b.k()])
        tr.op("dve", lambda e: e.tensor_sub(out=sfx[:, :, :], in0=lastc[:, :, :], in1=sfx[:, :, :]), reads=allc + [sfx_b.k()], writes=[sfx_b.k()])
        tr.op("dve", lambda e: e.tensor_tensor(out=sfx[:, :, :], in0=sfx[:, :, :], in1=pk[:, 512:516].unsqueeze(2).to_broadcast([128, 4, 32]), op=ALU.add),
              reads=[sfx_b.k(), pk_b.k("d")], writes=[sfx_b.k()])
        tr.op("act", lambda e: e.activation(out=sfx[:, :, :], in_=sfx[:, :, :], func=AF.Exp), reads=[sfx_b.k()], writes=[sfx_b.k()])
        tr.op("act", lambda e: e.activation(out=pk[:, 512:516], in_=pk[:, 512:516], func=AF.Exp), reads=[pk_b.k("d"), sfx_b.k()], writes=[pk_b.k("d")])

        for i in range(NT):
            pb_i = trpool.next()
            pt = ps_b[pb_i]
            for h in range(4):
                tr.op("pe", lambda e: e.transpose(out=pt[:, h * 128:(h + 1) * 128], in_=KdT.ap[:, h, i * 128:(i + 1) * 128],
                                                  identity=CT("identb")[:]),
                      reads=[KdT.k(h, i // 4), CB("identb").k()], writes=[psk_b[pb_i]])
            if i % 2:
                tr.op("dve", lambda e: e.tensor_copy(out=KdTM.ap[:, i, :], in_=pt[:, 0:512]), reads=[psk_b[pb_i]], writes=[KdTM.k(i)])
            else:
                tr.op("act", lambda e: e.activation(out=KdTM.ap[:, i, :], in_=pt[:, 0:512], func=AF.Copy), reads=[psk_b[pb_i]], writes=[KdTM.k(i)])
        KdPh = [ar.alloc(f"KdP{j}", [128, TT], BF16) for j in range(2)]
        KdPTMh = [ar.alloc(f"KdPTM{j}", [128, NT, 128], BF16) for j in range(2)]
        sbank = 4
        for h in range(4):
            kp, kpt = KdPh[h % 2], KdPTMh[h % 2]
            tr.op("pool", lambda e: e.tensor_tensor(out=kp.ap.rearrange("p (c t) -> p c t", t=64),
                                                    in0=KdT.ap[:, h, :].rearrange("p (c t) -> p c t", t=64),
                                                    in1=sfx[:, h, :].unsqueeze(2).to_broadcast([128, 32, 64]), op=ALU.mult),
                  reads=[KdT.k(h, tb) for tb in range(NB)] + [sfx_b.k()], writes=[kp.k()])
            for half in range(2):
                pb_i = trpool.next()
                pt = ps_b[pb_i]
                for ii in range(8):
                    i = 8 * half + ii
                    tr.op("pe", lambda e: e.transpose(out=pt[:, ii * 128:(ii + 1) * 128], in_=kp.ap[:, i * 128:(i + 1) * 128],
                                                      identity=CT("identb")[:]),
                          reads=[kp.k(), CB("identb").k()], writes=[psk_b[pb_i]])
                if half:
                    tr.op("dve", lambda e: e.tensor_copy(out=kpt.ap[:, 8 * half:8 * half + 8, :], in_=pt[:, :].rearrange("p (a b) -> p a b", a=8)),
                          reads=[psk_b[pb_i]], writes=[kpt.k(half)])
                else:
                    tr.op("act", lambda e: e.activation(out=kpt.ap[:, 8 * half:8 * half + 8, :], in_=pt[:, :].rearrange("p (a b) -> p a b", a=8), func=AF.Copy),
                          reads=[psk_b[pb_i]], writes=[kpt.k(half)])
            for i in range(NT):
                tr.op("pe", lambda e: e.matmul(ps_f[sbank][:, h * 128:(h + 1) * 128], lhsT=kpt.ap[:, i, :],
                                               rhs=V.ap[:, i, h * 128:(h + 1) * 128], start=(i == 0), stop=(i == NT - 1)),
                      reads=[kpt.k(i // 8), V.k(i)], writes=[psk_f[sbank]])
        tr.op("dve", lambda e: e.tensor_copy(out=pk[:, 0:512], in_=ps_f[sbank][:, :]), reads=[psk_f[sbank]], writes=[pk_b.k("s")])
        ar.free(*KdPh, *KdPTMh)
        checkpoint("pass1")
        pk_keys = [pk_b.k("d"), pk_b.k("s")]
        agi_k, ago_k = ("agi_h",), ("ago_h",)
        tr.dma("pool", "agh", agi_h[:, :], pk[:, :], reads=pk_keys, writes=[agi_k])
        tr.collective("cc", lambda e: e.collective_compute("AllGather", ALU.bypass, replica_groups=[[0, 1, 2, 3], [4, 5, 6, 7]],
                                                           ins=[agi_h.ap().opt()], outs=[ago_h.ap().opt()]),
                      reads=[agi_k], writes=[ago_k])
        next(setup_gen)
        gath = ar.alloc("gath", [128, 4, 516], F32)
        tr.dma("sp", "agh2", gath.ap, ago_h[:, :].rearrange("(j p) f -> p j f", p=128), reads=[ago_k], writes=[gath.k()])
        Sin_b, Sin = small("Sin", [128, 4, 128])
        ft = ar.alloc("ft", [128, 4, 128], F32)
        tr.op("dve", lambda e: e.memset(Sin[:, :, :], 0.0), writes=[Sin_b.k()])
        for j in range(3):
            tr.op("dve", lambda e: e.tensor_tensor(out=ft.ap, in0=Sin[:, :, :],
                                                   in1=gath.ap[:, j, 512:516].unsqueeze(2).to_broadcast([128, 4, 128]), op=ALU.mult),
                  reads=[Sin_b.k(), gath.k()], writes=[ft.k()])
            tr.op("dve", lambda e: e.tensor_add(out=ft.ap, in0=ft.ap, in1=gath.ap[:, j, 0:512].rearrange("p (h v) -> p h v", h=4)),
                  reads=[ft.k(), gath.k()], writes=[ft.k()])
            tr.op("dve", lambda e: e.tensor_sub(out=ft.ap, in0=ft.ap, in1=Sin[:, :, :]), reads=[ft.k(), Sin_b.k()], writes=[ft.k()])
            tr.op("dve", lambda e: e.scalar_tensor_tensor(out=Sin[:, :, :], in0=ft.ap, scalar=CT("use")[:, j:j + 1], in1=Sin[:, :, :],
                                                          op0=ALU.mult, op1=ALU.add),
                  reads=[ft.k(), Sin_b.k(), CB("use").k()], writes=[Sin_b.k()])
        ar.free(ft, gath)
        if debug:
            tr.dma("sp", "dbg", dbg["sin"], Sin[:, :, :], reads=[Sin_b.k()])

        checkpoint("fold")
        U_b = ar.alloc("U", [128, 2, 4, 128], F32)
        U = U_b.ap
        Sbf_b = ar.alloc("Sbf", [128, 4, 4, 128], BF16)
        Sbf = Sbf_b.ap
        ybT = ar.alloc("ybT", [128, 4, TT], BF16)
        p2 = {n: [ar.alloc(f"p2_{n}{j}", [128, 512], F32) for j in range(k)] for n, k in (("sog", 3), ("szb", 4), ("go", 3))}
        scb = [ar.alloc(f"scb{j}", [128, 4, 128], BF16) for j in range(2)]
        ybt = [ar.alloc(f"ybt{j}", [128, 512], BF16) for j in range(2)]
        junk2 = ar.alloc("junk2", [128, 128], BF16)
        ss_b, ss = small("ss", [128, NT, 4])
        ps_u = [ps_b[0][:, :].bitcast(F32), ps_b[1][:, :].bitcast(F32)]

        def upd_mm(c):
            i, hh = c // 2, c % 2
            for h in range(4):
                tr.op("pe", lambda e: e.matmul(ps_u[hh][:, h * 128:(h + 1) * 128],
                                               lhsT=KdTM.ap[hh * 64:(hh + 1) * 64, i, h * 128:(h + 1) * 128],
                                               rhs=V.ap[hh * 64:(hh + 1) * 64, i, h * 128:(h + 1) * 128], start=True, stop=True),
                      reads=[KdTM.k(i), V.k(i)], writes=[psk_b[hh]])

        def upd_state(c):
            hh = c % 2
            for h in range(4):
                tr.op("dve", lambda e: e.scalar_tensor_tensor(out=U[:, hh, h, :], in0=U[:, 1 - hh, h, :], scalar=el[:, h, c:c + 1],
                                                              in1=ps_u[hh][:, h * 128:(h + 1) * 128], op0=ALU.mult, op1=ALU.add),
                      reads=[U_b.k(1 - hh, h), el_b.k(), psk_b[hh]], writes=[U_b.k(hh, h)])
            slot = (c + 1) % 4
            for h in range(4):
                tr.op("pool", lambda e: e.tensor_scalar(out=Sbf[:, h, slot, :], in0=U[:, hh, h, :], scalar1=el[:, h, c + 1:c + 2], scalar2=0.0, op0=ALU.mult, op1=ALU.add),
                      reads=[U_b.k(hh, h), el_b.k()], writes=[Sbf_b.k(slot, h)])

        def st_proj(i):
            r = {}
            for nm, wb in (("og", wbf[SL_OG]), ("zb", wbf[SL_ZB])):
                bi = p2pool.next()
                for kc in range(8):
                    tr.op("pe", lambda e: e.matmul(ps_f[bi][:, :], lhsT=xnT.ap[:, kc, i * 128:(i + 1) * 128],
                                                   rhs=wb.ap[:, kc, :], start=(kc == 0), stop=(kc == 7)),
                          reads=[wb.k(), xnT.k(i)], writes=[psk_f[bi]])
                r[nm] = bi
            pbank[i] = r

        def st_gate_act(i):
            pog, pzb = pbank[i]["og"], pbank[i]["zb"]
            sigmoid3(p2["sog"][i % 3].ap, [p2["sog"][i % 3].k()], ps_f[pog][:, :], [psk_f[pog]], None)
            sigmoid3(p2["szb"][i % 4].ap, [p2["szb"][i % 4].k()], ps_f[pzb][:, :], [psk_f[pzb]], None)

        def st_gate_dve(i):
            pzb = pbank[i]["zb"]
            tr.op("dve", lambda e: e.tensor_tensor(out=p2["szb"][i % 4].ap, in0=ps_f[pzb][:, :], in1=p2["szb"][i % 4].ap, op=ALU.mult),
                  reads=[psk_f[pzb], p2["szb"][i % 4].k()], writes=[p2["szb"][i % 4].k()])

        def st_b1(i):
            obi = 4 + i % 2
            go = p2["go"][i % 3]
            tr.op("dve", lambda e: e.tensor_mul(out=go.ap, in0=ps_f[obi][:, :], in1=p2["sog"][i % 3].ap),
                  reads=[psk_f[obi], p2["sog"][i % 3].k()], writes=[go.k()])
            for h in range(4):
                tr.op("act", lambda e: e.activation(out=junk2.ap, in_=go.ap[:, h * 128:(h + 1) * 128], func=AF.Square,
                                                    accum_out=ss[:, i, h:h + 1]),
                      reads=[go.k()], writes=[junk2.k(), ss_b.k(i, h)])

        def st_b2(i):
            go = p2["go"][i % 3]
            ssk = [ss_b.k(i, h) for h in range(4)]
            rstd_act(ss[:, i, :], ssk, 1.0 / 128)
            for h in range(4):
                cols = slice(h * 128, (h + 1) * 128)
                tr.op("dve", lambda e: e.scalar_tensor_tensor(out=ybt[i % 2].ap[:, cols], in0=go.ap[:, cols], scalar=ss[:, i, h:h + 1],
                                                              in1=p2["szb"][i % 4].ap[:, cols], op0=ALU.mult, op1=ALU.mult),
                      reads=[go.k(), p2["szb"][i % 4].k()] + ssk, writes=[ybt[i % 2].k()])

        def st_b3(i):
            pt = ps_f[3][:, :].bitcast(BF16)
            for h in range(4):
                c0 = (i % 2) * 512 + h * 128
                tr.op("pe", lambda e: e.transpose(out=pt[:, c0:c0 + 128], in_=ybt[i % 2].ap[:, h * 128:(h + 1) * 128],
                                                  identity=CT("identb")[:]),
                      reads=[ybt[i % 2].k(), CB("identb").k()], writes=[("ps3h", i % 2)])

        def st_b4(i):
            pt = ps_f[3][:, :].bitcast(BF16)
            c0 = (i % 2) * 512
            tr.op("act", lambda e: e.activation(out=ybT.ap[:, :, i * 128:(i + 1) * 128],
                                                in_=pt[:, c0:c0 + 512].rearrange("p (a b) -> p a b", a=4), func=AF.Copy),
                  reads=[("ps3h", i % 2)], writes=[ybT.k(i)])

        def st_scores(i):
            sbi = 4 + i % 2
            for h in range(4):
                tr.op("pe", lambda e: e.matmul(ps_f[sbi][:, h * 128:(h + 1) * 128], lhsT=KdT.ap[:, h, i * 128:(i + 1) * 128],
                                               rhs=QdT.ap[:, h, i * 128:(i + 1) * 128], start=True, stop=True),
                      reads=[KdT.k(h, i // 4), QdT.k(h, i // 4)], writes=[psk_f[sbi]])
            upd_mm(2 * i)
            upd_mm(2 * i + 1)

        def st_state(i):
            sbi = 4 + i % 2
            tr.op("dve", lambda e: e.tensor_tensor(out=scb[i % 2].ap, in0=ps_f[sbi][:, :].rearrange("p (h t) -> p h t", h=4),
                                                   in1=CT("maskbc")[:, :].unsqueeze(1).to_broadcast([128, 4, 128]), op=ALU.mult),
                  reads=[psk_f[sbi], CB("maskbc").k()], writes=[scb[i % 2].k()])
            upd_state(2 * i)
            upd_state(2 * i + 1)

        def st_o(i):
            sbi = 4 + i % 2
            s0, s1 = (2 * i) % 4, (2 * i + 1) % 4
            for h in range(4):
                cols = slice(h * 128, (h + 1) * 128)
                tr.op("pe", lambda e: e.matmul(ps_f[sbi][:, cols], lhsT=scb[i % 2].ap[:, h, :], rhs=V.ap[:, i, cols], start=True, stop=False),
                      reads=[scb[i % 2].k(), V.k(i)], writes=[psk_f[sbi]])
                tr.op("pe", lambda e: e.matmul(ps_f[sbi][0:64, cols], lhsT=QdT.ap[:, h, i * 128:i * 128 + 64], rhs=Sbf[:, h, s0, :],
                                               start=False, stop=True),
                      reads=[QdT.k(h, i // 4), Sbf_b.k(s0, h)], writes=[psk_f[sbi]])
                tr.op("pe", lambda e: e.matmul(ps_f[sbi][64:128, cols], lhsT=QdT.ap[:, h, i * 128 + 64:(i + 1) * 128], rhs=Sbf[:, h, s1, :],
                                               start=False, stop=True),
                      reads=[QdT.k(h, i // 4), Sbf_b.k(s1, h)], writes=[psk_f[sbi]])

        p2pool = PsumPool([0, 1, 2])
        pbank = {}
        tr.op("dve", lambda e: e.tensor_copy(out=U[:, 1, :, :], in_=Sin[:, :, :]), reads=[Sin_b.k()], writes=[U_b.k(1, h) for h in range(4)])
        tr.op("act", lambda e: e.activation(out=Sbf[:, :, 0, :], in_=Sin[:, :, :], func=AF.Copy), reads=[Sin_b.k()], writes=[Sbf_b.k(0, h) for h in range(4)])
        ok = lambda t: 0 <= t < NT
        st_proj(0)
        st_gate_act(0)
        st_gate_dve(0)
        for i in range(NT + 4):
            if ok(i + 1):
                st_proj(i + 1)
            if ok(i - 1):
                st_b1(i - 1)
            if ok(i - 2):
                st_b2(i - 2)
            if ok(i):
                st_scores(i)
            if ok(i + 1):
                st_gate_act(i + 1)
            if ok(i):
                st_state(i)
            if ok(i + 1):
                st_gate_dve(i + 1)
            if ok(i):
                st_o(i)
            if ok(i - 3):
                st_b3(i - 3)
            if ok(i - 4):
                st_b4(i - 4)
        for n in p2:
            ar.free(*p2[n])
        ar.free(*scb, *ybt, junk2, KdT, QdT, KdTM, V, U_b, Sbf_b)

        if debug:
            tr.dma("sp", "dbg", dbg["yb"], ybT.ap, reads=[ybT.k(i) for i in range(NT)])

        checkpoint("hgrn")
        allpool = PsumPool([0, 1, 2, 3, 4, 5])
        ar.free(wbf[0], wbf[3])
        wbf[SL_ZA] = ar.alloc("wbf2b", [128, 8, 512], BF16)
        load_win_group(SL_ZA, C_ZA)
        for _ in setup_gen:
            pass
        Bp = ar.alloc("Bp", [128, 2, 16, 256], BF16)
        tabs = s5h["tabs"]
        Toep = ar.alloc("Toep", [128, 32, 256], BF16)
        Cp = ar.alloc("Cp", [128, 2, 16, 256], BF16)
        tr.dma("sp", "s5ld", Bp.ap, bp_dr.ap().rearrange("p j a r n -> p j a (r n)"), reads=[("bp_dr",)], writes=[Bp.k()])
        tr.dma("sp", "s5ld", Toep.ap, toep_dr.ap(), reads=[("toep_dr",)], writes=[Toep.k()])
        tr.dma("sp", "s5ld", Cp.ap, cp_dr.ap(), reads=[("cp_dr",)], writes=[Cp.k()])
        for b_ in (Bp, Toep, Cp):
            tr.res[b_.k()] = {"w": ("s5ld", tr.cnt["s5ld"]), "r": dict(b_.fence)}
        ucm = ar.alloc("ucm", [128, 32, 16, 16], BF16)
        for s_ in range(16):
            bi = allpool.next()
            for kc in range(8):
                tr.op("pe", lambda e: e.matmul(ps_f[bi][:, :], lhsT=xnT.ap[:, kc, s_:TT:16], rhs=wbf[SL_U].ap[:, kc, :],
                                               start=(kc == 0), stop=(kc == 7)),
                      reads=[wbf[SL_U].k()] + [xnT.k(i) for i in range(NT)], writes=[psk_f[bi]])
            if s_ % 2 == 0:
                tr.op("act", lambda e: e.activation(out=ucm.ap[:, :, s_, :], in_=ps_f[bi][:, :].rearrange("p (g q) -> p g q", g=32), func=AF.Copy),
                      reads=[psk_f[bi]], writes=[ucm.k(s_)])
            else:
                tr.op("dve", lambda e: e.tensor_copy(out=ucm.ap[:, :, s_, :], in_=ps_f[bi][:, :].rearrange("p (g q) -> p g q", g=32)),
                      reads=[psk_f[bi]], writes=[ucm.k(s_)])
        ar.free(wbf[1])
        Ub = ar.alloc("U", [128, 2, 32, 128], BF16)
        n_ev = 0
        for j in range(2):
            for gb in range(4):
                pb_i = trpool.next()
                pt = ps_b[pb_i]
                for g in range(8 * gb, 8 * gb + 8):
                    tr.op("pe", lambda e: e.transpose(out=pt[:, (g % 8) * 128:(g % 8 + 1) * 128], in_=ucm.ap[:, g, 8 * j:8 * j + 8, :].rearrange("p s q -> p (s q)"),
                                                      identity=CT("identb")[:]),
                          reads=[ucm.k(s_) for s_ in range(8 * j, 8 * j + 8)] + [CB("identb").k()], writes=[psk_b[pb_i]])
                if n_ev % 2 == 0:
                    tr.op("dve", lambda e: e.tensor_copy(out=Ub.ap[:, j, 8 * gb:8 * gb + 8, :], in_=pt[:, :].rearrange("p (a b) -> p a b", a=8)),
                          reads=[psk_b[pb_i]], writes=[Ub.k(j, gb)])
                else:
                    tr.op("act", lambda e: e.activation(out=Ub.ap[:, j, 8 * gb:8 * gb + 8, :], in_=pt[:, :].rearrange("p (a b) -> p a b", a=8), func=AF.Copy),
                          reads=[psk_b[pb_i]], writes=[Ub.k(j, gb)])
                n_ev += 1
        ar.free(ucm)
        Gre = ar.alloc("Gre", [128, 16, 128], F32)
        Gim = ar.alloc("Gim", [128, 16, 128], F32)
        rt = [ar.alloc(f"rt{j}", [128, 512], F32) for j in range(4)]
        for q in range(4):
            bre_i, bim_i = allpool.next(), allpool.next()
            for pr in range(4 * q, 4 * q + 4):
                for g2 in range(2):
                    g = 2 * pr + g2
                    for ri, bi in ((0, bre_i), (1, bim_i)):
                        for j in range(2):
                            tr.op("pe", lambda e: e.matmul(ps_f[bi][g2 * 64:(g2 + 1) * 64, (pr % 4) * 128:(pr % 4 + 1) * 128],
                                                           lhsT=Bp.ap[:, j, pr, ri * 128 + g2 * 64:ri * 128 + (g2 + 1) * 64], rhs=Ub.ap[:, j, g, :],
                                                           start=(j == 0), stop=(j == 1)),
                                  reads=[Bp.k(), Ub.k(j, g // 8)], writes=[psk_f[bi]])
            cosq = tabs.ap[:, 1, 4 * q:4 * q + 4, :].rearrange("p a b -> p (a b)")
            sinq = tabs.ap[:, 2, 4 * q:4 * q + 4, :].rearrange("p a b -> p (a b)")
            gre_q = Gre.ap[:, 4 * q:4 * q + 4, :].rearrange("p a b -> p (a b)")
            gim_q = Gim.ap[:, 4 * q:4 * q + 4, :].rearrange("p a b -> p (a b)")
            tr.op("dve", lambda e: e.tensor_tensor(out=rt[0].ap, in0=ps_f[bre_i][:, :], in1=cosq, op=ALU.mult), reads=[psk_f[bre_i], tabs.k()], writes=[rt[0].k()])
            tr.op("dve", lambda e: e.tensor_tensor(out=rt[1].ap, in0=ps_f[bim_i][:, :], in1=sinq, op=ALU.mult), reads=[psk_f[bim_i], tabs.k()], writes=[rt[1].k()])
            tr.op("dve", lambda e: e.tensor_tensor(out=rt[2].ap, in0=ps_f[bim_i][:, :], in1=cosq, op=ALU.mult), reads=[psk_f[bim_i], tabs.k()], writes=[rt[2].k()])
            tr.op("dve", lambda e: e.tensor_tensor(out=rt[3].ap, in0=ps_f[bre_i][:, :], in1=sinq, op=ALU.mult), reads=[psk_f[bre_i], tabs.k()], writes=[rt[3].k()])
            tr.op("pool", lambda e: e.tensor_tensor(out=gre_q, in0=rt[0].ap, in1=rt[1].ap, op=ALU.add), reads=[rt[0].k(), rt[1].k()], writes=[Gre.k(q)])
            tr.op("pool", lambda e: e.tensor_tensor(out=gim_q, in0=rt[2].ap, in1=rt[3].ap, op=ALU.subtract), reads=[rt[2].k(), rt[3].k()], writes=[Gim.k(q)])
        ar.free(Bp)
        for pr in range(16):
            for Gb in (Gre, Gim):
                tr.op("dve", lambda e: e.tensor_tensor_scan(out=Gb.ap[:, pr, :], data0=rho16[:, pr:pr + 1].to_broadcast([128, 128]), data1=Gb.ap[:, pr, :],
                                                            initial=0.0, op0=ALU.mult, op1=ALU.add),
                      reads=[Gb.k(pr // 4), rho16_b.k()], writes=[Gb.k(pr // 4)])
        pk2_b, pk2 = small("pk2", [128, 2, 16])
        e1_b, e1 = small("e1", [128, 16])
        e2_b, e2 = small("e2", [128, 16])
        gk = [Gre.k(q) for q in range(4)] + [Gim.k(q) for q in range(4)]
        cos127, sin127 = tabs.ap[:, 1, :, 127], tabs.ap[:, 2, :, 127]
        gre127, gim127 = Gre.ap[:, :, 127], Gim.ap[:, :, 127]
        tr.op("dve", lambda e: e.tensor_tensor(out=e1[:, :], in0=cos127, in1=gre127, op=ALU.mult), reads=gk + [tabs.k()], writes=[e1_b.k()])
        tr.op("dve", lambda e: e.tensor_tensor(out=e2[:, :], in0=sin127, in1=gim127, op=ALU.mult), reads=gk + [tabs.k()], writes=[e2_b.k()])
        tr.op("dve", lambda e: e.tensor_tensor(out=pk2[:, 0, :], in0=e1[:, :], in1=e2[:, :], op=ALU.subtract), reads=[e1_b.k(), e2_b.k()], writes=[pk2_b.k(0)])
        tr.op("dve", lambda e: e.tensor_tensor(out=e1[:, :], in0=cos127, in1=gim127, op=ALU.mult), reads=gk + [tabs.k()], writes=[e1_b.k()])
        tr.op("dve", lambda e: e.tensor_tensor(out=e2[:, :], in0=sin127, in1=gre127, op=ALU.mult), reads=gk + [tabs.k()], writes=[e2_b.k()])
        tr.op("dve", lambda e: e.tensor_tensor(out=pk2[:, 1, :], in0=e1[:, :], in1=e2[:, :], op=ALU.add), reads=[e1_b.k(), e2_b.k()], writes=[pk2_b.k(1)])
        tr.dma("pool", "ags", agi_s[:, :], pk2[:, :, :].rearrange("p a b -> p (a b)"), reads=[pk2_b.k(0), pk2_b.k(1)], writes=[("agi_s",)])
        tr.collective("cc", lambda e: e.collective_compute("AllGather", ALU.bypass, replica_groups=[[0, 1, 2, 3], [4, 5, 6, 7]],
                                                           ins=[agi_s.ap().opt()], outs=[ago_s.ap().opt()]),
                      reads=[("agi_s",)], writes=[("ago_s",)])
        g2_b, g2t = small("gath2", [128, 4, 2, 16])
        tr.dma("sp", "ags2", g2t[:, :, :, :].rearrange("p j a b -> p j (a b)"), ago_s[:, :].rearrange("(j p) f -> p j f", p=128),
               reads=[("ago_s",)], writes=[g2_b.k()])
        hin_b, hin = small("hin", [128, 2, 16])
        nw_b, nwt = small("hnew", [128, 2, 16])
        tr.op("dve", lambda e: e.memset(hin[:, :, :], 0.0), writes=[hin_b.k()])
        for j in range(3):
            R_ = [hin_b.k(), a2k_b.k(), g2_b.k(), e1_b.k(), e2_b.k(), nw_b.k()]
            tr.op("dve", lambda e: e.tensor_tensor(out=e1[:, :], in0=a2k[:, 0, :], in1=hin[:, 0, :], op=ALU.mult), reads=R_, writes=[e1_b.k()])
            tr.op("dve", lambda e: e.tensor_tensor(out=e2[:, :], in0=a2k[:, 1, :], in1=hin[:, 1, :], op=ALU.mult), reads=R_, writes=[e2_b.k()])
            tr.op("dve", lambda e: e.tensor_tensor(out=nwt[:, 0, :], in0=e1[:, :], in1=e2[:, :], op=ALU.subtract), reads=R_, writes=[nw_b.k()])
            tr.op("dve", lambda e: e.tensor_tensor(out=e1[:, :], in0=a2k[:, 0, :], in1=hin[:, 1, :], op=ALU.mult), reads=R_, writes=[e1_b.k()])
            tr.op("dve", lambda e: e.tensor_tensor(out=e2[:, :], in0=a2k[:, 1, :], in1=hin[:, 0, :], op=ALU.mult), reads=R_, writes=[e2_b.k()])
            tr.op("dve", lambda e: e.tensor_tensor(out=nwt[:, 1, :], in0=e1[:, :], in1=e2[:, :], op=ALU.add), reads=R_, writes=[nw_b.k()])
            tr.op("dve", lambda e: e.tensor_tensor(out=nwt[:, :, :], in0=nwt[:, :, :], in1=g2t[:, j, :, :], op=ALU.add), reads=R_, writes=[nw_b.k()])
            tr.op("dve", lambda e: e.tensor_tensor(out=nwt[:, :, :], in0=nwt[:, :, :], in1=hin[:, :, :], op=ALU.subtract), reads=R_, writes=[nw_b.k()])
            tr.op("dve", lambda e: e.scalar_tensor_tensor(out=hin[:, :, :], in0=nwt[:, :, :], scalar=CT("use")[:, j:j + 1], in1=hin[:, :, :],
                                                          op0=ALU.mult, op1=ALU.add), reads=R_ + [CB("use").k()], writes=[hin_b.k()])
        if debug:
            tr.dma("sp", "dbg", dbg["hin"], hin[:, :, :], reads=[hin_b.k()])
        checkpoint("s5local")
        HreB = ar.alloc("HreB", [128, 16, 130], BF16)
        HimB = ar.alloc("HimB", [128, 16, 130], BF16)
        rpow = tabs.ap[:, 0]
        cosT, sinT = tabs.ap[:, 1], tabs.ap[:, 2]
        big = [ar.alloc(f"big{j}", [128, 8, 128], F32) for j in range(2)]
        for hf in range(2):
            ps_ = slice(8 * hf, 8 * hf + 8)
            qk = [2 * hf, 2 * hf + 1]
            for ri, Gb in ((0, Gre), (1, Gim)):
                tr.op("dve", lambda e: e.tensor_tensor(out=big[0].ap, in0=rpow[:, ps_, :], in1=hin[:, ri, ps_].unsqueeze(2).to_broadcast([128, 8, 128]), op=ALU.mult),
                      reads=[tabs.k(), hin_b.k()], writes=[big[0].k()])
                tr.op("dve", lambda e: e.tensor_tensor(out=Gb.ap[:, ps_, :], in0=Gb.ap[:, ps_, :], in1=big[0].ap, op=ALU.add),
                      reads=[big[0].k()] + [Gb.k(q) for q in qk], writes=[Gb.k(q) for q in qk])
            gkh = [Gre.k(q) for q in qk] + [Gim.k(q) for q in qk]
            tr.op("dve", lambda e: e.tensor_tensor(out=big[0].ap, in0=cosT[:, ps_, :], in1=Gre.ap[:, ps_, :], op=ALU.mult), reads=gkh + [tabs.k()], writes=[big[0].k()])
            tr.op("dve", lambda e: e.tensor_tensor(out=big[1].ap, in0=sinT[:, ps_, :], in1=Gim.ap[:, ps_, :], op=ALU.mult), reads=gkh + [tabs.k()], writes=[big[1].k()])
            tr.op("dve", lambda e: e.tensor_tensor(out=HreB.ap[:, ps_, 1:129], in0=big[0].ap, in1=big[1].ap, op=ALU.subtract),
                  reads=[big[0].k(), big[1].k()], writes=[HreB.k()])
            tr.op("dve", lambda e: e.tensor_tensor(out=big[0].ap, in0=cosT[:, ps_, :], in1=Gim.ap[:, ps_, :], op=ALU.mult), reads=gkh + [tabs.k()], writes=[big[0].k()])
            tr.op("dve", lambda e: e.tensor_tensor(out=big[1].ap, in0=sinT[:, ps_, :], in1=Gre.ap[:, ps_, :], op=ALU.mult), reads=gkh + [tabs.k()], writes=[big[1].k()])
            tr.op("dve", lambda e: e.tensor_tensor(out=HimB.ap[:, ps_, 1:129], in0=big[0].ap, in1=big[1].ap, op=ALU.add),
                  reads=[big[0].k(), big[1].k()], writes=[HimB.k()])
        tr.op("act", lambda e: e.activation(out=HreB.ap[:, :, 0], in_=hin[:, 0, :], func=AF.Copy), reads=[hin_b.k(), HreB.k()], writes=[HreB.k()])
        tr.op("act", lambda e: e.activation(out=HimB.ap[:, :, 0], in_=hin[:, 1, :], func=AF.Copy), reads=[hin_b.k(), HimB.k()], writes=[HimB.k()])
        ar.free(*big, *rt, Gre, Gim, tabs)
        ycm = ar.alloc("ycm", [128, 16, 32, 16], BF16)
        for pr in range(16):
            bi = allpool.next()
            for g2 in range(2):
                g = 2 * pr + g2
                rows = slice(g2 * 64, g2 * 64 + 64)
                o_ = ps_f[bi][:, g2 * 256:(g2 + 1) * 256]
                tr.op("pe", lambda e: e.matmul(o_, lhsT=Ub.ap[:, 0, g, :], rhs=Toep.ap[:, g, :], start=True, stop=False),
                      reads=[Ub.k(0, g // 8), Toep.k()], writes=[psk_f[bi]])
                tr.op("pe", lambda e: e.matmul(ps_f[bi][:, g2 * 256 + 128:(g2 + 1) * 256], lhsT=Ub.ap[:, 1, g, :], rhs=Toep.ap[:, g, 0:128],
                                               start=False, stop=False),
                      reads=[Ub.k(1, g // 8), Toep.k()], writes=[psk_f[bi]])
                tr.op("pe", lambda e: e.matmul(o_, lhsT=HreB.ap[rows, pr, 0:128], rhs=Cp.ap[rows, 0, pr, :], start=False, stop=False),
                      reads=[HreB.k(), Cp.k()], writes=[psk_f[bi]])
                tr.op("pe", lambda e: e.matmul(o_, lhsT=HimB.ap[rows, pr, 0:128], rhs=Cp.ap[rows, 1, pr, :], start=False, stop=True),
                      reads=[HimB.k(), Cp.k()], writes=[psk_f[bi]])
            tr.op("act", lambda e: e.activation(out=ycm.ap[:, :, 2 * pr:2 * pr + 2, :], in_=ps_f[bi][:, :].rearrange("p (g s q) -> p s g q", g=2, s=16),
                                                func=AF.Gelu_apprx_tanh),
                  reads=[psk_f[bi]], writes=[ycm.k(pr)])
        ar.free(Toep, Cp, Ub, HreB, HimB)
        yFM = ar.alloc("yFM", [128, 4, TT], BF16)
        n_ev = 0
        for q in range(4):
            for sb_ in range(2):
                pb_i = trpool.next()
                pt = ps_b[pb_i]
                for s8 in range(8):
                    s_ = 8 * sb_ + s8
                    tr.op("pe", lambda e: e.transpose(out=pt[:, s8 * 128:(s8 + 1) * 128], in_=ycm.ap[:, s_, 8 * q:8 * q + 8, :].rearrange("p g q -> p (g q)"),
                                                      identity=CT("identb")[:]),
                          reads=[ycm.k(pr) for pr in range(4 * q, 4 * q + 4)] + [CB("identb").k()], writes=[psk_b[pb_i]])
                dst = yFM.ap[:, q, :].rearrange("p (c s) -> p s c", s=16)[:, 8 * sb_:8 * sb_ + 8, :]
                src = pt[:, :].rearrange("p (a b) -> p a b", a=8)
                if n_ev % 2 == 0:
                    tr.op("dve", lambda e: e.tensor_copy(out=dst, in_=src), reads=[psk_b[pb_i]], writes=[yFM.k(q, sb_)])
                else:
                    tr.op("act", lambda e: e.activation(out=dst, in_=src, func=AF.Copy), reads=[psk_b[pb_i]], writes=[yFM.k(q, sb_)])
                n_ev += 1
        ar.free(ycm)
        if debug:
            tr.dma("sp", "dbg", dbg["yfm"], yFM.ap, reads=[yFM.k(q, s) for q in range(4) for s in range(2)])
        yaFM = ar.alloc("yaFM", [128, 4, TT], BF16)
        wglu = ar.alloc("wglu", [128, 4, 512], BF16)
        load_weight_cols(wglu, lambda c0: wglu.ap[:, :, c0:c0 + 256], wglu_d, 4, 0, 512)
        gt = {n: [ar.alloc(f"g_{n}{j}", [128, 512], F32) for j in range(2)] for n in ["sg", "sz"]}
        wga = ar.alloc("wga", [128, 8, 1024], BF16)
        wgb = ar.alloc("wgb", [128, 8, 1024], BF16)
        wpa = ar.alloc("wpa", [128, 4, 1024], BF16)
        wpb = ar.alloc("wpb", [128, 4, 1024], BF16)
        wout = ar.alloc("wout", [128, 8, 1024], BF16)
        load_weight_cols(wga, lambda c0: wga.ap[:, :, c0:c0 + 256], w_in_d, 8, C_GA, 1024)
        load_weight_cols(wgb, lambda c0: wgb.ap[:, :, c0:c0 + 256], w_in_d, 8, C_GB, 1024)
        load_weight_cols(wpa, lambda c0: wpa.ap[:, :, c0:c0 + 256], wpa_d, 4, 0, 1024)
        wpb32 = ar.alloc("wpb32", [128, 4, 1024], F32)
        tr.dma("sp", "wpb32", wpb32.ap, wpb_d.rearrange("(kc p) c -> p kc c", p=128), writes=[wpb32.k()])
        tr.op("pool", lambda e: e.tensor_tensor(out=wpb.ap, in0=wpb32.ap, in1=CT("hnw")[:, 0:4].unsqueeze(2).to_broadcast([128, 4, 1024]), op=ALU.mult),
              reads=[wpb32.k(), CB("hnw").k()], writes=[wpb.k()])
        ar.free(wpb32)
        load_weight_cols(wout, lambda c0: wout.ap[:, :, c0:c0 + 256], wout_d, 8, 0, 1024)
        nbg_b, nbg = small("nbglu", [128, 4])
        tr.op("dve", lambda e: e.tensor_scalar(out=nbg[:, :], in0=CT("bglu")[:, :], scalar1=-1.0, scalar2=None, op0=ALU.mult),
              reads=[CB("bglu").k()], writes=[nbg_b.k()])
        yk = [yFM.k(q, s) for q in range(4) for s in range(2)]
        it = 0
        for ct in range(4):
            for tb in range(NB):
                j = it % 2
                it += 1
                bg = allpool.next()
                for kc in range(4):
                    tr.op("pe", lambda e: e.matmul(ps_f[bg][:, :], lhsT=wglu.ap[:, kc, ct * 128:(ct + 1) * 128], rhs=yFM.ap[:, kc, tb * 512:(tb + 1) * 512],
                                                   start=(kc == 0), stop=(kc == 3)),
                          reads=[wglu.k()] + yk, writes=[psk_f[bg]])
                bz = allpool.next()
                for kc in range(8):
                    tr.op("pe", lambda e: e.matmul(ps_f[bz][:, :], lhsT=wbf[SL_ZA].ap[:, kc, ct * 128:(ct + 1) * 128],
                                                   rhs=xnT.ap[:, kc, tb * 512:(tb + 1) * 512], start=(kc == 0), stop=(kc == 7)),
                          reads=[wbf[SL_ZA].k()] + [xnT.k(4 * tb + jj) for jj in range(4)], writes=[psk_f[bz]])
                G = {n: gt[n][j] for n in gt}
                sigmoid3(G["sg"].ap, [G["sg"].k()], ps_f[bg][:, :], [psk_f[bg]], None, nbias=nbg[:, ct:ct + 1], nbias_keys=[nbg_b.k()])
                sigmoid3(G["sz"].ap, [G["sz"].k()], ps_f[bz][:, :], [psk_f[bz]], None)
                tr.op("dve", lambda e: e.tensor_tensor(out=G["sz"].ap, in0=ps_f[bz][:, :], in1=G["sz"].ap, op=ALU.mult),
                      reads=[psk_f[bz], G["sz"].k()], writes=[G["sz"].k()])
                tr.op("dve", lambda e: e.tensor_tensor(out=G["sg"].ap, in0=yFM.ap[:, ct, tb * 512:(tb + 1) * 512], in1=G["sg"].ap, op=ALU.mult),
                      reads=yk + [G["sg"].k()], writes=[G["sg"].k()])
                tr.op("dve", lambda e: e.tensor_tensor(out=yaFM.ap[:, ct, tb * 512:(tb + 1) * 512], in0=G["sg"].ap, in1=G["sz"].ap, op=ALU.mult),
                      reads=[G["sg"].k(), G["sz"].k()], writes=[yaFM.k(ct, tb)])
        for n in gt:
            ar.free(*gt[n])
        ar.free(yFM, wglu)
        if debug:
            tr.dma("sp", "dbg", dbg["ya"], yaFM.ap, reads=[yaFM.k(c, t) for c in range(4) for t in range(NB)])
        checkpoint("s5")

        ar.free(wbf[SL_ZA])
        fnw = ar.alloc("fnw", [128, D], F32)
        tr.dma("sp", "fnw", fnw.ap, fnw_d, writes=[fnw.k()])
        mg = [ar.alloc(f"mg{j}", [128, 8, 512], BF16) for j in range(2)]
        mt = {n: [ar.alloc(f"m_{n}{j}", [128, 512], F32) for j in range(2)] for n in ["sa", "sb"]}
        xr = [ar.alloc(f"xr{j}", [128, D], F32) for j in range(2)]
        hb = [ar.alloc(f"hb{j}", [128, D], F32) for j in range(2)]
        junk3 = ar.alloc("junk3", [128, 512], BF16)
        ss2_b, ss2 = small("ss2", [128, NT, 2])
        r2_b, r2 = small("r2", [128, NT])
        yak = lambda tb: [yaFM.k(c, tb) for c in range(4)]
        ybk = lambda tb: [ybT.k(4 * tb + jj) for jj in range(4)]
        xk = lambda tb: [xnT.k(4 * tb + jj) for jj in range(4)]
        it = 0
        for tb in range(NB):
            M = mg[tb % 2]
            for dt_ in range(8):
                j = it % 2
                it += 1
                T_ = {n: mt[n][j] for n in mt}
                cs = slice(dt_ * 128, (dt_ + 1) * 128)
                ts_ = slice(tb * 512, (tb + 1) * 512)
                bpa, bpb_, bga, bgb = allpool.next(), allpool.next(), allpool.next(), allpool.next()
                for kc in range(8):
                    tr.op("pe", lambda e: e.matmul(ps_f[bga][:, :], lhsT=wga.ap[:, kc, cs], rhs=xnT.ap[:, kc, ts_], start=(kc == 0), stop=(kc == 7)),
                          reads=[wga.k()] + xk(tb), writes=[psk_f[bga]])
                for kc in range(8):
                    tr.op("pe", lambda e: e.matmul(ps_f[bgb][:, :], lhsT=wgb.ap[:, kc, cs], rhs=xnT.ap[:, kc, ts_], start=(kc == 0), stop=(kc == 7)),
                          reads=[wgb.k()] + xk(tb), writes=[psk_f[bgb]])
                for kc in range(4):
                    tr.op("pe", lambda e: e.matmul(ps_f[bpa][:, :], lhsT=wpa.ap[:, kc, cs], rhs=yaFM.ap[:, kc, ts_], start=(kc == 0), stop=(kc == 3)),
                          reads=[wpa.k()] + yak(tb), writes=[psk_f[bpa]])
                for kc in range(4):
                    tr.op("pe", lambda e: e.matmul(ps_f[bpb_][:, :], lhsT=wpb.ap[:, kc, cs], rhs=ybT.ap[:, kc, ts_], start=(kc == 0), stop=(kc == 3)),
                          reads=[wpb.k()] + ybk(tb), writes=[psk_f[bpb_]])
                sigmoid3(T_["sa"].ap, [T_["sa"].k()], ps_f[bga][:, :], [psk_f[bga]], None)
                sigmoid3(T_["sb"].ap, [T_["sb"].k()], ps_f[bgb][:, :], [psk_f[bgb]], None)
                tr.op("dve", lambda e: e.tensor_tensor(out=T_["sa"].ap, in0=ps_f[bpa][:, :], in1=T_["sa"].ap, op=ALU.mult),
                      reads=[psk_f[bpa], T_["sa"].k()], writes=[T_["sa"].k()])
                tr.op("dve", lambda e: e.tensor_tensor(out=T_["sb"].ap, in0=ps_f[bpb_][:, :], in1=T_["sb"].ap, op=ALU.mult),
                      reads=[psk_f[bpb_], T_["sb"].k()], writes=[T_["sb"].k()])
                tr.op("pool", lambda e: e.tensor_tensor(out=M.ap[:, dt_, :], in0=T_["sa"].ap, in1=T_["sb"].ap, op=ALU.add),
                      reads=[T_["sa"].k(), T_["sb"].k()], writes=[M.k(dt_)])
            for il in range(4):
                i = 4 * tb + il
                X_, H_ = xr[i % 2], hb[i % 2]
                tr.dma("pool", f"xr{i % 2}", X_.ap, x_d[i * 128:(i + 1) * 128, :], writes=[X_.k()])
                for half in range(2):
                    bo = allpool.next()
                    hs = slice(half * 512, (half + 1) * 512)
                    for kc in range(8):
                        tr.op("pe", lambda e: e.matmul(ps_f[bo][:, :], lhsT=M.ap[:, kc, il * 128:(il + 1) * 128], rhs=wout.ap[:, kc, hs],
                                                       start=(kc == 0), stop=(kc == 7)),
                              reads=[M.k(kc), wout.k()], writes=[psk_f[bo]])
                    tr.op("dve", lambda e: e.tensor_tensor(out=H_.ap[:, hs], in0=ps_f[bo][:, :], in1=X_.ap[:, hs], op=ALU.add),
                          reads=[psk_f[bo], X_.k()], writes=[H_.k(half)])
                    tr.op("act", lambda e: e.activation(out=junk3.ap, in_=H_.ap[:, hs], func=AF.Square, accum_out=ss2[:, i, half:half + 1]),
                          reads=[H_.k(half)], writes=[junk3.k(), ss2_b.k(i, half)])
                rk = [ss2_b.k(i, 0), ss2_b.k(i, 1)]
                tr.op("dve", lambda e: e.tensor_tensor(out=r2[:, i:i + 1], in0=ss2[:, i, 0:1], in1=ss2[:, i, 1:2], op=ALU.add), reads=rk, writes=[r2_b.k(i)])
                rstd_act(r2[:, i:i + 1], [r2_b.k(i)], 1.0 / D)
                tr.op("dve", lambda e: e.scalar_tensor_tensor(out=H_.ap, in0=H_.ap, scalar=r2[:, i:i + 1], in1=fnw.ap, op0=ALU.mult, op1=ALU.mult),
                      reads=[H_.k(0), H_.k(1), r2_b.k(i), fnw.k()], writes=[H_.k(0), H_.k(1)])
                tr.dma("sp", f"ob{i % 2}", out_d[i * 128:(i + 1) * 128, :], H_.ap, reads=[H_.k(0), H_.k(1)])


    try:
        rest()
    except _Stop:
        pass
    tr.final_wait("sp")
    print("instr counts", tr.ninstr)
    return nc


def make_inputs(inputs):
    f32 = np.float32
    x = np.asarray(inputs["x"], f32)
    per_core = []
    common = {
        "w_in": np.ascontiguousarray(inputs["w_in"][0], f32),
        "nw": np.ascontiguousarray(np.asarray(inputs["norm_w"][0], f32).reshape(8, 128).T),
        "lbl": np.ascontiguousarray(np.asarray(inputs["hgrn_lb_logits"], f32).reshape(2, 4, 128).transpose(2, 0, 1)),
        "hnw": np.ascontiguousarray(np.asarray(inputs["hgrn_norm_w"][0], f32).reshape(4, 128).T),
        "identb": np.eye(128, dtype=f32).astype(ml_dtypes.bfloat16),
        "w_proj_b": np.ascontiguousarray(inputs["w_proj_b"][0], f32),
        "w_proj_a": np.ascontiguousarray(inputs["w_proj_a"][0], f32),
        "w_out": np.ascontiguousarray(inputs["w_out"][0], f32),
        "w_glu": np.ascontiguousarray(inputs["ssm_w_glu"][0], f32),
        "bglu": np.ascontiguousarray(np.asarray(inputs["ssm_b_glu"][0], f32).reshape(4, 128).T),
        "fnw": np.ascontiguousarray(np.broadcast_to(np.asarray(inputs["final_norm_w"], f32).reshape(1, D), (128, D))),
    }
    sn = lambda a: np.ascontiguousarray(np.asarray(a, f32).reshape(16, 2, 64).transpose(1, 2, 0).reshape(128, 16))
    common["lamre"] = sn(inputs["ssm_lambda_re"][0])
    common["lamim"] = sn(inputs["ssm_lambda_im"][0])
    ld = np.asarray(inputs["ssm_log_dt"][0], f32).reshape(16, 2).T
    common["logdt"] = np.ascontiguousarray(np.repeat(ld[:, None, :], 64, axis=1).reshape(128, 16))
    bsn = lambda a: np.ascontiguousarray(np.asarray(a, f32).reshape(16, 2, 64, 16).transpose(1, 2, 0, 3).reshape(128, 16, 16))
    common["bre"] = bsn(inputs["ssm_b_re"][0])
    common["bim"] = bsn(inputs["ssm_b_im"][0])
    csn = lambda a: np.ascontiguousarray(np.asarray(a, f32).reshape(16, 2, 16, 64).transpose(1, 3, 0, 2).reshape(128, 16, 16))
    common["cre"] = csn(inputs["ssm_c_re"][0])
    common["cim"] = csn(inputs["ssm_c_im"][0])
    common["dbc"] = np.ascontiguousarray(np.broadcast_to(np.asarray(inputs["ssm_d"][0], f32)[None], (128, 32, 16)))
    kvv = np.concatenate([-np.arange(1, 9), np.arange(0, 17), np.arange(15, -1, -1)]).astype(f32)
    common["kv"] = np.ascontiguousarray(np.broadcast_to(kvv[None], (128, 41)))
    common["cidx"] = np.ascontiguousarray(np.broadcast_to(np.arange(1, 129, dtype=f32)[None], (128, 128)))
    common["identf"] = np.eye(128, dtype=f32)
    rr_ = np.arange(128)[:, None] // 16
    cc_ = np.arange(256)[None, :] // 16
    common["maskT"] = (cc_ >= rr_).astype(f32)
    s = np.arange(128)[:, None]
    t = np.arange(128)[None, :]
    common["maskbc"] = ((s // 64 == t // 64) & (t >= s)).astype(f32)
    m = np.ones((128, 512), f32)
    m[:, 0::64] = 0.0
    common["mask512"] = m
    for r in range(8):
        b, k = r // 4, r % 4
        d = dict(common)
        d["x"] = np.ascontiguousarray(x[b, k * TT:(k + 1) * TT, :])
        use = np.zeros((128, 4), f32)
        use[:, :k] = 1.0
        d["use"] = use
        per_core.append(d)
    return per_core


def kernel(**inputs):
    nc = build_nc()
    in_maps = make_inputs(inputs)
    res = run_bass_kernel_spmd(nc, in_maps, core_ids=list(range(8)))
    out = np.zeros((2, 8192, D), np.float32)
    for r in range(8):
        b, k = r // 4, r % 4
        out[b, k * TT:(k + 1) * TT, :] = res.results[r]["out"]
    return out
```
